# Optimizing a Trainium2 kernel written in Bass

```python
import jax, jax.numpy as jnp
from jax import lax
import numpy as np

D_MODEL = 1024
BATCH = 8
SEQ = 4096
DEPTH = 2

GRID_W = 64
CTX_LEN = 256
EPS = 1e-6
MLA_HEADS = 8
Q_LORA = 256
KV_LORA = 128
QK_NOPE = 64
QK_ROPE = 32
V_HEAD = 64
QK_HEAD = QK_NOPE + QK_ROPE
ROPE_BASE = 10000.0
Q_BLOCK = 128
FNET_GROUPS = 4
FNET_GROUP_DIM = 64
FNET_C = FNET_GROUPS * FNET_GROUP_DIM
CONV_C = 256
CONV_W = 31
N_BRANCH = 3
PEER_HEADS = 8
N_KEYS = 128
N_EXPERTS = N_KEYS * N_KEYS
D_KEY = 128
PEER_TOPK = 16
PEER_CHUNK = 128
KV_IN = KV_LORA + QK_ROPE
REST_SPLITS = [Q_LORA, Q_LORA + FNET_C, Q_LORA + FNET_C + 2 * CONV_C]
IN_COLS = KV_IN + Q_LORA + FNET_C + 2 * CONV_C + N_BRANCH * D_MODEL

kernel_name = 'hybrid_mla_fnet_conformer_peer_dit'


def rmsnorm(x, g):
    xf = x.astype(jnp.float32)
    y = xf * lax.rsqrt(jnp.mean(xf * xf, axis=-1, keepdims=True) + EPS)
    return (y * g.astype(jnp.float32)).astype(x.dtype)


def layernorm(x, g, b):
    xf = x.astype(jnp.float32)
    mu = jnp.mean(xf, axis=-1, keepdims=True)
    var = jnp.mean(jnp.square(xf - mu), axis=-1, keepdims=True)
    y = (xf - mu) * lax.rsqrt(var + EPS)
    return (y * g.astype(jnp.float32) + b.astype(jnp.float32)).astype(x.dtype)


def adaln(cond, w_mod, b_mod):
    return jnp.split(jax.nn.silu(cond) @ w_mod + b_mod, 6, axis=-1)


def modulate(h, shift, scale):
    return h * (1.0 + scale[:, None, :]) + shift[:, None, :]


def axial_rope_tables(n, dtype):
    rows = n // GRID_W
    row = jnp.repeat(jnp.arange(rows, dtype=jnp.float32), GRID_W)
    col = jnp.tile(jnp.arange(GRID_W, dtype=jnp.float32), rows)
    half = QK_ROPE // 2
    inv = ROPE_BASE ** (-jnp.arange(0, half, 2, dtype=jnp.float32) / half)
    ar = row[:, None] * inv
    ac = col[:, None] * inv
    ang = jnp.concatenate([ar, ar, ac, ac], axis=-1)
    return jnp.cos(ang).astype(dtype), jnp.sin(ang).astype(dtype)


def _rotate_half(z):
    z1, z2 = jnp.split(z, 2, axis=-1)
    return jnp.concatenate([-z2, z1], axis=-1)


def apply_axial_rope(z, cos, sin):
    zr, zc = jnp.split(z, 2, axis=-1)
    rot = jnp.concatenate([_rotate_half(zr), _rotate_half(zc)], axis=-1)
    return z * cos[:, None, :] + rot * sin[:, None, :]


def mla_q(cq, lp, rope):
    b, t, _ = cq.shape
    q = (rmsnorm(cq, lp['g_cq']) @ lp['w_uq']).reshape(b, t, MLA_HEADS, QK_HEAD)
    q = rmsnorm(q, lp['g_qn'])
    if rope is not None:
        q = jnp.concatenate([q[..., :QK_NOPE], apply_axial_rope(q[..., QK_NOPE:], *rope)], axis=-1)
    return q


def mla_kv(z, lp, rope):
    b, t, _ = z.shape
    ckv, kr = jnp.split(z, [KV_LORA], axis=-1)
    kv = (rmsnorm(ckv, lp['g_ckv']) @ lp['w_ukv']).reshape(b, t, MLA_HEADS, QK_NOPE + V_HEAD)
    k_nope, v = kv[..., :QK_NOPE], kv[..., QK_NOPE:]
    k_rope = jnp.broadcast_to(kr[:, :, None, :], (b, t, MLA_HEADS, QK_ROPE))
    k = rmsnorm(jnp.concatenate([k_nope, k_rope], axis=-1), lp['g_kn'])
    if rope is not None:
        k = jnp.concatenate([k[..., :QK_NOPE], apply_axial_rope(k[..., QK_NOPE:], *rope)], axis=-1)
    return k, v


def softmax_attend(q, k, v):
    s = jnp.einsum('bqhd,bkhd->bhqk', q, k).astype(jnp.float32) * (QK_HEAD ** -0.5)
    p = jax.nn.softmax(s, axis=-1).astype(v.dtype)
    return jnp.einsum('bhqk,bkhd->bqhd', p, v)


def latent_attention(q, k, v):
    b, s, h, _ = q.shape
    nb = s // Q_BLOCK
    qb = q.reshape(b, nb, Q_BLOCK, h, QK_HEAD).transpose(1, 0, 2, 3, 4)
    o = lax.map(lambda qi: softmax_attend(qi, k, v), qb)
    return o.transpose(1, 0, 2, 3, 4).reshape(b, s, h * V_HEAD)


def fourier_mix(z):
    b, t, _ = z.shape
    zz = z.astype(jnp.float32).reshape(b, t, FNET_GROUPS, FNET_GROUP_DIM)
    f = jnp.fft.fft2(zz, axes=(1, 3), norm='ortho').real
    return f.reshape(b, t, FNET_C).astype(z.dtype)


def conformer_conv(z, lp):
    a, g = jnp.split(z, 2, axis=-1)
    u = a * jax.nn.sigmoid(g)
    u = lax.conv_general_dilated(u, lp['w_dw'], window_strides=(1,), padding=[(CONV_W // 2, CONV_W // 2)],
                                 dimension_numbers=('NWC', 'WIO', 'NWC'), feature_group_count=CONV_C) + lp['b_dw']
    return jax.nn.silu(layernorm(u, lp['g_cln'], lp['b_cln']))


def branch_merge(a_o, zf, zc, zg, lp):
    b, t, _ = zg.shape
    g = jax.nn.sigmoid(zg.astype(jnp.float32)).astype(zg.dtype).reshape(b, t, N_BRANCH, D_MODEL)
    m = (g[:, :, 0] * (a_o @ lp['wb_attn'])
         + g[:, :, 1] * (fourier_mix(zf) @ lp['wb_fnet'])
         + g[:, :, 2] * (conformer_conv(zc, lp) @ lp['wb_conv']))
    return m @ lp['w_out']


def mixer_sublayer(x, ctx, mod_x, mod_c, lp, rope, update_ctx):
    sh, sc, gt = mod_x
    csh, csc, cgt = mod_c
    hx = modulate(rmsnorm(x, lp['g_norm1']), sh, sc)
    hc = modulate(rmsnorm(ctx, lp['g_norm1']), csh, csc)
    k_c, v_c = mla_kv(hc @ lp['w_in'][:, :KV_IN], lp, None)
    k_x, v_x = mla_kv(hx @ lp['w_in'][:, :KV_IN], lp, rope)
    cq_x, zf_x, zc_x, zg_x = jnp.split(hx @ lp['w_in'][:, KV_IN:], REST_SPLITS, axis=-1)
    q_x = mla_q(cq_x, lp, rope)
    a_x = latent_attention(q_x, jnp.concatenate([k_c, k_x], axis=1), jnp.concatenate([v_c, v_x], axis=1))
    x = x + gt[:, None, :] * branch_merge(a_x, zf_x, zc_x, zg_x, lp)
    if update_ctx:
        b, l, _ = ctx.shape
        cq_c, zf_c, zc_c, zg_c = jnp.split(hc @ lp['w_in'][:, KV_IN:], REST_SPLITS, axis=-1)
        a_c = softmax_attend(mla_q(cq_c, lp, None), k_c, v_c).reshape(b, l, MLA_HEADS * V_HEAD)
        ctx = ctx + cgt[:, None, :] * branch_merge(a_c, zf_c, zc_c, zg_c, lp)
    return x, ctx


def peer_ffn(h, w_query, sub_keys, u_tab, v_tab):
    b, t, d = h.shape
    chunks = h.reshape(-1, PEER_CHUNK, d)

    def chunk(hc):
        q = (hc @ w_query).reshape(PEER_CHUNK, PEER_HEADS, 2, D_KEY // 2)
        s = jnp.einsum('chpd,hpnd->chpn', q, sub_keys).astype(jnp.float32)
        s_top, i_top = lax.top_k(s, PEER_TOPK)
        cand = (s_top[:, :, 0, :, None] + s_top[:, :, 1, None, :]).reshape(PEER_CHUNK, PEER_HEADS, PEER_TOPK * PEER_TOPK)
        cand_id = (i_top[:, :, 0, :, None] * N_KEYS + i_top[:, :, 1, None, :]).reshape(PEER_CHUNK, PEER_HEADS, PEER_TOPK * PEER_TOPK)
        best, pos = lax.top_k(cand, PEER_TOPK)
        eid = jnp.take_along_axis(cand_id, pos, axis=-1)
        gate = jax.nn.softmax(best, axis=-1).astype(hc.dtype)
        u = jnp.take(u_tab, eid, axis=0)
        act = jax.nn.gelu(jnp.einsum('chkd,cd->chk', u, hc)) * gate
        v = jnp.take(v_tab, eid, axis=0)
        return jnp.einsum('chk,chkd->cd', act, v)

    return lax.map(chunk, chunks).reshape(b, t, d)


def setup_inputs(seed: int = 0) -> dict:
    key = jax.random.key(seed)
    ks = jax.random.split(key, 28)
    f32 = jnp.float32

    def nrm(k, shape, scale):
        return jax.random.normal(k, shape, f32) * scale

    def gain(k, shape):
        return 1.0 + 0.02 * jax.random.normal(k, shape, f32)

    L = DEPTH
    return {
        'x': nrm(ks[0], (BATCH, SEQ, D_MODEL), 1.0),
        'c': nrm(ks[1], (BATCH, D_MODEL), 1.0),
        'ctx': nrm(ks[2], (BATCH, CTX_LEN, D_MODEL), 1.0),
        'c_ctx': nrm(ks[3], (D_MODEL,), 1.0),
        'w_mod': nrm(ks[4], (L, D_MODEL, 6 * D_MODEL), 0.5 * D_MODEL ** -0.5),
        'b_mod': nrm(ks[5], (L, 6 * D_MODEL), 0.02),
        'g_norm1': gain(ks[6], (L, D_MODEL)),
        'w_in': nrm(ks[7], (L, D_MODEL, IN_COLS), D_MODEL ** -0.5),
        'g_ckv': gain(ks[8], (L, KV_LORA)),
        'w_ukv': nrm(ks[9], (L, KV_LORA, MLA_HEADS * (QK_NOPE + V_HEAD)), KV_LORA ** -0.5),
        'g_cq': gain(ks[10], (L, Q_LORA)),
        'w_uq': nrm(ks[11], (L, Q_LORA, MLA_HEADS * QK_HEAD), Q_LORA ** -0.5),
        'g_qn': gain(ks[12], (L, QK_HEAD)),
        'g_kn': gain(ks[13], (L, QK_HEAD)),
        'w_dw': nrm(ks[14], (L, CONV_W, 1, CONV_C), CONV_W ** -0.5),
        'b_dw': nrm(ks[15], (L, CONV_C), 0.02),
        'g_cln': gain(ks[16], (L, CONV_C)),
        'b_cln': nrm(ks[17], (L, CONV_C), 0.02),
        'wb_attn': nrm(ks[18], (L, MLA_HEADS * V_HEAD, D_MODEL), (MLA_HEADS * V_HEAD) ** -0.5),
        'wb_fnet': nrm(ks[19], (L, FNET_C, D_MODEL), FNET_C ** -0.5),
        'wb_conv': nrm(ks[20], (L, CONV_C, D_MODEL), CONV_C ** -0.5),
        'w_out': nrm(ks[21], (L, D_MODEL, D_MODEL), D_MODEL ** -0.5),
        'g_norm2': gain(ks[22], (L, D_MODEL)),
        'w_query': nrm(ks[23], (L, D_MODEL, PEER_HEADS * D_KEY), D_MODEL ** -0.5),
        'sub_keys': nrm(ks[24], (L, PEER_HEADS, 2, N_KEYS, D_KEY // 2), (D_KEY // 2) ** -0.5),
        'u_tab': nrm(ks[25], (L, N_EXPERTS, D_MODEL), D_MODEL ** -0.5),
        'v_tab': nrm(ks[26], (L, N_EXPERTS, D_MODEL), PEER_HEADS ** -0.5),
    }


def reference(x, c, ctx, c_ctx, w_mod, b_mod, g_norm1, w_in, g_ckv, w_ukv, g_cq, w_uq, g_qn, g_kn,
              w_dw, b_dw, g_cln, b_cln, wb_attn, wb_fnet, wb_conv, w_out, g_norm2, w_query, sub_keys,
              u_tab, v_tab):
    rope = axial_rope_tables(x.shape[1], x.dtype)
    for i in range(DEPTH):
        lp = {'g_norm1': g_norm1[i], 'w_in': w_in[i], 'g_ckv': g_ckv[i], 'w_ukv': w_ukv[i],
              'g_cq': g_cq[i], 'w_uq': w_uq[i], 'g_qn': g_qn[i], 'g_kn': g_kn[i],
              'w_dw': w_dw[i], 'b_dw': b_dw[i], 'g_cln': g_cln[i], 'b_cln': b_cln[i],
              'wb_attn': wb_attn[i], 'wb_fnet': wb_fnet[i], 'wb_conv': wb_conv[i], 'w_out': w_out[i]}
        sh1, sc1, gt1, sh2, sc2, gt2 = adaln(c, w_mod[i], b_mod[i])
        csh1, csc1, cgt1, csh2, csc2, cgt2 = adaln(c_ctx[None, :], w_mod[i], b_mod[i])
        update_ctx = i < DEPTH - 1
        x, ctx = mixer_sublayer(x, ctx, (sh1, sc1, gt1), (csh1, csc1, cgt1), lp, rope, update_ctx)
        x = x + gt2[:, None, :] * peer_ffn(modulate(rmsnorm(x, g_norm2[i]), sh2, sc2),
                                           w_query[i], sub_keys[i], u_tab[i], v_tab[i])
        if update_ctx:
            ctx = ctx + cgt2[:, None, :] * peer_ffn(modulate(rmsnorm(ctx, g_norm2[i]), csh2, csc2),
                                                    w_query[i], sub_keys[i], u_tab[i], v_tab[i])
    return x
```

```python
import numpy as np
import ml_dtypes
import concourse.bass as bass
import concourse.mybir as mybir
from concourse.bass_utils import run_bass_kernel_spmd

F32 = mybir.dt.float32
BF16 = mybir.dt.bfloat16
U32 = mybir.dt.uint32
AF = mybir.ActivationFunctionType
ALU = mybir.AluOpType
AX = mybir.AxisListType

L = 2
D = 1024
T = 4096
CT = 256
NT = T + CT
TB = 256
NBLK = NT // TB
EPS = 1e-6
NCH = 128

CE = ('pe', 'act', 'dve', 'pool')
ENG = ('pe', 'act', 'dve', 'pool', 'sp')


class Buf:
    __slots__ = ('name', 'w', 'r', 'dkey', 'dcnt')

    def __init__(self, name=''):
        self.name = name
        self.w = None
        self.r = {}
        self.dkey = None
        self.dcnt = 0


class Prog:
    def __init__(self, nc, sbuf_words=207 * 256):
        self.nc = nc
        self.q = {e: [] for e in ENG}
        self.semh = {e: nc.alloc_semaphore('s_' + e) for e in CE}
        self.cnt = {e: 0 for e in CE}
        self.known = {e: {} for e in ENG}
        self.dtot = {}
        self.nd = 0
        self.sb = nc.alloc_sbuf_tensor('sb_all', [128, sbuf_words], F32)
        self.sb_words = sbuf_words
        self.sb_off = 0
        self.ps = nc.alloc_psum_tensor('ps_all', [128, 4096], F32)
        self.marks = []
        self.free_keys = []
        self.scope_keys = []
        self.pbuf = [Buf('bank%d' % i) for i in range(8)]

    def alloc(self, nelem, dtype=F32):
        nbytes = nelem * (2 if dtype == BF16 else 4)
        words = (nbytes + 3) // 4
        words = (words + 7) // 8 * 8
        assert self.sb_off + words <= self.sb_words, ('SBUF overflow', self.sb_off, words)
        ap = self.sb[:, self.sb_off:self.sb_off + words]
        self.sb_off += words
        if dtype != F32:
            ap = ap.bitcast(dtype)
        return ap[:, 0:nelem]

    def mark(self):
        self.marks.append((self.sb_off, len(self.scope_keys)))

    def release(self):
        self.sb_off, nk = self.marks.pop()
        self.free_keys += self.scope_keys[nk:]
        del self.scope_keys[nk:]

    def bank(self, b, n=1):
        return self.ps[:, b * 512:(b + n) * 512]

    def _wait(self, e, key, count):
        if key == e and e == 'pe':
            return
        if self.known[e].get(key, 0) >= count:
            return
        self.known[e][key] = count
        h = self.semh[key]
        self.q[e].append(lambda eng, h=h, c=count: eng.wait_ge(h, c))

    def _deps(self, e, r, w):
        for b in r:
            if b.w is not None:
                self._wait(e, *b.w)
        for b in w:
            if b.w is not None:
                self._wait(e, *b.w)
            for k, c in b.r.items():
                self._wait(e, k, c)

    def op(self, e, fn, r=(), w=()):
        self._deps(e, r, w)
        self.cnt[e] += 1
        c = self.cnt[e]
        h = self.semh[e]
        self.q[e].append(lambda eng, fn=fn, h=h: fn(eng).then_inc(h, 1))
        for b in r:
            b.r[e] = c
        for b in w:
            b.w = (e, c)
            b.r = {}

    def dma(self, out, in_, buf, load, q='sp', extra_r=(), **kw):
        if buf.dkey is None:
            if self.free_keys:
                buf.dkey = self.free_keys.pop()
            else:
                buf.dkey = 'd%d' % self.nd
                self.nd += 1
                self.semh[buf.dkey] = self.nc.alloc_semaphore(buf.dkey)
            if self.marks:
                self.scope_keys.append(buf.dkey)
            buf.dcnt = self.dtot.get(buf.dkey, 0)
        if load:
            self._deps(q, list(extra_r), [buf])
        else:
            self._deps(q, [buf] + list(extra_r), [])
        buf.dcnt += 16
        c = buf.dcnt
        self.dtot[buf.dkey] = c
        h = self.semh[buf.dkey]
        self.q[q].append(lambda eng, h=h, out=out, in_=in_, kw=kw: eng.dma_start(out=out, in_=in_, **kw).then_inc(h, 16))
        if load:
            buf.w = (buf.dkey, c)
            buf.r = {}
        else:
            buf.r[buf.dkey] = c

    def barrier(self, engines=ENG):
        for e in engines:
            for k in CE:
                if k != e and self.cnt[k] > 0:
                    self._wait(e, k, self.cnt[k])
            for k, c in self.dtot.items():
                self._wait(e, k, c)

    def emit(self):
        nc = self.nc
        self.barrier(ENG)
        with nc.Block() as block:
            @block.tensor
            def _(eng):
                for f in self.q['pe']:
                    f(eng)

            @block.scalar
            def _(eng):
                for f in self.q['act']:
                    f(eng)

            @block.vector
            def _(eng):
                for f in self.q['dve']:
                    f(eng)

            @block.gpsimd
            def _(eng):
                for f in self.q['pool']:
                    f(eng)

            @block.sync
            def _(eng):
                for f in self.q['sp']:
                    f(eng)


def MM(P, out, lhsT, rhs, start, stop, r, w, skip=False):
    P.op('pe', lambda e: e.matmul(out, lhsT=lhsT, rhs=rhs, start=start, stop=stop, skip_group_check=skip), r, w)


def TR(P, out, in_, ident, r, w):
    P.op('pe', lambda e: e.transpose(out=out, in_=in_, identity=ident), r, w)


def ACT(P, out, in_, func, r, w, scale=None, bias=None, accum=None):
    kw = {}
    if scale is not None:
        kw['scale'] = scale
    if bias is not None:
        kw['bias'] = bias
    if accum is not None:
        kw['accum_out'] = accum
    P.op('act', lambda e: e.activation(out=out, in_=in_, func=func, **kw), r, w)


def TT(P, eng, out, in0, in1, op, r, w):
    P.op(eng, lambda e: e.tensor_tensor(out=out, in0=in0, in1=in1, op=op), r, w)


def TS(P, eng, out, in0, s1, s2, op0, op1, r, w):
    if op1 is None:
        P.op(eng, lambda e: e.tensor_scalar(out=out, in0=in0, scalar1=s1, scalar2=None, op0=op0), r, w)
    else:
        P.op(eng, lambda e: e.tensor_scalar(out=out, in0=in0, scalar1=s1, scalar2=s2, op0=op0, op1=op1), r, w)


def STT(P, out, in0, scalar, in1, op0, op1, r, w):
    P.op('dve', lambda e: e.scalar_tensor_tensor(out=out, in0=in0, scalar=scalar, in1=in1, op0=op0, op1=op1), r, w)


def CP(P, eng, out, in_, r, w):
    if eng == 'act':
        P.op('act', lambda e: e.copy(out=out, in_=in_), r, w)
    else:
        P.op(eng, lambda e: e.tensor_copy(out=out, in_=in_), r, w)


def RED(P, out, in_, op, r, w):
    P.op('dve', lambda e: e.tensor_reduce(out=out, in_=in_, axis=AX.X, op=op), r, w)


def TTR(P, out, in0, in1, accum, r, w):
    P.op('act', lambda e: e.activation(out=out, in_=in0, func=AF.Square, accum_out=accum), r, w)


def MEMSET(P, eng, ap, val, w):
    P.op(eng, lambda e: e.memset(ap, val), (), w)


def RSTD(P, out, ss, n, r, w):
    TS(P, 'dve', out, ss, 1.0 / n, EPS, ALU.mult, ALU.add, r, w)
    ACT(P, out, out, AF.Sqrt, w, w)
    P.op('dve', lambda e: e.reciprocal(out=out, in_=out), w, w)


def r3(ap, a, b):
    return ap.rearrange('p (a b) -> p a b', a=a, b=b)


def r4(ap, a, b, c):
    return ap.rearrange('p (a b c) -> p a b c', a=a, b=b, c=c)


def build_program(debug=False, n_layers=L, stop_after=None):
    nc = bass.Bass('TRN2', target_bir_lowering=False)

    def din(name, shape, dt=F32):
        return nc.dram_tensor(name, list(shape), dt, kind='ExternalInput').ap()

    skind = 'ExternalOutput' if debug else 'Internal'

    def dsc(name, shape, dt):
        return nc.dram_tensor(name, list(shape), dt, kind=skind).ap()

    I = {}
    I['x'] = din('x', [T, D])
    I['ctx'] = din('ctx', [CT, D])
    I['cvec'] = din('cvec', [128, 16])
    I['w_mod'] = din('w_mod', [L, D, 6 * D])
    I['bmodT'] = din('bmodT', [L, 128, 48])
    I['bmod2'] = din('bmod2', [L, 2, 6 * D])
    I['g1T'] = din('g1T', [L, 128, 8])
    I['g2T'] = din('g2T', [L, 128, 8])
    I['w_in'] = din('w_in', [L, D, 4256])
    I['g_ckv'] = din('g_ckv', [L, 128, 1])
    I['w_ukv'] = din('w_ukv', [L, 128, 1024])
    I['g_cqT'] = din('g_cqT', [L, 128, 2])
    I['w_uq'] = din('w_uq', [L, 256, 768])
    I['g_qn'] = din('g_qn', [L, 1, 96])
    I['g_kn'] = din('g_kn', [L, 1, 96])
    I['wdwT'] = din('wdwT', [L, 128, 2 * 31])
    I['cvp'] = din('cvp', [L, 128, 6])
    I['wb_attn'] = din('wb_attn', [L, 512, D])
    I['wb_fnet'] = din('wb_fnet', [L, 256, D])
    I['wb_conv'] = din('wb_conv', [L, 256, D])
    I['w_out'] = din('w_out', [L, D, D])
    I['w_query'] = din('w_query', [L, D, D])
    I['skbd'] = din('skbd', [L, 128, 8 * 256])
    I['u_tab'] = din('u_tab', [L, 16384, D])
    I['v_tab'] = din('v_tab', [L, 16384, D])
    I['ropeC'] = din('ropeC', [T, 32])
    I['ropeS'] = din('ropeS', [T, 32])
    I['dftP'] = din('dftP', [2, T, T], BF16)
    I['dftC'] = din('dftC', [2, CT, CT], BF16)
    I['csbd'] = din('csbd', [128, 256])
    I['ident'] = din('ident', [128, 128])
    I['sel'] = din('sel', [2, 256])
    out_d = nc.dram_tensor('out', [T, D], F32, kind='ExternalOutput').ap()

    S = {}
    S['xres'] = dsc('xres', [NT, D], F32)
    S['QT'] = dsc('QT', [8, 96, NT], BF16)
    S['ZCS'] = dsc('ZCS', [NT, 512], BF16)
    S['UT'] = dsc('UT', [2, 128, NT], F32)
    S['aT'] = dsc('aT', [4, 128, NT], BF16)
    S['fT'] = dsc('fT', [2, 128, NT], BF16)
    S['cvT'] = dsc('cvT', [2, 128, NT], BF16)
    S['uTs'] = nc.dram_tensor('uTs', [NCH, 128, 1024], BF16, kind='Internal').ap()
    S['vs'] = nc.dram_tensor('vs', [NCH, 128, 1024], BF16, kind='Internal').ap()

    P = Prog(nc)
    PB = P.pbuf

    identF = P.alloc(128)
    b_idF = Buf()
    P.dma(identF, I['ident'], b_idF, True)
    identB = P.alloc(128, BF16)
    b_idB = Buf()
    CP(P, 'dve', identB, identF, [b_idF], [b_idB])
    sel = P.alloc(256)
    b_sel = Buf()
    P.dma(sel[0:2, :], I['sel'], b_sel, True)
    cvec = P.alloc(16)
    b_cvec = Buf()
    P.dma(cvec, I['cvec'], b_cvec, True)
    scv = P.alloc(16)
    b_scv = Buf()
    ACT(P, scv, cvec, AF.Silu, [b_cvec], [b_scv])
    ones256 = P.alloc(128)
    b_ones = Buf()
    MEMSET(P, 'dve', ones256, 1.0 / 256, [b_ones])
    iota16 = P.alloc(16)
    iota128 = P.alloc(128)
    b_iota = Buf()
    P.op('pool', lambda e: e.iota(iota16, pattern=[[1, 16]], base=0, channel_multiplier=0,
                                  allow_small_or_imprecise_dtypes=True), (), [b_iota])
    P.op('pool', lambda e: e.iota(iota128, pattern=[[1, 128]], base=0, channel_multiplier=0,
                                  allow_small_or_imprecise_dtypes=True), (), [b_iota])
    iota128h = P.alloc(128, BF16)
    CP(P, 'dve', iota128h, iota128, [b_iota], [b_iota])
    csbdF = P.alloc(256)
    b_csF = Buf()
    P.dma(csbdF, I['csbd'], b_csF, True)
    csbd = P.alloc(256, BF16)
    b_cs = Buf()
    CP(P, 'dve', csbd, csbdF, [b_csF], [b_cs])

    modF = P.alloc(4 * 8 * 2)
    b_modF = Buf()
    gtb = [P.alloc(1024) for _ in range(4)]
    b_gtb = [Buf() for _ in range(4)]

    def xsrc(l, r0, n):
        if l == 0:
            if r0 < CT:
                return I['ctx'][r0:r0 + n, :]
            return I['x'][r0 - CT:r0 - CT + n, :]
        return S['xres'][r0:r0 + n, :]

    def phase_adaln(l):
        P.mark()
        gT = [P.alloc(8), P.alloc(8)]
        b_g = Buf()
        P.dma(gT[0], I['g1T'][l], b_g, True)
        b_g2 = Buf()
        P.dma(gT[1], I['g2T'][l], b_g2, True)
        bmT = P.alloc(48)
        b_bm = Buf()
        P.dma(bmT, I['bmodT'][l], b_bm, True)
        bm2 = P.alloc(6 * D)
        b_bm2 = Buf()
        P.dma(bm2[0:2, :], I['bmod2'][l], b_bm2, True)
        rows = P.alloc(2048)
        b_rows = Buf()
        wslot = [P.alloc(8 * 512) for _ in range(2)]
        b_ws = [Buf(), Buf()]
        wsrc = I['w_mod'][l].rearrange('(k p) n -> p k n', p=128)
        modF4 = r4(modF, 4, 8, 2)
        kindmap = {0: 0, 1: 1, 3: 2, 4: 3}
        for nb in range(12):
            ws, bw = wslot[nb % 2], b_ws[nb % 2]
            ws3 = r3(ws, 8, 512)
            P.dma(ws3, wsrc[:, :, nb * 512:(nb + 1) * 512], bw, True)
            kind = nb // 2
            if kind in (2, 5):
                ps = P.bank(0)
                for k in range(8):
                    MM(P, ps[0:2, :], r3(scv, 8, 2)[:, k, :], ws3[:, k, :], k == 0, k == 7, [bw, b_scv], [PB[0]])
                gi = 0 if kind == 2 else 1
                c0 = gi * 1024 + (nb % 2) * 512
                TT(P, 'dve', rows[0:2, c0:c0 + 512], ps[0:2, :], bm2[0:2, nb * 512:(nb + 1) * 512], ALU.add,
                   [PB[0], b_bm2], [b_rows])
            else:
                ps = P.bank(1)
                for cc in range(4):
                    for k in range(8):
                        MM(P, ps[:, cc * 2:cc * 2 + 2], ws3[:, k, cc * 128:(cc + 1) * 128], r3(scv, 8, 2)[:, k, :],
                           k == 0, k == 7, [bw, b_scv], [PB[1]])
                for cc in range(4):
                    gc = nb * 4 + cc
                    kk = gc % 8
                    dst = modF4[:, kindmap[kind], kk, :]
                    if kind in (0, 3):
                        TS(P, 'dve', dst, ps[:, cc * 2:cc * 2 + 2], bmT[:, gc:gc + 1], None, ALU.add, None,
                           [PB[1], b_bm], [b_modF])
                    else:
                        g = gT[0] if kind == 1 else gT[1]
                        TS(P, 'dve', dst, ps[:, cc * 2:cc * 2 + 2], bmT[:, gc:gc + 1], 1.0, ALU.add, ALU.add,
                           [PB[1], b_bm], [b_modF])
                        TS(P, 'dve', dst, dst, g[:, kk:kk + 1], None, ALU.mult, None, [b_modF, b_g, b_g2], [b_modF])
        for gi in range(2):
            for mi in range(2):
                for hf in range(2):
                    ps = P.bank(2 + hf)
                    MM(P, ps, sel[0:2, mi * 128:(mi + 1) * 128], rows[0:2, gi * 1024 + hf * 512: gi * 1024 + hf * 512 + 512],
                       True, True, [b_sel, b_rows], [PB[2 + hf]])
                    CP(P, 'act', gtb[gi * 2 + mi][:, hf * 512:(hf + 1) * 512], ps, [PB[2 + hf]], [b_gtb[gi * 2 + mi]])
        P.barrier()
        P.release()

    class HT:
        def __init__(self, nx=4, nh=2):
            self.nx, self.nh = nx, nh
            self.x = [P.alloc(1024) for _ in range(nx)]
            self.bx = [Buf() for _ in range(nx)]
            self.xn = [P.alloc(1024, BF16) for _ in range(2)]
            self.bxn = [Buf(), Buf()]
            self.junk = P.alloc(1024)
            self.bj = Buf()
            self.ss = P.alloc(2)
            self.bss = Buf()
            self.hT = [P.alloc(8 * TB, BF16) for _ in range(nh)]
            self.bh = [Buf() for _ in range(nh)]
            self.n = 0

        def make(self, l, tb, kind, tbank=0, from_res=False):
            mi = 1 if tb == 0 else 0
            modF4 = r4(modF, 4, 8, 2)
            slot = self.n % self.nh
            hT, bh = self.hT[slot], self.bh[slot]
            hT3 = r3(hT, 8, TB)
            xs, bxs = [], []
            for tt in range(2):
                xi = (self.n * 2 + tt) % self.nx
                xt, bx = self.x[xi], self.bx[xi]
                r0_ = tb * TB + tt * 128
                P.dma(xt, S['xres'][r0_:r0_ + 128, :] if from_res else xsrc(l, r0_, 128), bx, True)
                TTR(P, self.junk, xt, xt, self.ss[:, tt:tt + 1], [bx], [self.bj, self.bss])
                xs.append(xt)
                bxs.append(bx)
            RSTD(P, self.ss, self.ss, D, [self.bss], [self.bss])
            for tt in range(2):
                xn, bxn = self.xn[tt], self.bxn[tt]
                ACT(P, xn, xs[tt], AF.Copy, [bxs[tt], self.bss], [bxn], scale=self.ss[:, tt:tt + 1])
                pt = r3(P.bank(tbank).bitcast(BF16), 8, 128)
                for k in range(8):
                    TR(P, pt[:, k, :], xn[:, k * 128:(k + 1) * 128], identB, [bxn, b_idB], [PB[tbank]])
                for k in range(8):
                    eng = 'dve' if k % 2 == 0 else 'pool'
                    if eng == 'pool':
                        ACT(P, hT3[:, k, tt * 128:(tt + 1) * 128], pt[:, k, :], AF.Identity, [PB[tbank], b_modF], [bh],
                            scale=modF4[:, kind + 1, k, mi:mi + 1], bias=modF4[:, kind, k, mi:mi + 1])
                    else:
                        TS(P, 'dve', hT3[:, k, tt * 128:(tt + 1) * 128], pt[:, k, :], modF4[:, kind + 1, k, mi:mi + 1],
                           modF4[:, kind, k, mi:mi + 1], ALU.mult, ALU.add, [PB[tbank], b_modF], [bh])
            self.n += 1
            return hT3, bh, xs, bxs

    def load_w_bf16(dst3, bdst, src3, ncols, stg, bstg, k_n):
        for k in range(k_n):
            s, bs = stg[k % len(stg)], bstg[k % len(stg)]
            P.dma(s[:, 0:ncols], src3[:, k, :], bs, True)
            if k % 2 == 0:
                CP(P, 'act', dst3[:, k, :], s[:, 0:ncols], [bs], [bdst])
            else:
                CP(P, 'pool', dst3[:, k, :], s[:, 0:ncols], [bs], [bdst])

    AT = {}

    def alloc_attn():
        P.mark()
        KT = P.alloc(8 * NT, BF16)
        Vs = P.alloc(34 * 8 * 65, BF16)
        AT['b_KT'] = Buf()
        AT['b_V'] = Buf()
        AT['KT3'] = r3(KT, 8, NT)
        AT['V4'] = r4(Vs, 34, 8, 65)
        MEMSET(P, 'pool', Vs, 1.0, [AT['b_V']])

    def phase_A(l):
        alloc_attn()
        KT3, V4, b_KT, b_V = AT['KT3'], AT['V4'], AT['b_KT'], AT['b_V']
        P.mark()
        win = I['w_in'][l].rearrange('(k p) n -> p k n', p=128)
        w_tm = P.alloc(8 * 416, BF16)
        w_fm = P.alloc(8 * 768, BF16)
        b_wtm, b_wfm = Buf(), Buf()
        stg = [P.alloc(1184)] * 2
        bstg = [Buf()] * 2
        w_tm3, w_fm3 = r3(w_tm, 8, 416), r3(w_fm, 8, 768)
        for k in range(8):
            s, bs = stg[k % 2], bstg[k % 2]
            P.dma(s, win[:, k, 0:1184], bs, True)
            CP(P, 'act', w_tm3[:, k, :], s[:, 0:416], [bs], [b_wtm])
            CP(P, 'pool', w_fm3[:, k, :], s[:, 416:1184], [bs], [b_wfm])
        gck = P.alloc(1)
        gcq = P.alloc(2)
        b_gc = Buf()
        P.dma(gck, I['g_ckv'][l], b_gc, True)
        b_gq2 = Buf()
        P.dma(gcq, I['g_cqT'][l], b_gq2, True)
        wukv = P.alloc(1024, BF16)
        b_wukv = Buf()
        P.dma(stg[0][:, 0:1024], I['w_ukv'][l], bstg[0], True)
        TS(P, 'dve', wukv, stg[0][:, 0:1024], gck[:, 0:1], None, ALU.mult, None, [bstg[0], b_gc], [b_wukv])
        wuq = P.alloc(2 * 768, BF16)
        b_wuq = Buf()
        wuq3 = r3(wuq, 2, 768)
        for kk in range(2):
            s, bs = stg[1 - kk], bstg[1 - kk]
            P.dma(s[:, 0:768], I['w_uq'][l][kk * 128:(kk + 1) * 128, :], bs, True)
            TS(P, 'dve', wuq3[:, kk, :], s[:, 0:768], gcq[:, kk:kk + 1], None, ALU.mult, None, [bs, b_gq2], [b_wuq])
        gqb = P.alloc(96)
        gkb = P.alloc(96)
        b_gqk = Buf()
        P.dma(gqb, I['g_qn'][l].partition_broadcast(128), b_gqk, True)
        b_gqk2 = Buf()
        P.dma(gkb, I['g_kn'][l].partition_broadcast(128), b_gqk2, True)
        gvec = [b_gqk, b_gqk2]

        ht = HT(nx=2, nh=1)
        tmz = P.alloc(416)
        b_tmz = Buf()
        zfT = P.alloc(2 * TB, BF16)
        b_zfT = Buf()
        zfT3 = r3(zfT, 2, TB)
        sg = P.alloc(TB)
        b_sg = Buf()
        ut = [P.alloc(2 * TB) for _ in range(2)]
        b_ut = [Buf(), Buf()]
        zcs = [P.alloc(512, BF16) for _ in range(2)]
        b_zcs = [Buf(), Buf()]
        st = P.alloc(24)
        b_st = Buf()
        cn = P.alloc(384, BF16)
        b_cn = Buf()
        cT = P.alloc(3 * 128, BF16)
        b_cT = Buf()
        cT3 = r3(cT, 3, 128)
        kvs = P.alloc(1024)
        b_kvs = Buf()
        qs = P.alloc(768)
        b_qs = Buf()
        sq = P.alloc(768)
        b_sq = Buf()
        ktm = P.alloc(768, BF16)
        b_ktm = Buf()
        qtm = P.alloc(768, BF16)
        b_qtm = Buf()
        krr = P.alloc(32)
        b_krr = Buf()
        rtmp = P.alloc(256)
        b_rtmp = Buf()
        rc = P.alloc(32)
        rs = P.alloc(32)
        b_rope = Buf()
        qTt = [P.alloc(8 * 128, BF16) for _ in range(2)]
        b_qTt = [Buf(), Buf()]

        for tb in range(NBLK):
            hT3, bh, xs, bxs = ht.make(l, tb, 0, tbank=0)
            is_ctx = (tb == 0)
            u_t, b_u = ut[tb % 2], b_ut[tb % 2]
            u3 = r3(u_t, 2, TB)
            for cc in range(2):
                bk = 2 + (cc % 2)
                ps = P.bank(bk)[:, 0:TB]
                for k in range(8):
                    MM(P, ps, w_fm3[:, k, cc * 128:(cc + 1) * 128], hT3[:, k, :], k == 0, k == 7, [b_wfm, bh], [PB[bk]])
                CP(P, 'act', zfT3[:, cc, :], ps, [PB[bk]], [b_zfT])
            for c2 in range(2):
                pa = P.bank(2)[:, 0:TB]
                pg = P.bank(3)[:, 0:TB]
                for k in range(8):
                    MM(P, pa, w_fm3[:, k, (2 + c2) * 128:(3 + c2) * 128], hT3[:, k, :], k == 0, k == 7, [b_wfm, bh], [PB[2]])
                for k in range(8):
                    MM(P, pg, w_fm3[:, k, (4 + c2) * 128:(5 + c2) * 128], hT3[:, k, :], k == 0, k == 7, [b_wfm, bh], [PB[3]])
                ACT(P, sg, pg, AF.Sigmoid, [PB[3]], [b_sg])
                TT(P, 'dve', u3[:, c2, :], pa, sg, ALU.mult, [PB[2], b_sg], [b_u])
            P.dma(S['UT'][:, :, tb * TB:(tb + 1) * TB].rearrange('c p t -> p c t'), u3, b_u, False)
            for tt in range(2):
                t0 = tb * TB + tt * 128
                ti = t0 // 128
                tsl = slice(tt * 128, (tt + 1) * 128)
                z, bz = zcs[tt], b_zcs[tt]
                pz = P.bank(4)
                for kc in range(2):
                    MM(P, pz[:, kc * 256:(kc + 1) * 256], zfT3[:, kc, tsl], csbd, True, True, [b_zfT, b_cs], [PB[4]])
                CP(P, 'act', z, pz, [PB[4]], [bz])
                P.dma(S['ZCS'][t0:t0 + 128, :], z, bz, False)
                pp = P.bank(1)[:, 0:416]
                for k in range(8):
                    MM(P, pp, hT3[:, k, tsl], w_tm3[:, k, :], k == 0, k == 7, [bh, b_wtm], [PB[1]])
                CP(P, 'act', tmz, pp, [PB[1]], [b_tmz])
                TTR(P, sq[:, 0:128], tmz[:, 0:128], tmz[:, 0:128], st[:, 0:1], [b_tmz], [b_sq, b_st])
                TTR(P, sq[:, 0:256], tmz[:, 160:416], tmz[:, 160:416], st[:, 1:2], [b_tmz], [b_sq, b_st])
                TS(P, 'dve', st[:, 1:2], st[:, 1:2], 0.5, None, ALU.mult, None, [b_st], [b_st])
                RSTD(P, st[:, 0:2], st[:, 0:2], 128, [b_st], [b_st])
                ACT(P, cn[:, 0:128], tmz[:, 0:128], AF.Copy, [b_tmz, b_st], [b_cn], scale=st[:, 0:1])
                ACT(P, cn[:, 128:384], tmz[:, 160:416], AF.Copy, [b_tmz, b_st], [b_cn], scale=st[:, 1:2])
                pt = r3(P.bank(0).bitcast(BF16), 8, 128)
                for j in range(3):
                    TR(P, pt[:, j, :], cn[:, j * 128:(j + 1) * 128], identB, [b_cn, b_idB], [PB[0]])
                CP(P, 'dve', cT3, pt[:, 0:3, :], [PB[0]], [b_cT])
                for hf in range(2):
                    MM(P, P.bank(5 + hf), cT3[:, 0, :], wukv[:, hf * 512:(hf + 1) * 512], True, True, [b_cT, b_wukv], [PB[5 + hf]])
                for hf in range(2):
                    CP(P, 'act', kvs[:, hf * 512:(hf + 1) * 512], P.bank(5 + hf), [PB[5 + hf]], [b_kvs])
                pq = P.bank(6, 2)
                for nh in range(2):
                    for kk in range(2):
                        MM(P, pq[:, nh * 512:nh * 512 + 384], cT3[:, 1 + kk, :], wuq3[:, kk, nh * 384:(nh + 1) * 384],
                           kk == 0, kk == 1, [b_cT, b_wuq], [PB[6], PB[7]])
                kv4 = r3(kvs, 8, 128)
                CP(P, 'pool', V4[:, ti, :, 0:64], kv4[:, :, 64:128], [b_kvs], [b_V])
                sq3 = r3(sq[:, 0:512], 8, 64)
                TT(P, 'dve', sq3, kv4[:, :, 0:64], kv4[:, :, 0:64], ALU.mult, [b_kvs], [b_sq])
                RED(P, st[:, 8:16], sq3, ALU.add, [b_sq], [b_st])
                TTR(P, sq[:, 512:544], tmz[:, 128:160], tmz[:, 128:160], st[:, 2:3], [b_tmz], [b_sq, b_st])
                TS(P, 'dve', st[:, 8:16], st[:, 8:16], st[:, 2:3], None, ALU.add, None, [b_st], [b_st])
                RSTD(P, st[:, 8:16], st[:, 8:16], 96, [b_st], [b_st])
                ktm3 = r3(ktm, 8, 96)
                TT(P, 'dve', sq3, kv4[:, :, 0:64], st[:, 8:16].unsqueeze(2).to_broadcast([128, 8, 64]), ALU.mult,
                   [b_kvs, b_st], [b_sq])
                TT(P, 'dve', ktm3[:, :, 0:64], sq3, gkb[:, 0:64].unsqueeze(1).to_broadcast([128, 8, 64]), ALU.mult,
                   [b_sq] + gvec, [b_ktm])
                TT(P, 'dve', krr, tmz[:, 128:160], gkb[:, 64:96], ALU.mult, [b_tmz] + gvec, [b_krr])
                if not is_ctx:
                    P.dma(rc, I['ropeC'][t0 - CT:t0 - CT + 128, :], b_rope, True)
                    P.dma(rs, I['ropeS'][t0 - CT:t0 - CT + 128, :], b_rope, True)
                    kr4 = r3(krr, 4, 8)
                    rt4 = r3(rtmp[:, 0:32], 4, 8)
                    for hb in range(2):
                        for blk in range(2):
                            TT(P, 'dve', rt4[:, hb * 2 + blk, :], kr4[:, hb * 2 + (1 - blk), :],
                               r3(rs, 4, 8)[:, hb * 2 + blk, :], ALU.mult, [b_krr, b_rope], [b_rtmp])
                    TT(P, 'dve', krr, krr, rc, ALU.mult, [b_krr, b_rope], [b_krr])
                    TT(P, 'dve', krr, krr, rtmp[:, 0:32], ALU.add, [b_krr, b_rtmp], [b_krr])
                TT(P, 'dve', ktm3[:, :, 64:96], krr.unsqueeze(1).to_broadcast([128, 8, 32]),
                   st[:, 8:16].unsqueeze(2).to_broadcast([128, 8, 32]), ALU.mult, [b_krr, b_st], [b_ktm])
                pk = r3(P.bank(0).bitcast(BF16), 8, 128)
                for h in range(8):
                    TR(P, pk[0:96, h, :], ktm3[:, h, :], identB, [b_ktm, b_idB], [PB[0]])
                CP(P, 'act', KT3[0:96, :, t0:t0 + 128], pk[0:96, :, :], [PB[0]], [b_KT])
                for nh in range(2):
                    CP(P, 'act', qs[:, nh * 384:(nh + 1) * 384], pq[:, nh * 512:nh * 512 + 384], [PB[6], PB[7]], [b_qs])
                q3 = r3(qs, 8, 96)
                s3 = r3(sq, 8, 96)
                TT(P, 'dve', s3, q3, q3, ALU.mult, [b_qs], [b_sq])
                RED(P, st[:, 16:24], s3, ALU.add, [b_sq], [b_st])
                RSTD(P, st[:, 16:24], st[:, 16:24], 96, [b_st], [b_st])
                TT(P, 'dve', s3, q3, st[:, 16:24].unsqueeze(2).to_broadcast([128, 8, 96]), ALU.mult, [b_qs, b_st], [b_sq])
                TT(P, 'dve', q3, s3, gqb.unsqueeze(1).to_broadcast([128, 8, 96]), ALU.mult, [b_sq] + gvec, [b_qs])
                qtm3 = r3(qtm, 8, 96)
                CP(P, 'pool', qtm3[:, :, 0:64], q3[:, :, 0:64], [b_qs], [b_qtm])
                if not is_ctx:
                    q5 = qs.rearrange('p (h c) -> p h c', h=8, c=96)[:, :, 64:96].rearrange('p h (a b) -> p h a b', a=4, b=8)
                    rt5 = rtmp.rearrange('p (h a b) -> p h a b', h=8, a=4, b=8)
                    rs4 = r3(rs, 4, 8)
                    for hb in range(2):
                        for blk in range(2):
                            TT(P, 'dve', rt5[:, :, hb * 2 + blk, :], q5[:, :, hb * 2 + (1 - blk), :],
                               rs4[:, hb * 2 + blk, :].unsqueeze(1).to_broadcast([128, 8, 8]), ALU.mult,
                               [b_qs, b_rope], [b_rtmp])
                    qr = q3[:, :, 64:96]
                    TT(P, 'dve', s3[:, :, 0:32], qr, rc.unsqueeze(1).to_broadcast([128, 8, 32]), ALU.mult,
                       [b_qs, b_rope], [b_sq])
                    TT(P, 'dve', qtm3[:, :, 64:96], s3[:, :, 0:32], r3(rtmp, 8, 32), ALU.add, [b_sq, b_rtmp], [b_qtm])
                else:
                    CP(P, 'pool', qtm3[:, :, 64:96], q3[:, :, 64:96], [b_qs], [b_qtm])
                pqT = r3(P.bank(4).bitcast(BF16), 8, 128)
                for h in range(8):
                    TR(P, pqT[0:96, h, :], qtm3[:, h, :], identB, [b_qtm, b_idB], [PB[4]])
                qq, bqq = qTt[tt], b_qTt[tt]
                CP(P, 'act', r3(qq, 8, 128)[0:96, :, :], pqT[0:96, :, :], [PB[4]], [bqq])
                P.dma(S['QT'][:, :, t0:t0 + 128].rearrange('h p t -> p h t'), r3(qq, 8, 128)[0:96, :, :], bqq, False)
        if debug:
            kd = nc.dram_tensor('KTd%d' % l, [128, 8 * NT], BF16, kind='ExternalOutput').ap()
            vd = nc.dram_tensor('Vd%d' % l, [128, 34 * 8 * 65], BF16, kind='ExternalOutput').ap()
            P.dma(kd, KT3.rearrange('p a b -> p (a b)'), b_KT, False)
            P.dma(vd, V4.rearrange('p a b c -> p (a b c)'), b_V, False)
        P.barrier()
        P.release()

    def phase_B(l):
        KT3, V4, b_KT, b_V = AT['KT3'], AT['V4'], AT['b_KT'], AT['b_V']
        P.mark()
        qt = [P.alloc(8 * 512, BF16) for _ in range(2)]
        b_qt = [Buf(), Buf()]
        E = [P.alloc(512, BF16) for _ in range(3)]
        b_E = [Buf() for _ in range(3)]
        atm = P.alloc(4 * 512)
        b_atm = Buf()
        atb = P.alloc(4 * 512, BF16)
        b_atb = Buf()
        rcp = P.alloc(4)
        b_rcp = Buf()
        aTt = [P.alloc(4 * 512, BF16) for _ in range(2)]
        b_aTt = [Buf(), Buf()]
        scale = 96.0 ** -0.5
        blocks = [(CT + i * 512, 512, list(range(34))) for i in range(8)]
        if l < L - 1:
            blocks.append((0, 256, [0, 1]))
        ne = 0
        for bi, (q0, nq, kts) in enumerate(blocks):
            nqi = nq // 128
            q_t, bq = qt[bi % 2], b_qt[bi % 2]
            q3 = r3(q_t, 8, 512)
            P.dma(q3[0:96, :, 0:nq], S['QT'][:, :, q0:q0 + nq].rearrange('h p t -> p h t'), bq, True)
            atm3 = r3(atm, 4, 512)
            steps = [(h, ki, kt) for h in range(8) for ki, kt in enumerate(kts)]

            def qk(i):
                h, ki, kt = steps[i]
                sb = (ne + i) % 3
                MM(P, P.bank(sb)[:, 0:nq], KT3[0:96, h, kt * 128:(kt + 1) * 128], q3[0:96, h, 0:nq], True, True,
                   [b_KT, bq], [PB[sb]])

            qk(0)
            for i, (h, ki, kt) in enumerate(steps):
                if i + 1 < len(steps):
                    qk(i + 1)
                sb = (ne + i) % 3
                accb = 4 + (h % 2)
                acc = r3(P.bank(accb)[:, 0:4 * 65], 4, 65)
                e_t, be = E[sb], b_E[sb]
                ACT(P, e_t[:, 0:nq], P.bank(sb)[:, 0:nq], AF.Exp, [PB[sb]], [be], scale=scale)
                for qi in range(nqi):
                    MM(P, acc[:, qi, :], e_t[:, qi * 128:(qi + 1) * 128], V4[:, kt, h, :], ki == 0 and qi == 0,
                       ki == len(kts) - 1, [be, b_V], [PB[accb]], skip=True)
                if ki == len(kts) - 1:
                    P.op('dve', lambda e, acc=acc, nqi=nqi: e.reciprocal(out=rcp[:, 0:nqi], in_=acc[:, 0:nqi, 64]),
                         [PB[accb]], [b_rcp])
                    TT(P, 'dve', atm3[:, 0:nqi, h * 64:(h + 1) * 64], acc[:, 0:nqi, 0:64],
                       rcp[:, 0:nqi].unsqueeze(2).to_broadcast([128, nqi, 64]), ALU.mult, [PB[accb], b_rcp], [b_atm])
            ne += len(steps)
            CP(P, 'pool', atb, atm, [b_atm], [b_atb])
            atb3 = r3(atb, 4, 512)
            a_t, ba = aTt[bi % 2], b_aTt[bi % 2]
            a3 = r3(a_t, 4, 512)
            for qi in range(nqi):
                pt = r3(P.bank(6 + (qi % 2)).bitcast(BF16)[:, 0:512], 4, 128)
                for c in range(4):
                    TR(P, pt[:, c, :], atb3[:, qi, c * 128:(c + 1) * 128], identB, [b_atb, b_idB], [PB[6 + (qi % 2)]])
                CP(P, 'act', a3[:, :, qi * 128:(qi + 1) * 128], pt, [PB[6 + (qi % 2)]], [ba])
            P.dma(S['aT'][:, :, q0:q0 + nq].rearrange('c p t -> p c t'), a3[:, :, 0:nq], ba, False)
        P.barrier()
        P.release()
        P.release()

    def phase_C1(l):
        P.mark()
        Z = P.alloc(32 * 512, BF16)
        bZ = Buf()
        Z3 = r3(Z, 32, 512)
        P.dma(Z3, S['ZCS'][CT:NT, :].rearrange('(tt p) n -> p tt n', p=128), bZ, True)
        slab = [P.alloc(8 * 512, BF16) for _ in range(4)]
        b_slab = [Buf() for _ in range(4)]
        fo = [P.alloc(2 * 512, BF16) for _ in range(2)]
        b_fo = [Buf(), Buf()]
        ns = 0
        for kb in range(8):
            for cs in range(2):
                src = I['dftP'][cs].rearrange('(tt p) k -> p tt k', p=128)
                for tg in range(4):
                    sl, bs = slab[ns % 4], b_slab[ns % 4]
                    sl3 = r3(sl, 8, 512)
                    P.dma(sl3, src[:, tg * 8:(tg + 1) * 8, kb * 512:(kb + 1) * 512], bs, True)
                    for t8 in range(8):
                        tt = tg * 8 + t8
                        first = (cs == 0 and tt == 0)
                        last = (cs == 1 and tt == 31)
                        for mc in range(2):
                            MM(P, P.bank(mc), Z3[:, tt, mc * 256 + cs * 128: mc * 256 + cs * 128 + 128], sl3[:, t8, :],
                               first, last, [bZ, bs], [PB[mc]])
                    ns += 1
            f_t, bf = fo[kb % 2], b_fo[kb % 2]
            f3 = r3(f_t, 2, 512)
            CP(P, 'act', f3[:, 0, :], P.bank(0), [PB[0]], [bf])
            CP(P, 'dve', f3[:, 1, :], P.bank(1), [PB[1]], [bf])
            P.dma(S['fT'][:, :, CT + kb * 512:CT + (kb + 1) * 512].rearrange('c p t -> p c t'), f3, bf, False)
        if l < L - 1:
            Zc = P.alloc(2 * 512, BF16)
            bZc = Buf()
            Zc3 = r3(Zc, 2, 512)
            P.dma(Zc3, S['ZCS'][0:CT, :].rearrange('(tt p) n -> p tt n', p=128), bZc, True)
            dc = P.alloc(2 * 2 * 256, BF16)
            bdc = Buf()
            dc4 = r4(dc, 2, 2, 256)
            for cs in range(2):
                P.dma(dc4[:, cs, :, :], I['dftC'][cs].rearrange('(tt p) k -> p tt k', p=128), bdc, True)
            for mc in range(2):
                n = 0
                for cs in range(2):
                    for tt in range(2):
                        MM(P, P.bank(2 + mc)[:, 0:256], Zc3[:, tt, mc * 256 + cs * 128: mc * 256 + cs * 128 + 128],
                           dc4[:, cs, tt, :], n == 0, n == 3, [bZc, bdc], [PB[2 + mc]])
                        n += 1
            f_t, bf = fo[0], b_fo[0]
            f3 = r3(f_t, 2, 512)
            CP(P, 'act', f3[:, 0, 0:256], P.bank(2)[:, 0:256], [PB[2]], [bf])
            CP(P, 'dve', f3[:, 1, 0:256], P.bank(3)[:, 0:256], [PB[3]], [bf])
            P.dma(S['fT'][:, :, 0:CT].rearrange('c p t -> p c t'), f3[:, :, 0:256], bf, False)
        P.barrier()
        P.release()

    def phase_C2(l):
        P.mark()
        LB = 15 + CT + 15 + T + 15
        U = P.alloc(2 * LB)
        bU = Buf()
        U3 = r3(U, 2, LB)
        MEMSET(P, 'pool', U, 0.0, [bU])
        OC, OX = 15, 15 + CT + 15
        for c in range(2):
            P.dma(U3[:, c, OC:OC + CT], S['UT'][c, :, 0:CT], bU, True)
            P.dma(U3[:, c, OX:OX + T], S['UT'][c, :, CT:NT], bU, True)
        wd = P.alloc(62)
        bwd = Buf()
        P.dma(wd, I['wdwT'][l], bwd, True)
        wd3 = r3(wd, 2, 31)
        cp = P.alloc(6)
        bcp = Buf()
        P.dma(cp, I['cvp'][l], bcp, True)
        cp3 = r3(cp, 2, 3)
        A = P.alloc(2 * LB)
        bA = Buf()
        A3 = r3(A, 2, LB)
        NV = LB - 30
        for c in range(2):
            TS(P, 'dve', A3[:, c, 15:15 + NV], U3[:, c, 0:NV], wd3[:, c, 0:1], cp3[:, c, 0:1], ALU.mult, ALU.add,
               [bU, bwd, bcp], [bA])
            for w in range(1, 31):
                STT(P, A3[:, c, 15:15 + NV], U3[:, c, w:w + NV], wd3[:, c, w:w + 1], A3[:, c, 15:15 + NV], ALU.mult, ALU.add,
                    [bU, bwd, bA], [bA])
        sqt = P.alloc(2 * 512)
        bsq = Buf()
        sq3 = r3(sqt, 2, 512)
        mean = P.alloc(512)
        bmean = Buf()
        var = P.alloc(512)
        bvar = Buf()
        y = P.alloc(512)
        by = Buf()
        co = [P.alloc(2 * 512, BF16) for _ in range(2)]
        bco = [Buf(), Buf()]
        blocks = [(OX + i * 512, CT + i * 512, 512) for i in range(8)]
        if l < L - 1:
            blocks.append((OC, 0, 256))
        for bi, (o, t0, n) in enumerate(blocks):
            for c in range(2):
                ACT(P, sq3[:, c, 0:n], A3[:, c, o:o + n], AF.Square, [bA], [bsq])
            for c in range(2):
                MM(P, P.bank(0)[:, 0:n], ones256, A3[:, c, o:o + n], c == 0, c == 1, [b_ones, bA], [PB[0]])
            for c in range(2):
                MM(P, P.bank(1)[:, 0:n], ones256, sq3[:, c, 0:n], c == 0, c == 1, [b_ones, bsq], [PB[1]])
            CP(P, 'act', mean[:, 0:n], P.bank(0)[:, 0:n], [PB[0]], [bmean])
            TT(P, 'dve', var[:, 0:n], mean[:, 0:n], mean[:, 0:n], ALU.mult, [bmean], [bvar])
            TT(P, 'dve', var[:, 0:n], P.bank(1)[:, 0:n], var[:, 0:n], ALU.subtract, [PB[1], bvar], [bvar])
            TS(P, 'dve', var[:, 0:n], var[:, 0:n], EPS, None, ALU.add, None, [bvar], [bvar])
            ACT(P, var[:, 0:n], var[:, 0:n], AF.Sqrt, [bvar], [bvar])
            P.op('dve', lambda e, n=n: e.reciprocal(out=var[:, 0:n], in_=var[:, 0:n]), [bvar], [bvar])
            c_t, bc = co[bi % 2], bco[bi % 2]
            c3 = r3(c_t, 2, 512)
            for c in range(2):
                TT(P, 'dve', y[:, 0:n], A3[:, c, o:o + n], mean[:, 0:n], ALU.subtract, [bA, bmean], [by])
                TT(P, 'dve', y[:, 0:n], y[:, 0:n], var[:, 0:n], ALU.mult, [by, bvar], [by])
                ACT(P, c3[:, c, 0:n], y[:, 0:n], AF.Silu, [by, bcp], [bc], scale=cp3[:, c, 1:2], bias=cp3[:, c, 2:3])
            P.dma(S['cvT'][:, :, t0:t0 + n].rearrange('c p t -> p c t'), c3[:, :, 0:n], bc, False)
        P.barrier()
        P.release()

    def phase_D(l):
        P.mark()
        win = I['w_in'][l].rearrange('(k p) n -> p k n', p=128)
        stg = [P.alloc(1024) for _ in range(3)]
        bstg = [Buf() for _ in range(3)]
        wg = P.alloc(8 * 3072, BF16)
        b_wg = Buf()
        wg3 = r3(wg, 8, 3072)
        for part in range(3):
            load_w_bf16(wg3[:, :, part * 1024:(part + 1) * 1024], b_wg, win[:, :, 1184 + part * 1024:1184 + (part + 1) * 1024],
                        1024, stg, bstg, 8)
        wba = P.alloc(4 * 1024, BF16)
        wbf = P.alloc(2 * 1024, BF16)
        wbc = P.alloc(2 * 1024, BF16)
        wo = P.alloc(8 * 1024, BF16)
        b_wb = Buf()
        load_w_bf16(r3(wba, 4, 1024), b_wb, I['wb_attn'][l].rearrange('(k p) n -> p k n', p=128), 1024, stg, bstg, 4)
        load_w_bf16(r3(wbf, 2, 1024), b_wb, I['wb_fnet'][l].rearrange('(k p) n -> p k n', p=128), 1024, stg, bstg, 2)
        load_w_bf16(r3(wbc, 2, 1024), b_wb, I['wb_conv'][l].rearrange('(k p) n -> p k n', p=128), 1024, stg, bstg, 2)
        load_w_bf16(r3(wo, 8, 1024), b_wb, I['w_out'][l].rearrange('(k p) n -> p k n', p=128), 1024, stg, bstg, 8)
        wbr = [(r3(wba, 4, 1024), 4), (r3(wbf, 2, 1024), 2), (r3(wbc, 2, 1024), 2)]
        wo3 = r3(wo, 8, 1024)
        ht = HT()
        sgt = P.alloc(24 * TB)
        b_sgt = Buf()
        sg3 = r3(sgt, 24, TB)
        br = [P.alloc(8 * TB, BF16) for _ in range(2)]
        b_br = [Buf(), Buf()]
        mT = P.alloc(8 * TB, BF16)
        b_mT = Buf()
        mT3 = r3(mT, 8, TB)
        t1 = P.alloc(TB)
        t2 = P.alloc(TB)
        b_t1, b_t2 = Buf(), Buf()
        xo = [P.alloc(1024) for _ in range(2)]
        b_xo = [Buf(), Buf()]
        nblk = NBLK if l < L - 1 else NBLK
        for tb in range(nblk):
            if tb == 0 and l == L - 1:
                continue
            mi = 1 if tb == 0 else 0
            hT3, bh, xs, bxs = ht.make(l, tb, 0, tbank=0)
            b_t, bb = br[tb % 2], b_br[tb % 2]
            b3 = r3(b_t, 8, TB)
            tsl = slice(tb * TB, (tb + 1) * TB)
            P.dma(b3[:, 0:4, :], S['aT'][:, :, tsl].rearrange('c p t -> p c t'), bb, True)
            P.dma(b3[:, 4:6, :], S['fT'][:, :, tsl].rearrange('c p t -> p c t'), bb, True)
            P.dma(b3[:, 6:8, :], S['cvT'][:, :, tsl].rearrange('c p t -> p c t'), bb, True)
            for gc in range(24):
                bk = 1 + (gc % 2)
                ps = P.bank(bk)[:, 0:TB]
                for k in range(8):
                    MM(P, ps, wg3[:, k, gc * 128:(gc + 1) * 128], hT3[:, k, :], k == 0, k == 7, [b_wg, bh], [PB[bk]])
                ACT(P, sg3[:, gc, :], ps, AF.Sigmoid, [PB[bk]], [b_sgt])
            for oc in range(8):
                koff = 0
                for bi, (w3, nk) in enumerate(wbr):
                    ps = P.bank(3 + bi)[:, 0:TB]
                    for kc in range(nk):
                        MM(P, ps, w3[:, kc, oc * 128:(oc + 1) * 128], b3[:, koff + kc, :], kc == 0, kc == nk - 1, [b_wb, bb], [PB[3 + bi]])
                    koff += nk
                TT(P, 'dve', t1, P.bank(3)[:, 0:TB], sg3[:, oc, :], ALU.mult, [PB[3], b_sgt], [b_t1])
                TT(P, 'dve', t2, P.bank(4)[:, 0:TB], sg3[:, 8 + oc, :], ALU.mult, [PB[4], b_sgt], [b_t2])
                TT(P, 'dve', t1, t1, t2, ALU.add, [b_t1, b_t2], [b_t1])
                TT(P, 'dve', t2, P.bank(5)[:, 0:TB], sg3[:, 16 + oc, :], ALU.mult, [PB[5], b_sgt], [b_t2])
                TT(P, 'dve', mT3[:, oc, :], t1, t2, ALU.add, [b_t1, b_t2], [b_mT])
            for tt in range(2):
                x_o, bxo = xo[tt], b_xo[tt]
                for nh in range(2):
                    ps = P.bank(6 + nh)
                    for k in range(8):
                        MM(P, ps, mT3[:, k, tt * 128:(tt + 1) * 128], wo3[:, k, nh * 512:(nh + 1) * 512], k == 0, k == 7,
                           [b_mT, b_wb], [PB[6 + nh]])
                    TT(P, 'dve', x_o[:, nh * 512:(nh + 1) * 512], ps, gtb[mi][:, nh * 512:(nh + 1) * 512], ALU.mult,
                       [PB[6 + nh], b_gtb[mi]], [bxo])
                TT(P, 'pool', x_o, x_o, xs[tt], ALU.add, [bxo, bxs[tt]], [bxo])
                P.dma(S['xres'][tb * TB + tt * 128: tb * TB + (tt + 1) * 128, :], x_o, bxo, False)
        P.barrier()
        P.release()

    def phase_T(l):
        P.mark()
        us = [P.alloc(1024) for _ in range(2)]
        b_us = [Buf(), Buf()]
        vsl = [P.alloc(1024) for _ in range(2)]
        b_vs = [Buf(), Buf()]
        ub = [P.alloc(1024, BF16) for _ in range(2)]
        b_ub = [Buf(), Buf()]
        uo = [P.alloc(1024, BF16) for _ in range(2)]
        b_uo = [Buf(), Buf()]
        vb = [P.alloc(1024, BF16) for _ in range(2)]
        b_vb = [Buf(), Buf()]
        usrc = I['u_tab'][l].rearrange('(i j) d -> j i d', j=NCH)
        vsrc = I['v_tab'][l].rearrange('(i j) d -> j i d', j=NCH)
        for j in range(NCH):
            s = j % 2
            P.dma(us[s], usrc[j], b_us[s], True)
            P.dma(vsl[s], vsrc[j], b_vs[s], True)
            CP(P, 'pool', vb[s], vsl[s], [b_vs[s]], [b_vb[s]])
            P.dma(S['vs'][j], vb[s], b_vb[s], False)
            CP(P, 'dve', ub[s], us[s], [b_us[s]], [b_ub[s]])
            pt = r3(P.bank(s).bitcast(BF16), 8, 128)
            for k in range(8):
                TR(P, pt[:, k, :], ub[s][:, k * 128:(k + 1) * 128], identB, [b_ub[s], b_idB], [PB[s]])
            CP(P, 'act', uo[s], P.bank(s).bitcast(BF16), [PB[s]], [b_uo[s]])
            P.dma(S['uTs'][j], uo[s], b_uo[s], False)
        P.barrier()
        P.release()

    def phase_E(l):
        P.mark()
        last = (l == L - 1)
        wq = P.alloc(8 * 1024, BF16)
        b_wq = Buf()
        wq3 = r3(wq, 8, 1024)
        skb = P.alloc(8 * 256, BF16)
        b_sk = Buf()
        P.mark()
        stg = [P.alloc(1024) for _ in range(2)]
        bstg = [Buf(), Buf()]
        load_w_bf16(wq3, b_wq, I['w_query'][l].rearrange('(k p) n -> p k n', p=128), 1024, stg, bstg, 8)
        for hf in range(2):
            P.dma(stg[hf], I['skbd'][l][:, hf * 1024:(hf + 1) * 1024], bstg[hf], True)
            CP(P, 'dve', skb[:, hf * 1024:(hf + 1) * 1024], stg[hf], [bstg[hf]], [b_sk])
        P.barrier()
        P.release()
        skb3 = r3(skb, 8, 256)
        ht = HT(nx=4, nh=1)
        qT = P.alloc(8 * TB, BF16)
        b_qT = Buf()
        qT3 = r3(qT, 8, TB)
        sc = P.alloc(2048)
        b_sc = Buf()
        sc4 = r4(sc, 8, 2, 128)
        tmp = P.alloc(256)
        b_tmp = Buf()
        val = P.alloc(256)
        b_val = Buf()
        val4 = r4(val, 8, 2, 16)
        idx = P.alloc(256).bitcast(U32)
        b_idx = Buf()
        idx4 = r4(idx, 8, 2, 16)
        idxf = P.alloc(256)
        b_idxf = Buf()
        idxf4 = r4(idxf, 8, 2, 16)
        cand = P.alloc(2048)
        b_cand = Buf()
        cand4 = r4(cand, 8, 16, 16)
        ctmp = P.alloc(256)
        b_ctmp = Buf()
        best = P.alloc(128)
        b_best = Buf()
        best3 = r3(best, 8, 16)
        pos = P.alloc(128).bitcast(U32)
        b_pos = Buf()
        pos3 = r3(pos, 8, 16)
        pa = P.alloc(128).bitcast(U32)
        pb_ = P.alloc(128).bitcast(U32)
        paf = P.alloc(128)
        pbf = P.alloc(128)
        b_pab = Buf()
        oh = cand
        b_oh = b_cand
        oh4 = r4(oh, 8, 16, 16)
        tok3 = P.alloc(3 * 128)
        b_tok = Buf()
        tok33 = r3(tok3, 3, 128)
        gsum = P.alloc(8)
        b_gsum = Buf()
        ijwT = P.alloc(3 * TB, BF16)
        b_ijw = Buf()
        ijwT3 = r3(ijwT, 3, TB)
        GT = 16
        Aoh = [P.alloc(GT * 128, BF16) for _ in range(2)]
        Boh = [P.alloc(GT * 128, BF16) for _ in range(2)]
        b_A = [Buf(), Buf()]
        b_B = [Buf(), Buf()]
        G = P.alloc(TB * 128, BF16)
        b_G = Buf()
        G3 = r3(G, TB, 128)
        NS = 4
        utr = [P.alloc(1024, BF16) for _ in range(NS)]
        b_utr = [Buf() for _ in range(NS)]
        vtr = [P.alloc(1024, BF16) for _ in range(NS)]
        b_vtr = [Buf() for _ in range(NS)]
        gs = [P.alloc(TB, BF16) for _ in range(2)]
        b_gs = [Buf(), Buf()]
        av = [P.alloc(TB, BF16) for _ in range(2)]
        b_av = [Buf(), Buf()]
        xo = [P.alloc(1024) for _ in range(2)]
        b_xo = [Buf(), Buf()]
        iota128b = iota128h.unsqueeze(1).to_broadcast([128, GT, 128])
        nchunk = 0
        for tb in range(NBLK):
            if tb == 0 and last:
                continue
            mi = 1 if tb == 0 else 0
            hT3, bh, xs, bxs = ht.make(l, tb, 2, tbank=0, from_res=True)
            for h in range(8):
                bk = 1 + (h % 2)
                ps = P.bank(bk)[:, 0:TB]
                for k in range(8):
                    MM(P, ps, wq3[:, k, h * 128:(h + 1) * 128], hT3[:, k, :], k == 0, k == 7, [b_wq, bh], [PB[bk]])
                CP(P, 'act', qT3[:, h, :], ps, [PB[bk]], [b_qT])
            for tt in range(2):
                tsl = slice(tt * 128, (tt + 1) * 128)
                for h in range(8):
                    bk = 1 + (h % 2)
                    ps = P.bank(bk)[:, 0:256]
                    MM(P, ps, qT3[:, h, tsl], skb3[:, h, :], True, True, [b_qT, b_sk], [PB[bk]])
                    CP(P, 'act', sc[:, h * 256:(h + 1) * 256], ps, [PB[bk]], [b_sc])
                for h in range(8):
                    for p in range(2):
                        s_hp = sc4[:, h, p, :]
                        v16 = val4[:, h, p, :]
                        i16 = idx4[:, h, p, :]
                        P.op('dve', lambda e, o=v16[:, 0:8], i=s_hp: e.max(out=o, in_=i), [b_sc], [b_val])
                        P.op('dve', lambda e, o=i16[:, 0:8], m=v16[:, 0:8], i=s_hp: e.max_index(out=o, in_max=m, in_values=i),
                             [b_sc, b_val], [b_idx])
                        P.op('dve', lambda e, o=tmp[:, 0:128], m=v16[:, 0:8], i=s_hp: e.match_replace(
                            out=o, in_to_replace=m, in_values=i, imm_value=-1e30), [b_sc, b_val], [b_tmp])
                        P.op('dve', lambda e, o=v16[:, 8:16], i=tmp[:, 0:128]: e.max(out=o, in_=i), [b_tmp], [b_val])
                        P.op('dve', lambda e, o=i16[:, 8:16], m=v16[:, 8:16], i=tmp[:, 0:128]: e.max_index(
                            out=o, in_max=m, in_values=i), [b_tmp, b_val], [b_idx])
                CP(P, 'dve', idxf, idx, [b_idx], [b_idxf])
                TT(P, 'dve', cand4, val4[:, :, 0, :].unsqueeze(3).to_broadcast([128, 8, 16, 16]),
                   val4[:, :, 1, :].unsqueeze(2).to_broadcast([128, 8, 16, 16]), ALU.add, [b_val], [b_cand])
                for h in range(8):
                    c_h = cand[:, h * 256:(h + 1) * 256]
                    b16 = best3[:, h, :]
                    p16 = pos3[:, h, :]
                    P.op('dve', lambda e, o=b16[:, 0:8], i=c_h: e.max(out=o, in_=i), [b_cand], [b_best])
                    P.op('dve', lambda e, o=p16[:, 0:8], m=b16[:, 0:8], i=c_h: e.max_index(out=o, in_max=m, in_values=i),
                         [b_cand, b_best], [b_pos])
                    P.op('dve', lambda e, o=ctmp, m=b16[:, 0:8], i=c_h: e.match_replace(
                        out=o, in_to_replace=m, in_values=i, imm_value=-1e30), [b_cand, b_best], [b_ctmp])
                    P.op('dve', lambda e, o=b16[:, 8:16], i=ctmp: e.max(out=o, in_=i), [b_ctmp], [b_best])
                    P.op('dve', lambda e, o=p16[:, 8:16], m=b16[:, 8:16], i=ctmp: e.max_index(out=o, in_max=m, in_values=i),
                         [b_ctmp, b_best], [b_pos])
                P.op('dve', lambda e: e.tensor_single_scalar(out=pa, in_=pos, scalar=4, op=ALU.logical_shift_right),
                     [b_pos], [b_pab])
                P.op('dve', lambda e: e.tensor_single_scalar(out=pb_, in_=pos, scalar=15, op=ALU.bitwise_and),
                     [b_pos], [b_pab])
                CP(P, 'dve', paf, pa, [b_pab], [b_pab])
                CP(P, 'dve', pbf, pb_, [b_pab], [b_pab])
                for which, pf in enumerate((paf, pbf)):
                    pf3 = r3(pf, 8, 16)
                    TT(P, 'dve', oh4, iota16.unsqueeze(1).unsqueeze(1).to_broadcast([128, 8, 16, 16]),
                       pf3.unsqueeze(3).to_broadcast([128, 8, 16, 16]), ALU.is_equal, [b_iota, b_pab], [b_oh])
                    TT(P, 'dve', oh4, oh4, idxf4[:, :, which, :].unsqueeze(2).to_broadcast([128, 8, 16, 16]), ALU.mult,
                       [b_oh, b_idxf], [b_oh])
                    RED(P, tok33[:, which, :], r3(oh, 128, 16), ALU.add, [b_oh], [b_tok])
                w3 = r3(tok33[:, 2, :], 8, 16)
                TT(P, 'dve', w3, best3, best3[:, :, 0:1].to_broadcast([128, 8, 16]), ALU.subtract, [b_best], [b_tok])
                ACT(P, tok33[:, 2, :], tok33[:, 2, :], AF.Exp, [b_tok], [b_tok])
                RED(P, gsum, w3, ALU.add, [b_tok], [b_gsum])
                P.op('dve', lambda e: e.reciprocal(out=gsum, in_=gsum), [b_gsum], [b_gsum])
                TT(P, 'dve', w3, w3, gsum.unsqueeze(2).to_broadcast([128, 8, 16]), ALU.mult, [b_tok, b_gsum], [b_tok])
                pT = r3(P.bank(3)[:, 0:384], 3, 128)
                for c in range(3):
                    TR(P, pT[:, c, :], tok33[:, c, :], identF, [b_tok, b_idF], [PB[3]])
                CP(P, 'act', ijwT3[:, :, tsl], pT, [PB[3]], [b_ijw])
                if debug and l == 0 and tb == 1 and tt == 0:
                    dsc_ = nc.dram_tensor('dbg_sc', [128, 2048], F32, kind='ExternalOutput').ap()
                    dtok = nc.dram_tensor('dbg_tok', [128, 384], F32, kind='ExternalOutput').ap()
                    dbest = nc.dram_tensor('dbg_best', [128, 128], F32, kind='ExternalOutput').ap()
                    dval = nc.dram_tensor('dbg_val', [128, 256], F32, kind='ExternalOutput').ap()
                    didx = nc.dram_tensor('dbg_idx', [128, 256], F32, kind='ExternalOutput').ap()
                    P.dma(dsc_, sc, b_sc, False)
                    P.dma(dtok, tok3, b_tok, False)
                    P.dma(dbest, best, b_best, False)
                    P.dma(dval, val, b_val, False)
                    P.dma(didx, idxf, b_idxf, False)
                    P.barrier()
            for g in range(TB // GT):
                A_t, bA = Aoh[g % 2], b_A[g % 2]
                B_t, bB = Boh[g % 2], b_B[g % 2]
                A3, B3 = r3(A_t, GT, 128), r3(B_t, GT, 128)
                gsl = slice(g * GT, (g + 1) * GT)
                TT(P, 'dve', A3, iota128b, ijwT3[:, 0, gsl].unsqueeze(2).to_broadcast([128, GT, 128]), ALU.is_equal,
                   [b_iota, b_ijw], [bA])
                TT(P, 'dve', B3, iota128b, ijwT3[:, 1, gsl].unsqueeze(2).to_broadcast([128, GT, 128]), ALU.is_equal,
                   [b_iota, b_ijw], [bB])
                TT(P, 'dve', B3, B3, ijwT3[:, 2, gsl].unsqueeze(2).to_broadcast([128, GT, 128]), ALU.mult,
                   [bB, b_ijw], [bB])
                for t4 in range(GT // 4):
                    bk = 1 + (t4 % 2)
                    for ti in range(4):
                        tloc = t4 * 4 + ti
                        MM(P, P.bank(bk)[:, ti * 128:(ti + 1) * 128], A3[:, tloc, :], B3[:, tloc, :], True, True, [bA, bB], [PB[bk]])
                    tg0 = g * GT + t4 * 4
                    CP(P, 'act', G[:, tg0 * 128:(tg0 + 4) * 128], P.bank(bk), [PB[bk]], [b_G])
            if debug and l == 0 and tb == 1:
                dG = nc.dram_tensor('dbg_G', [128, 8 * 128], BF16, kind='ExternalOutput').ap()
                dij = nc.dram_tensor('dbg_ijwT', [128, 3 * TB], BF16, kind='ExternalOutput').ap()
                dhT = nc.dram_tensor('dbg_hT', [128, 8 * TB], BF16, kind='ExternalOutput').ap()
                P.dma(dG, G[:, 0:1024], b_G, False)
                P.dma(dij, ijwT, b_ijw, False)
                P.dma(dhT, hT3.rearrange('p a b -> p (a b)'), bh, False)
                P.barrier()
            def load(j):
                s_ = (nchunk + j) % NS
                P.dma(utr[s_], S['uTs'][j], b_utr[s_], True, q='sp')
                P.dma(vtr[s_], S['vs'][j], b_vtr[s_], True, q='sp')

            def mm1(j):
                s_ = (nchunk + j) % NS
                u3 = r3(utr[s_], 8, 128)
                sb = 2 + ((nchunk + j) % 2)
                for k in range(8):
                    MM(P, P.bank(sb)[:, 0:TB], u3[:, k, :], hT3[:, k, :], k == 0, k == 7, [b_utr[s_], bh], [PB[sb]])

            for j in range(min(NS, NCH)):
                load(j)
            mm1(0)
            for j in range(NCH):
                if j + 1 < NCH:
                    mm1(j + 1)
                s_ = (nchunk + j) % NS
                sb = 2 + ((nchunk + j) % 2)
                g_t, bg = gs[(nchunk + j) % 2], b_gs[(nchunk + j) % 2]
                ACT(P, g_t, P.bank(sb)[:, 0:TB], AF.Gelu_apprx_tanh, [PB[sb]], [bg])
                a_t, ba = av[(nchunk + j) % 2], b_av[(nchunk + j) % 2]
                TT(P, 'dve', a_t, g_t, G3[:, :, j], ALU.mult, [bg, b_G], [ba])
                for tt in range(2):
                    for nh in range(2):
                        ob = 4 + tt * 2 + nh
                        MM(P, P.bank(ob), a_t[:, tt * 128:(tt + 1) * 128], vtr[s_][:, nh * 512:(nh + 1) * 512],
                           j == 0, j == NCH - 1, [ba, b_vtr[s_]], [PB[ob]])
                if j + NS < NCH:
                    load(j + NS)
            nchunk += NCH
            for tt in range(2):
                x_o, bxo = xo[tt], b_xo[tt]
                for nh in range(2):
                    ob = 4 + tt * 2 + nh
                    TT(P, 'dve', x_o[:, nh * 512:(nh + 1) * 512], P.bank(ob), gtb[2 + mi][:, nh * 512:(nh + 1) * 512], ALU.mult,
                       [PB[ob], b_gtb[2 + mi]], [bxo])
                TT(P, 'pool', x_o, x_o, xs[tt], ALU.add, [bxo, bxs[tt]], [bxo])
                r0 = tb * TB + tt * 128
                if last:
                    P.dma(out_d[r0 - CT:r0 - CT + 128, :], x_o, bxo, False)
                else:
                    P.dma(S['xres'][r0:r0 + 128, :], x_o, bxo, False)
        P.barrier()
        P.release()

    phases = []
    for l in range(n_layers):
        phases += [('adaln', phase_adaln, l), ('A', phase_A, l), ('B', phase_B, l), ('C1', phase_C1, l),
                   ('C2', phase_C2, l), ('D', phase_D, l), ('T', phase_T, l), ('E', phase_E, l)]
    for name, fn, l in phases:
        fn(l)
        if stop_after is not None and stop_after == (name, l):
            break
    P.emit()
    return nc


def _consts():
    t = np.arange(T, dtype=np.float64)
    row = np.repeat(np.arange(T // 64, dtype=np.float32), 64)
    col = np.tile(np.arange(64, dtype=np.float32), T // 64)
    inv = (np.float32(10000.0) ** (-np.arange(0, 16, 2, dtype=np.float32) / np.float32(16))).astype(np.float32)
    ar = row[:, None] * inv
    ac = col[:, None] * inv
    ang = np.concatenate([ar, ar, ac, ac], axis=-1).astype(np.float32)
    ropeC = np.cos(ang).astype(np.float32)
    sn = np.sin(ang).astype(np.float32)
    sign = np.tile(np.concatenate([-np.ones(8, np.float32), np.ones(8, np.float32)]), 2)
    ropeS = (sn * sign[None, :]).astype(np.float32)
    n = (np.outer(np.arange(T), np.arange(T)) % T).astype(np.float64)
    dftP = np.stack([np.cos(2 * np.pi * n / T) / 64.0, -np.sin(2 * np.pi * n / T) / 64.0]).astype(ml_dtypes.bfloat16)
    n2 = (np.outer(np.arange(CT), np.arange(CT)) % CT).astype(np.float64)
    dftC = np.stack([np.cos(2 * np.pi * n2 / CT) / 16.0, -np.sin(2 * np.pi * n2 / CT) / 16.0]).astype(ml_dtypes.bfloat16)
    c = np.arange(128)
    m = np.arange(128)
    same = (c[:, None] // 64) == (m[None, :] // 64)
    ph = 2 * np.pi * ((c[:, None] % 64) * (m[None, :] % 64) % 64) / 64.0
    csbd = np.concatenate([np.where(same, np.cos(ph) / 8.0, 0.0), np.where(same, np.sin(ph) / 8.0, 0.0)], axis=1).astype(np.float32)
    ident = np.eye(128, dtype=np.float32)
    sel = np.zeros((2, 256), np.float32)
    sel[0, 0:128] = 1.0
    sel[1, 128:256] = 1.0
    return dict(ropeC=ropeC, ropeS=ropeS, dftP=dftP, dftC=dftC, csbd=csbd, ident=ident, sel=sel)


_CONSTS = None
_NC_CACHE = {}


def make_in_maps(inp, cores):
    global _CONSTS
    if _CONSTS is None:
        _CONSTS = _consts()
    f = lambda a: np.ascontiguousarray(np.asarray(a, dtype=np.float32))
    shared = dict(_CONSTS)
    shared['w_mod'] = f(inp['w_mod'])
    bm = f(inp['b_mod'])
    shared['bmodT'] = np.ascontiguousarray(bm.reshape(L, 48, 128).transpose(0, 2, 1))
    shared['bmod2'] = np.ascontiguousarray(np.repeat(bm[:, None, :], 2, axis=1))
    shared['g1T'] = np.ascontiguousarray(f(inp['g_norm1']).reshape(L, 8, 128).transpose(0, 2, 1))
    shared['g2T'] = np.ascontiguousarray(f(inp['g_norm2']).reshape(L, 8, 128).transpose(0, 2, 1))
    shared['w_in'] = f(inp['w_in'])
    shared['g_ckv'] = f(inp['g_ckv']).reshape(L, 128, 1)
    shared['w_ukv'] = f(inp['w_ukv'])
    shared['g_cqT'] = np.ascontiguousarray(f(inp['g_cq']).reshape(L, 2, 128).transpose(0, 2, 1))
    shared['w_uq'] = f(inp['w_uq'])
    shared['g_qn'] = f(inp['g_qn']).reshape(L, 1, 96)
    shared['g_kn'] = f(inp['g_kn']).reshape(L, 1, 96)
    wdw = f(inp['w_dw']).reshape(L, 31, 2, 128)
    shared['wdwT'] = np.ascontiguousarray(wdw.transpose(0, 3, 2, 1)).reshape(L, 128, 62)
    cvp = np.stack([f(inp['b_dw']), f(inp['g_cln']), f(inp['b_cln'])], axis=-1)
    shared['cvp'] = np.ascontiguousarray(cvp.reshape(L, 2, 128, 3).transpose(0, 2, 1, 3)).reshape(L, 128, 6)
    for k in ('wb_attn', 'wb_fnet', 'wb_conv', 'w_out', 'w_query', 'u_tab', 'v_tab'):
        shared[k] = f(inp[k])
    sk = f(inp['sub_keys'])
    skbd = np.zeros((L, 128, 8, 256), np.float32)
    for p in range(2):
        skbd[:, p * 64:(p + 1) * 64, :, p * 128:(p + 1) * 128] = sk[:, :, p].transpose(0, 3, 1, 2)
    shared['skbd'] = skbd.reshape(L, 128, 2048)
    x = f(inp['x'])
    ctx = f(inp['ctx'])
    c = f(inp['c'])
    cc = f(inp['c_ctx'])
    maps = []
    for b in cores:
        m = dict(shared)
        m['x'] = x[b]
        m['ctx'] = ctx[b]
        cv = np.stack([c[b].reshape(8, 128).T, cc.reshape(8, 128).T], axis=-1)
        m['cvec'] = np.ascontiguousarray(cv).reshape(128, 16)
        maps.append(m)
    return maps


def kernel(**inputs):
    if 'full' not in _NC_CACHE:
        _NC_CACHE['full'] = build_program()
    nc = _NC_CACHE['full']
    maps = make_in_maps(inputs, list(range(8)))
    res = run_bass_kernel_spmd(nc, maps, core_ids=list(range(8)))
    return np.stack([np.asarray(r['out'], dtype=np.float32) for r in res.results], axis=0)
```

```python
import numpy as np
import ml_dtypes
import concourse.bass as bass
import concourse.mybir as mybir
from concourse.bass_utils import run_bass_kernel_spmd

F32 = mybir.dt.float32
BF16 = mybir.dt.bfloat16
U32 = mybir.dt.uint32
AF = mybir.ActivationFunctionType
ALU = mybir.AluOpType
AX = mybir.AxisListType

L = 2
D = 1024
T = 4096
CT = 256
NT = T + CT
TB = 256
NBLK = NT // TB
EPS = 1e-6
NCH = 128

CE = ('pe', 'act', 'dve', 'pool')
ENG = ('pe', 'act', 'dve', 'pool', 'sp')


class Buf:
    __slots__ = ('name', 'w', 'r', 'dkey', 'dcnt')

    def __init__(self, name=''):
        self.name = name
        self.w = None
        self.r = {}
        self.dkey = None
        self.dcnt = 0


class Prog:
    def __init__(self, nc, sbuf_words=207 * 256):
        self.nc = nc
        self.q = {e: [] for e in ENG}
        self.semh = {e: nc.alloc_semaphore('s_' + e) for e in CE}
        self.cnt = {e: 0 for e in CE}
        self.known = {e: {} for e in ENG}
        self.dtot = {}
        self.nd = 0
        self.sb = nc.alloc_sbuf_tensor('sb_all', [128, sbuf_words], F32)
        self.sb_words = sbuf_words
        self.sb_off = 0
        self.ps = nc.alloc_psum_tensor('ps_all', [128, 4096], F32)
        self.marks = []
        self.free_keys = []
        self.scope_keys = []
        self.pbuf = [Buf('bank%d' % i) for i in range(8)]

    def alloc(self, nelem, dtype=F32):
        nbytes = nelem * (2 if dtype == BF16 else 4)
        words = (nbytes + 3) // 4
        words = (words + 7) // 8 * 8
        assert self.sb_off + words <= self.sb_words, ('SBUF overflow', self.sb_off, words)
        ap = self.sb[:, self.sb_off:self.sb_off + words]
        self.sb_off += words
        if dtype != F32:
            ap = ap.bitcast(dtype)
        return ap[:, 0:nelem]

    def mark(self):
        self.marks.append((self.sb_off, len(self.scope_keys)))

    def release(self):
        self.sb_off, nk = self.marks.pop()
        self.free_keys += self.scope_keys[nk:]
        del self.scope_keys[nk:]

    def bank(self, b, n=1):
        return self.ps[:, b * 512:(b + n) * 512]

    def _wait(self, e, key, count):
        if key == e and e == 'pe':
            return
        if self.known[e].get(key, 0) >= count:
            return
        self.known[e][key] = count
        h = self.semh[key]
        self.q[e].append(lambda eng, h=h, c=count: eng.wait_ge(h, c))

    def _deps(self, e, r, w):
        for b in r:
            if b.w is not None:
                self._wait(e, *b.w)
        for b in w:
            if b.w is not None:
                self._wait(e, *b.w)
            for k, c in b.r.items():
                self._wait(e, k, c)

    def op(self, e, fn, r=(), w=()):
        self._deps(e, r, w)
        self.cnt[e] += 1
        c = self.cnt[e]
        h = self.semh[e]
        self.q[e].append(lambda eng, fn=fn, h=h: fn(eng).then_inc(h, 1))
        for b in r:
            b.r[e] = c
        for b in w:
            b.w = (e, c)
            b.r = {}

    def dma(self, out, in_, buf, load, q='sp', extra_r=(), **kw):
        if buf.dkey is None:
            if self.free_keys:
                buf.dkey = self.free_keys.pop()
            else:
                buf.dkey = 'd%d' % self.nd
                self.nd += 1
                self.semh[buf.dkey] = self.nc.alloc_semaphore(buf.dkey)
            if self.marks:
                self.scope_keys.append(buf.dkey)
            buf.dcnt = self.dtot.get(buf.dkey, 0)
        if load:
            self._deps(q, list(extra_r), [buf])
        else:
            self._deps(q, [buf] + list(extra_r), [])
        buf.dcnt += 16
        c = buf.dcnt
        self.dtot[buf.dkey] = c
        h = self.semh[buf.dkey]
        self.q[q].append(lambda eng, h=h, out=out, in_=in_, kw=kw: eng.dma_start(out=out, in_=in_, **kw).then_inc(h, 16))
        if load:
            buf.w = (buf.dkey, c)
            buf.r = {}
        else:
            buf.r[buf.dkey] = c

    def barrier(self, engines=ENG):
        for e in engines:
            for k in CE:
                if k != e and self.cnt[k] > 0:
                    self._wait(e, k, self.cnt[k])
            for k, c in self.dtot.items():
                self._wait(e, k, c)

    def emit(self):
        nc = self.nc
        self.barrier(ENG)
        with nc.Block() as block:
            @block.tensor
            def _(eng):
                for f in self.q['pe']:
                    f(eng)

            @block.scalar
            def _(eng):
                for f in self.q['act']:
                    f(eng)

            @block.vector
            def _(eng):
                for f in self.q['dve']:
                    f(eng)

            @block.gpsimd
            def _(eng):
                for f in self.q['pool']:
                    f(eng)

            @block.sync
            def _(eng):
                for f in self.q['sp']:
                    f(eng)


def MM(P, out, lhsT, rhs, start, stop, r, w, skip=False):
    P.op('pe', lambda e: e.matmul(out, lhsT=lhsT, rhs=rhs, start=start, stop=stop, skip_group_check=skip), r, w)


def TR(P, out, in_, ident, r, w):
    P.op('pe', lambda e: e.transpose(out=out, in_=in_, identity=ident), r, w)


def ACT(P, out, in_, func, r, w, scale=None, bias=None, accum=None):
    kw = {}
    if scale is not None:
        kw['scale'] = scale
    if bias is not None:
        kw['bias'] = bias
    if accum is not None:
        kw['accum_out'] = accum
    P.op('act', lambda e: e.activation(out=out, in_=in_, func=func, **kw), r, w)


def TT(P, eng, out, in0, in1, op, r, w):
    P.op(eng, lambda e: e.tensor_tensor(out=out, in0=in0, in1=in1, op=op), r, w)


def TS(P, eng, out, in0, s1, s2, op0, op1, r, w):
    if op1 is None:
        P.op(eng, lambda e: e.tensor_scalar(out=out, in0=in0, scalar1=s1, scalar2=None, op0=op0), r, w)
    else:
        P.op(eng, lambda e: e.tensor_scalar(out=out, in0=in0, scalar1=s1, scalar2=s2, op0=op0, op1=op1), r, w)


def STT(P, out, in0, scalar, in1, op0, op1, r, w):
    P.op('dve', lambda e: e.scalar_tensor_tensor(out=out, in0=in0, scalar=scalar, in1=in1, op0=op0, op1=op1), r, w)


def CP(P, eng, out, in_, r, w):
    if eng == 'act':
        P.op('act', lambda e: e.copy(out=out, in_=in_), r, w)
    else:
        P.op(eng, lambda e: e.tensor_copy(out=out, in_=in_), r, w)


def RED(P, out, in_, op, r, w):
    P.op('dve', lambda e: e.tensor_reduce(out=out, in_=in_, axis=AX.X, op=op), r, w)


def TTR(P, out, in0, in1, accum, r, w):
    P.op('act', lambda e: e.activation(out=out, in_=in0, func=AF.Square, accum_out=accum), r, w)


def MEMSET(P, eng, ap, val, w):
    P.op(eng, lambda e: e.memset(ap, val), (), w)


def RSTD(P, out, ss, n, r, w):
    TS(P, 'dve', out, ss, 1.0 / n, EPS, ALU.mult, ALU.add, r, w)
    ACT(P, out, out, AF.Sqrt, w, w)
    P.op('dve', lambda e: e.reciprocal(out=out, in_=out), w, w)


def r3(ap, a, b):
    return ap.rearrange('p (a b) -> p a b', a=a, b=b)


def r4(ap, a, b, c):
    return ap.rearrange('p (a b c) -> p a b c', a=a, b=b, c=c)


def build_program(debug=False, n_layers=L, stop_after=None):
    nc = bass.Bass('TRN2', target_bir_lowering=False)

    def din(name, shape, dt=F32):
        return nc.dram_tensor(name, list(shape), dt, kind='ExternalInput').ap()

    skind = 'ExternalOutput' if debug else 'Internal'

    def dsc(name, shape, dt):
        return nc.dram_tensor(name, list(shape), dt, kind=skind).ap()

    I = {}
    I['x'] = din('x', [T, D])
    I['ctx'] = din('ctx', [CT, D])
    I['cvec'] = din('cvec', [128, 16])
    I['w_mod'] = din('w_mod', [L, D, 6 * D])
    I['bmodT'] = din('bmodT', [L, 128, 48])
    I['bmod2'] = din('bmod2', [L, 2, 6 * D])
    I['g1T'] = din('g1T', [L, 128, 8])
    I['g2T'] = din('g2T', [L, 128, 8])
    I['w_in'] = din('w_in', [L, D, 4256])
    I['g_ckv'] = din('g_ckv', [L, 128, 1])
    I['w_ukv'] = din('w_ukv', [L, 128, 1024])
    I['g_cqT'] = din('g_cqT', [L, 128, 2])
    I['w_uq'] = din('w_uq', [L, 256, 768])
    I['g_qn'] = din('g_qn', [L, 1, 96])
    I['g_kn'] = din('g_kn', [L, 1, 96])
    I['wdwT'] = din('wdwT', [L, 128, 2 * 31])
    I['cvp'] = din('cvp', [L, 128, 6])
    I['wb_attn'] = din('wb_attn', [L, 512, D])
    I['wb_fnet'] = din('wb_fnet', [L, 256, D])
    I['wb_conv'] = din('wb_conv', [L, 256, D])
    I['w_out'] = din('w_out', [L, D, D])
    I['w_query'] = din('w_query', [L, D, D])
    I['skbd'] = din('skbd', [L, 128, 8 * 256])
    I['u_tab'] = din('u_tab', [L, 16384, D])
    I['v_tab'] = din('v_tab', [L, 16384, D])
    I['ropeC'] = din('ropeC', [T, 32])
    I['ropeS'] = din('ropeS', [T, 32])
    I['dftP'] = din('dftP', [2, T, T], BF16)
    I['dftC'] = din('dftC', [2, CT, CT], BF16)
    I['csbd'] = din('csbd', [128, 256])
    I['ident'] = din('ident', [128, 128])
    I['sel'] = din('sel', [2, 256])
    out_d = nc.dram_tensor('out', [T, D], F32, kind='ExternalOutput').ap()

    S = {}
    S['xres'] = dsc('xres', [NT, D], F32)
    S['QT'] = dsc('QT', [8, 96, NT], BF16)
    S['ZCS'] = dsc('ZCS', [NT, 512], BF16)
    S['UT'] = dsc('UT', [2, 128, NT], F32)
    S['aT'] = dsc('aT', [4, 128, NT], BF16)
    S['fT'] = dsc('fT', [2, 128, NT], BF16)
    S['cvT'] = dsc('cvT', [2, 128, NT], BF16)
    S['uTs'] = nc.dram_tensor('uTs', [NCH, 128, 1024], BF16, kind='Internal').ap()
    S['vs'] = nc.dram_tensor('vs', [NCH, 128, 1024], BF16, kind='Internal').ap()

    P = Prog(nc)
    PB = P.pbuf

    identF = P.alloc(128)
    b_idF = Buf()
    P.dma(identF, I['ident'], b_idF, True)
    identB = P.alloc(128, BF16)
    b_idB = Buf()
    CP(P, 'dve', identB, identF, [b_idF], [b_idB])
    sel = P.alloc(256)
    b_sel = Buf()
    P.dma(sel[0:2, :], I['sel'], b_sel, True)
    cvec = P.alloc(16)
    b_cvec = Buf()
    P.dma(cvec, I['cvec'], b_cvec, True)
    scv = P.alloc(16)
    b_scv = Buf()
    ACT(P, scv, cvec, AF.Silu, [b_cvec], [b_scv])
    ones256 = P.alloc(128)
    b_ones = Buf()
    MEMSET(P, 'dve', ones256, 1.0 / 256, [b_ones])
    iota16 = P.alloc(16)
    iota128 = P.alloc(128)
    b_iota = Buf()
    P.op('pool', lambda e: e.iota(iota16, pattern=[[1, 16]], base=0, channel_multiplier=0,
                                  allow_small_or_imprecise_dtypes=True), (), [b_iota])
    P.op('pool', lambda e: e.iota(iota128, pattern=[[1, 128]], base=0, channel_multiplier=0,
                                  allow_small_or_imprecise_dtypes=True), (), [b_iota])
    iota128h = P.alloc(128, BF16)
    CP(P, 'dve', iota128h, iota128, [b_iota], [b_iota])
    csbdF = P.alloc(256)
    b_csF = Buf()
    P.dma(csbdF, I['csbd'], b_csF, True)
    csbd = P.alloc(256, BF16)
    b_cs = Buf()
    CP(P, 'dve', csbd, csbdF, [b_csF], [b_cs])

    modF = P.alloc(4 * 8 * 2)
    b_modF = Buf()
    gtb = [P.alloc(1024) for _ in range(4)]
    b_gtb = [Buf() for _ in range(4)]

    def xsrc(l, r0, n):
        if l == 0:
            if r0 < CT:
                return I['ctx'][r0:r0 + n, :]
            return I['x'][r0 - CT:r0 - CT + n, :]
        return S['xres'][r0:r0 + n, :]

    def phase_adaln(l):
        P.mark()
        gT = [P.alloc(8), P.alloc(8)]
        b_g = Buf()
        P.dma(gT[0], I['g1T'][l], b_g, True)
        b_g2 = Buf()
        P.dma(gT[1], I['g2T'][l], b_g2, True)
        bmT = P.alloc(48)
        b_bm = Buf()
        P.dma(bmT, I['bmodT'][l], b_bm, True)
        bm2 = P.alloc(6 * D)
        b_bm2 = Buf()
        P.dma(bm2[0:2, :], I['bmod2'][l], b_bm2, True)
        rows = P.alloc(2048)
        b_rows = Buf()
        wslot = [P.alloc(8 * 512) for _ in range(2)]
        b_ws = [Buf(), Buf()]
        wsrc = I['w_mod'][l].rearrange('(k p) n -> p k n', p=128)
        modF4 = r4(modF, 4, 8, 2)
        kindmap = {0: 0, 1: 1, 3: 2, 4: 3}
        for nb in range(12):
            ws, bw = wslot[nb % 2], b_ws[nb % 2]
            ws3 = r3(ws, 8, 512)
            P.dma(ws3, wsrc[:, :, nb * 512:(nb + 1) * 512], bw, True)
            kind = nb // 2
            if kind in (2, 5):
                ps = P.bank(0)
                for k in range(8):
                    MM(P, ps[0:2, :], r3(scv, 8, 2)[:, k, :], ws3[:, k, :], k == 0, k == 7, [bw, b_scv], [PB[0]])
                gi = 0 if kind == 2 else 1
                c0 = gi * 1024 + (nb % 2) * 512
                TT(P, 'dve', rows[0:2, c0:c0 + 512], ps[0:2, :], bm2[0:2, nb * 512:(nb + 1) * 512], ALU.add,
                   [PB[0], b_bm2], [b_rows])
            else:
                ps = P.bank(1)
                for cc in range(4):
                    for k in range(8):
                        MM(P, ps[:, cc * 2:cc * 2 + 2], ws3[:, k, cc * 128:(cc + 1) * 128], r3(scv, 8, 2)[:, k, :],
                           k == 0, k == 7, [bw, b_scv], [PB[1]])
                for cc in range(4):
                    gc = nb * 4 + cc
                    kk = gc % 8
                    dst = modF4[:, kindmap[kind], kk, :]
                    if kind in (0, 3):
                        TS(P, 'dve', dst, ps[:, cc * 2:cc * 2 + 2], bmT[:, gc:gc + 1], None, ALU.add, None,
                           [PB[1], b_bm], [b_modF])
                    else:
                        g = gT[0] if kind == 1 else gT[1]
                        TS(P, 'dve', dst, ps[:, cc * 2:cc * 2 + 2], bmT[:, gc:gc + 1], 1.0, ALU.add, ALU.add,
                           [PB[1], b_bm], [b_modF])
                        TS(P, 'dve', dst, dst, g[:, kk:kk + 1], None, ALU.mult, None, [b_modF, b_g, b_g2], [b_modF])
        for gi in range(2):
            for mi in range(2):
                for hf in range(2):
                    ps = P.bank(2 + hf)
                    MM(P, ps, sel[0:2, mi * 128:(mi + 1) * 128], rows[0:2, gi * 1024 + hf * 512: gi * 1024 + hf * 512 + 512],
                       True, True, [b_sel, b_rows], [PB[2 + hf]])
                    CP(P, 'act', gtb[gi * 2 + mi][:, hf * 512:(hf + 1) * 512], ps, [PB[2 + hf]], [b_gtb[gi * 2 + mi]])
        P.barrier()
        P.release()

    class HT:
        def __init__(self, nx=4, nh=2):
            self.nx, self.nh = nx, nh
            self.x = [P.alloc(1024) for _ in range(nx)]
            self.bx = [Buf() for _ in range(nx)]
            self.xn = [P.alloc(1024, BF16) for _ in range(2)]
            self.bxn = [Buf(), Buf()]
            self.junk = P.alloc(1024)
            self.bj = Buf()
            self.ss = P.alloc(2)
            self.bss = Buf()
            self.hT = [P.alloc(8 * TB, BF16) for _ in range(nh)]
            self.bh = [Buf() for _ in range(nh)]
            self.n = 0

        def begin(self, l, tb, from_res=False):
            xs, bxs = [], []
            for tt in range(2):
                xi = (self.n * 2 + tt) % self.nx
                xt, bx = self.x[xi], self.bx[xi]
                r0_ = tb * TB + tt * 128
                P.dma(xt, S['xres'][r0_:r0_ + 128, :] if from_res else xsrc(l, r0_, 128), bx, True)
                TTR(P, self.junk, xt, xt, self.ss[:, tt:tt + 1], [bx], [self.bj, self.bss])
                xs.append(xt)
                bxs.append(bx)
            RSTD(P, self.ss, self.ss, D, [self.bss], [self.bss])
            for tt in range(2):
                ACT(P, self.xn[tt], xs[tt], AF.Copy, [bxs[tt], self.bss], [self.bxn[tt]], scale=self.ss[:, tt:tt + 1])
            self.cur = (tb, xs, bxs)

        def finish(self, kind, tbank=0):
            tb, xs, bxs = self.cur
            mi = 1 if tb == 0 else 0
            modF4 = r4(modF, 4, 8, 2)
            slot = self.n % self.nh
            hT, bh = self.hT[slot], self.bh[slot]
            hT3 = r3(hT, 8, TB)
            for tt in range(2):
                xn, bxn = self.xn[tt], self.bxn[tt]
                pt = r3(P.bank(tbank).bitcast(BF16), 8, 128)
                for k in range(8):
                    TR(P, pt[:, k, :], xn[:, k * 128:(k + 1) * 128], identB, [bxn, b_idB], [PB[tbank]])
                for k in range(8):
                    if k % 2 == 1:
                        ACT(P, hT3[:, k, tt * 128:(tt + 1) * 128], pt[:, k, :], AF.Identity, [PB[tbank], b_modF], [bh],
                            scale=modF4[:, kind + 1, k, mi:mi + 1], bias=modF4[:, kind, k, mi:mi + 1])
                    else:
                        TS(P, 'dve', hT3[:, k, tt * 128:(tt + 1) * 128], pt[:, k, :], modF4[:, kind + 1, k, mi:mi + 1],
                           modF4[:, kind, k, mi:mi + 1], ALU.mult, ALU.add, [PB[tbank], b_modF], [bh])
            self.n += 1
            return hT3, bh, xs, bxs

        def make(self, l, tb, kind, tbank=0, from_res=False):
            self.begin(l, tb, from_res)
            return self.finish(kind, tbank)

    def load_w_bf16(dst3, bdst, src3, ncols, stg, bstg, k_n):
        for k in range(k_n):
            s, bs = stg[k % len(stg)], bstg[k % len(stg)]
            P.dma(s[:, 0:ncols], src3[:, k, :], bs, True)
            if k % 2 == 0:
                CP(P, 'act', dst3[:, k, :], s[:, 0:ncols], [bs], [bdst])
            else:
                CP(P, 'pool', dst3[:, k, :], s[:, 0:ncols], [bs], [bdst])

    AT = {}

    def alloc_attn():
        P.mark()
        KT = P.alloc(8 * NT, BF16)
        Vs = P.alloc(34 * 8 * 65, BF16)
        AT['b_KT'] = Buf()
        AT['b_V'] = Buf()
        AT['KT3'] = r3(KT, 8, NT)
        AT['V4'] = r4(Vs, 34, 8, 65)
        MEMSET(P, 'pool', Vs, 1.0, [AT['b_V']])

    def phase_A(l):
        alloc_attn()
        KT3, V4, b_KT, b_V = AT['KT3'], AT['V4'], AT['b_KT'], AT['b_V']
        P.mark()
        win = I['w_in'][l].rearrange('(k p) n -> p k n', p=128)
        w_tm = P.alloc(8 * 416, BF16)
        w_fm = P.alloc(8 * 768, BF16)
        b_wtm, b_wfm = Buf(), Buf()
        stg = [P.alloc(1184)] * 2
        bstg = [Buf()] * 2
        w_tm3, w_fm3 = r3(w_tm, 8, 416), r3(w_fm, 8, 768)
        for k in range(8):
            s, bs = stg[k % 2], bstg[k % 2]
            P.dma(s, win[:, k, 0:1184], bs, True)
            CP(P, 'act', w_tm3[:, k, :], s[:, 0:416], [bs], [b_wtm])
            CP(P, 'pool', w_fm3[:, k, :], s[:, 416:1184], [bs], [b_wfm])
        gck = P.alloc(1)
        gcq = P.alloc(2)
        b_gc = Buf()
        P.dma(gck, I['g_ckv'][l], b_gc, True)
        b_gq2 = Buf()
        P.dma(gcq, I['g_cqT'][l], b_gq2, True)
        wukv = P.alloc(1024, BF16)
        b_wukv = Buf()
        P.dma(stg[0][:, 0:1024], I['w_ukv'][l], bstg[0], True)
        TS(P, 'dve', wukv, stg[0][:, 0:1024], gck[:, 0:1], None, ALU.mult, None, [bstg[0], b_gc], [b_wukv])
        wuq = P.alloc(2 * 768, BF16)
        b_wuq = Buf()
        wuq3 = r3(wuq, 2, 768)
        for kk in range(2):
            s, bs = stg[1 - kk], bstg[1 - kk]
            P.dma(s[:, 0:768], I['w_uq'][l][kk * 128:(kk + 1) * 128, :], bs, True)
            TS(P, 'dve', wuq3[:, kk, :], s[:, 0:768], gcq[:, kk:kk + 1], None, ALU.mult, None, [bs, b_gq2], [b_wuq])
        gqb = P.alloc(96)
        gkb = P.alloc(96)
        b_gqk = Buf()
        P.dma(gqb, I['g_qn'][l].partition_broadcast(128), b_gqk, True)
        b_gqk2 = Buf()
        P.dma(gkb, I['g_kn'][l].partition_broadcast(128), b_gqk2, True)
        gvec = [b_gqk, b_gqk2]

        ht = HT(nx=2, nh=1)
        tmz = P.alloc(416)
        b_tmz = Buf()
        zfT = P.alloc(2 * TB, BF16)
        b_zfT = Buf()
        zfT3 = r3(zfT, 2, TB)
        sg = P.alloc(TB)
        b_sg = Buf()
        ut = [P.alloc(2 * TB) for _ in range(2)]
        b_ut = [Buf(), Buf()]
        zcs = [P.alloc(512, BF16) for _ in range(2)]
        b_zcs = [Buf(), Buf()]
        st = P.alloc(24)
        b_st = Buf()
        cn = P.alloc(384, BF16)
        b_cn = Buf()
        cT = P.alloc(3 * 128, BF16)
        b_cT = Buf()
        cT3 = r3(cT, 3, 128)
        kvs = P.alloc(1024)
        b_kvs = Buf()
        qs = P.alloc(768)
        b_qs = Buf()
        sq = P.alloc(768)
        b_sq = Buf()
        ktm = P.alloc(768, BF16)
        b_ktm = Buf()
        qtm = P.alloc(768, BF16)
        b_qtm = Buf()
        krr = P.alloc(32)
        b_krr = Buf()
        rtmp = P.alloc(256)
        b_rtmp = Buf()
        rc = P.alloc(32)
        rs = P.alloc(32)
        b_rope = Buf()
        qTt = [P.alloc(8 * 128, BF16) for _ in range(2)]
        b_qTt = [Buf(), Buf()]

        for tb in range(NBLK):
            hT3, bh, xs, bxs = ht.make(l, tb, 0, tbank=0)
            is_ctx = (tb == 0)
            u_t, b_u = ut[tb % 2], b_ut[tb % 2]
            u3 = r3(u_t, 2, TB)
            for cc in range(2):
                bk = 2 + (cc % 2)
                ps = P.bank(bk)[:, 0:TB]
                for k in range(8):
                    MM(P, ps, w_fm3[:, k, cc * 128:(cc + 1) * 128], hT3[:, k, :], k == 0, k == 7, [b_wfm, bh], [PB[bk]])
                CP(P, 'act', zfT3[:, cc, :], ps, [PB[bk]], [b_zfT])
            for c2 in range(2):
                pa = P.bank(2)[:, 0:TB]
                pg = P.bank(3)[:, 0:TB]
                for k in range(8):
                    MM(P, pa, w_fm3[:, k, (2 + c2) * 128:(3 + c2) * 128], hT3[:, k, :], k == 0, k == 7, [b_wfm, bh], [PB[2]])
                for k in range(8):
                    MM(P, pg, w_fm3[:, k, (4 + c2) * 128:(5 + c2) * 128], hT3[:, k, :], k == 0, k == 7, [b_wfm, bh], [PB[3]])
                ACT(P, sg, pg, AF.Sigmoid, [PB[3]], [b_sg])
                TT(P, 'dve', u3[:, c2, :], pa, sg, ALU.mult, [PB[2], b_sg], [b_u])
            P.dma(S['UT'][:, :, tb * TB:(tb + 1) * TB].rearrange('c p t -> p c t'), u3, b_u, False)
            for tt in range(2):
                t0 = tb * TB + tt * 128
                ti = t0 // 128
                tsl = slice(tt * 128, (tt + 1) * 128)
                z, bz = zcs[tt], b_zcs[tt]
                pz = P.bank(4)
                for kc in range(2):
                    MM(P, pz[:, kc * 256:(kc + 1) * 256], zfT3[:, kc, tsl], csbd, True, True, [b_zfT, b_cs], [PB[4]])
                CP(P, 'act', z, pz, [PB[4]], [bz])
                P.dma(S['ZCS'][t0:t0 + 128, :], z, bz, False)
                pp = P.bank(1)[:, 0:416]
                for k in range(8):
                    MM(P, pp, hT3[:, k, tsl], w_tm3[:, k, :], k == 0, k == 7, [bh, b_wtm], [PB[1]])
                CP(P, 'act', tmz, pp, [PB[1]], [b_tmz])
                TTR(P, sq[:, 0:128], tmz[:, 0:128], tmz[:, 0:128], st[:, 0:1], [b_tmz], [b_sq, b_st])
                TTR(P, sq[:, 0:256], tmz[:, 160:416], tmz[:, 160:416], st[:, 1:2], [b_tmz], [b_sq, b_st])
                TS(P, 'dve', st[:, 1:2], st[:, 1:2], 0.5, None, ALU.mult, None, [b_st], [b_st])
                RSTD(P, st[:, 0:2], st[:, 0:2], 128, [b_st], [b_st])
                ACT(P, cn[:, 0:128], tmz[:, 0:128], AF.Copy, [b_tmz, b_st], [b_cn], scale=st[:, 0:1])
                ACT(P, cn[:, 128:384], tmz[:, 160:416], AF.Copy, [b_tmz, b_st], [b_cn], scale=st[:, 1:2])
                pt = r3(P.bank(0).bitcast(BF16), 8, 128)
                for j in range(3):
                    TR(P, pt[:, j, :], cn[:, j * 128:(j + 1) * 128], identB, [b_cn, b_idB], [PB[0]])
                CP(P, 'dve', cT3, pt[:, 0:3, :], [PB[0]], [b_cT])
                for hf in range(2):
                    MM(P, P.bank(5 + hf), cT3[:, 0, :], wukv[:, hf * 512:(hf + 1) * 512], True, True, [b_cT, b_wukv], [PB[5 + hf]])
                for hf in range(2):
                    CP(P, 'act', kvs[:, hf * 512:(hf + 1) * 512], P.bank(5 + hf), [PB[5 + hf]], [b_kvs])
                pq = P.bank(6, 2)
                for nh in range(2):
                    for kk in range(2):
                        MM(P, pq[:, nh * 512:nh * 512 + 384], cT3[:, 1 + kk, :], wuq3[:, kk, nh * 384:(nh + 1) * 384],
                           kk == 0, kk == 1, [b_cT, b_wuq], [PB[6], PB[7]])
                kv4 = r3(kvs, 8, 128)
                CP(P, 'pool', V4[:, ti, :, 0:64], kv4[:, :, 64:128], [b_kvs], [b_V])
                sq3 = r3(sq[:, 0:512], 8, 64)
                TT(P, 'dve', sq3, kv4[:, :, 0:64], kv4[:, :, 0:64], ALU.mult, [b_kvs], [b_sq])
                RED(P, st[:, 8:16], sq3, ALU.add, [b_sq], [b_st])
                TTR(P, sq[:, 512:544], tmz[:, 128:160], tmz[:, 128:160], st[:, 2:3], [b_tmz], [b_sq, b_st])
                TS(P, 'dve', st[:, 8:16], st[:, 8:16], st[:, 2:3], None, ALU.add, None, [b_st], [b_st])
                RSTD(P, st[:, 8:16], st[:, 8:16], 96, [b_st], [b_st])
                ktm3 = r3(ktm, 8, 96)
                TT(P, 'dve', sq3, kv4[:, :, 0:64], st[:, 8:16].unsqueeze(2).to_broadcast([128, 8, 64]), ALU.mult,
                   [b_kvs, b_st], [b_sq])
                TT(P, 'dve', ktm3[:, :, 0:64], sq3, gkb[:, 0:64].unsqueeze(1).to_broadcast([128, 8, 64]), ALU.mult,
                   [b_sq] + gvec, [b_ktm])
                TT(P, 'dve', krr, tmz[:, 128:160], gkb[:, 64:96], ALU.mult, [b_tmz] + gvec, [b_krr])
                if not is_ctx:
                    P.dma(rc, I['ropeC'][t0 - CT:t0 - CT + 128, :], b_rope, True)
                    P.dma(rs, I['ropeS'][t0 - CT:t0 - CT + 128, :], b_rope, True)
                    kr4 = r3(krr, 4, 8)
                    rt4 = r3(rtmp[:, 0:32], 4, 8)
                    for hb in range(2):
                        for blk in range(2):
                            TT(P, 'dve', rt4[:, hb * 2 + blk, :], kr4[:, hb * 2 + (1 - blk), :],
                               r3(rs, 4, 8)[:, hb * 2 + blk, :], ALU.mult, [b_krr, b_rope], [b_rtmp])
                    TT(P, 'dve', krr, krr, rc, ALU.mult, [b_krr, b_rope], [b_krr])
                    TT(P, 'dve', krr, krr, rtmp[:, 0:32], ALU.add, [b_krr, b_rtmp], [b_krr])
                TT(P, 'dve', ktm3[:, :, 64:96], krr.unsqueeze(1).to_broadcast([128, 8, 32]),
                   st[:, 8:16].unsqueeze(2).to_broadcast([128, 8, 32]), ALU.mult, [b_krr, b_st], [b_ktm])
                pk = r3(P.bank(0).bitcast(BF16), 8, 128)
                for h in range(8):
                    TR(P, pk[0:96, h, :], ktm3[:, h, :], identB, [b_ktm, b_idB], [PB[0]])
                CP(P, 'act', KT3[0:96, :, t0:t0 + 128], pk[0:96, :, :], [PB[0]], [b_KT])
                for nh in range(2):
                    CP(P, 'act', qs[:, nh * 384:(nh + 1) * 384], pq[:, nh * 512:nh * 512 + 384], [PB[6], PB[7]], [b_qs])
                q3 = r3(qs, 8, 96)
                s3 = r3(sq, 8, 96)
                TT(P, 'dve', s3, q3, q3, ALU.mult, [b_qs], [b_sq])
                RED(P, st[:, 16:24], s3, ALU.add, [b_sq], [b_st])
                RSTD(P, st[:, 16:24], st[:, 16:24], 96, [b_st], [b_st])
                TT(P, 'dve', s3, q3, st[:, 16:24].unsqueeze(2).to_broadcast([128, 8, 96]), ALU.mult, [b_qs, b_st], [b_sq])
                TT(P, 'dve', q3, s3, gqb.unsqueeze(1).to_broadcast([128, 8, 96]), ALU.mult, [b_sq] + gvec, [b_qs])
                qtm3 = r3(qtm, 8, 96)
                CP(P, 'pool', qtm3[:, :, 0:64], q3[:, :, 0:64], [b_qs], [b_qtm])
                if not is_ctx:
                    q5 = qs.rearrange('p (h c) -> p h c', h=8, c=96)[:, :, 64:96].rearrange('p h (a b) -> p h a b', a=4, b=8)
                    rt5 = rtmp.rearrange('p (h a b) -> p h a b', h=8, a=4, b=8)
                    rs4 = r3(rs, 4, 8)
                    for hb in range(2):
                        for blk in range(2):
                            TT(P, 'dve', rt5[:, :, hb * 2 + blk, :], q5[:, :, hb * 2 + (1 - blk), :],
                               rs4[:, hb * 2 + blk, :].unsqueeze(1).to_broadcast([128, 8, 8]), ALU.mult,
                               [b_qs, b_rope], [b_rtmp])
                    qr = q3[:, :, 64:96]
                    TT(P, 'dve', s3[:, :, 0:32], qr, rc.unsqueeze(1).to_broadcast([128, 8, 32]), ALU.mult,
                       [b_qs, b_rope], [b_sq])
                    TT(P, 'dve', qtm3[:, :, 64:96], s3[:, :, 0:32], r3(rtmp, 8, 32), ALU.add, [b_sq, b_rtmp], [b_qtm])
                else:
                    CP(P, 'pool', qtm3[:, :, 64:96], q3[:, :, 64:96], [b_qs], [b_qtm])
                pqT = r3(P.bank(4).bitcast(BF16), 8, 128)
                for h in range(8):
                    TR(P, pqT[0:96, h, :], qtm3[:, h, :], identB, [b_qtm, b_idB], [PB[4]])
                qq, bqq = qTt[tt], b_qTt[tt]
                CP(P, 'act', r3(qq, 8, 128)[0:96, :, :], pqT[0:96, :, :], [PB[4]], [bqq])
                P.dma(S['QT'][:, :, t0:t0 + 128].rearrange('h p t -> p h t'), r3(qq, 8, 128)[0:96, :, :], bqq, False)
        if debug:
            kd = nc.dram_tensor('KTd%d' % l, [128, 8 * NT], BF16, kind='ExternalOutput').ap()
            vd = nc.dram_tensor('Vd%d' % l, [128, 34 * 8 * 65], BF16, kind='ExternalOutput').ap()
            P.dma(kd, KT3.rearrange('p a b -> p (a b)'), b_KT, False)
            P.dma(vd, V4.rearrange('p a b c -> p (a b c)'), b_V, False)
        P.barrier()
        P.release()

    def phase_B(l):
        KT3, V4, b_KT, b_V = AT['KT3'], AT['V4'], AT['b_KT'], AT['b_V']
        P.mark()
        qt = [P.alloc(8 * 512, BF16) for _ in range(2)]
        b_qt = [Buf(), Buf()]
        E = [P.alloc(512, BF16) for _ in range(3)]
        b_E = [Buf() for _ in range(3)]
        atm = P.alloc(4 * 512)
        b_atm = Buf()
        atb = P.alloc(4 * 512, BF16)
        b_atb = Buf()
        rcp = P.alloc(4)
        b_rcp = Buf()
        aTt = [P.alloc(4 * 512, BF16) for _ in range(2)]
        b_aTt = [Buf(), Buf()]
        scale = 96.0 ** -0.5
        blocks = [(CT + i * 512, 512, list(range(34))) for i in range(8)]
        if l < L - 1:
            blocks.append((0, 256, [0, 1]))
        ne = 0
        for bi, (q0, nq, kts) in enumerate(blocks):
            nqi = nq // 128
            q_t, bq = qt[bi % 2], b_qt[bi % 2]
            q3 = r3(q_t, 8, 512)
            P.dma(q3[0:96, :, 0:nq], S['QT'][:, :, q0:q0 + nq].rearrange('h p t -> p h t'), bq, True)
            atm3 = r3(atm, 4, 512)
            steps = [(h, ki, kt) for h in range(8) for ki, kt in enumerate(kts)]

            def qk(i):
                h, ki, kt = steps[i]
                sb = (ne + i) % 3
                MM(P, P.bank(sb)[:, 0:nq], KT3[0:96, h, kt * 128:(kt + 1) * 128], q3[0:96, h, 0:nq], True, True,
                   [b_KT, bq], [PB[sb]])

            qk(0)
            for i, (h, ki, kt) in enumerate(steps):
                if i + 1 < len(steps):
                    qk(i + 1)
                sb = (ne + i) % 3
                accb = 4 + (h % 2)
                acc = r3(P.bank(accb)[:, 0:4 * 65], 4, 65)
                e_t, be = E[sb], b_E[sb]
                ACT(P, e_t[:, 0:nq], P.bank(sb)[:, 0:nq], AF.Exp, [PB[sb]], [be], scale=scale)
                for qi in range(nqi):
                    MM(P, acc[:, qi, :], e_t[:, qi * 128:(qi + 1) * 128], V4[:, kt, h, :], ki == 0 and qi == 0,
                       ki == len(kts) - 1, [be, b_V], [PB[accb]], skip=True)
                if ki == len(kts) - 1:
                    P.op('dve', lambda e, acc=acc, nqi=nqi: e.reciprocal(out=rcp[:, 0:nqi], in_=acc[:, 0:nqi, 64]),
                         [PB[accb]], [b_rcp])
                    TT(P, 'dve', atm3[:, 0:nqi, h * 64:(h + 1) * 64], acc[:, 0:nqi, 0:64],
                       rcp[:, 0:nqi].unsqueeze(2).to_broadcast([128, nqi, 64]), ALU.mult, [PB[accb], b_rcp], [b_atm])
            ne += len(steps)
            CP(P, 'pool', atb, atm, [b_atm], [b_atb])
            atb3 = r3(atb, 4, 512)
            a_t, ba = aTt[bi % 2], b_aTt[bi % 2]
            a3 = r3(a_t, 4, 512)
            for qi in range(nqi):
                pt = r3(P.bank(6 + (qi % 2)).bitcast(BF16)[:, 0:512], 4, 128)
                for c in range(4):
                    TR(P, pt[:, c, :], atb3[:, qi, c * 128:(c + 1) * 128], identB, [b_atb, b_idB], [PB[6 + (qi % 2)]])
                CP(P, 'act', a3[:, :, qi * 128:(qi + 1) * 128], pt, [PB[6 + (qi % 2)]], [ba])
            P.dma(S['aT'][:, :, q0:q0 + nq].rearrange('c p t -> p c t'), a3[:, :, 0:nq], ba, False)
        P.barrier()
        P.release()
        P.release()

    def phase_C1(l):
        P.mark()
        Z = P.alloc(32 * 512, BF16)
        bZ = Buf()
        Z3 = r3(Z, 32, 512)
        P.dma(Z3, S['ZCS'][CT:NT, :].rearrange('(tt p) n -> p tt n', p=128), bZ, True)
        slab = [P.alloc(8 * 512, BF16) for _ in range(4)]
        b_slab = [Buf() for _ in range(4)]
        fo = [P.alloc(2 * 512, BF16) for _ in range(2)]
        b_fo = [Buf(), Buf()]
        ns = 0
        for kb in range(8):
            for cs in range(2):
                src = I['dftP'][cs].rearrange('(tt p) k -> p tt k', p=128)
                for tg in range(4):
                    sl, bs = slab[ns % 4], b_slab[ns % 4]
                    sl3 = r3(sl, 8, 512)
                    P.dma(sl3, src[:, tg * 8:(tg + 1) * 8, kb * 512:(kb + 1) * 512], bs, True)
                    for t8 in range(8):
                        tt = tg * 8 + t8
                        first = (cs == 0 and tt == 0)
                        last = (cs == 1 and tt == 31)
                        for mc in range(2):
                            MM(P, P.bank(mc), Z3[:, tt, mc * 256 + cs * 128: mc * 256 + cs * 128 + 128], sl3[:, t8, :],
                               first, last, [bZ, bs], [PB[mc]])
                    ns += 1
            f_t, bf = fo[kb % 2], b_fo[kb % 2]
            f3 = r3(f_t, 2, 512)
            CP(P, 'act', f3[:, 0, :], P.bank(0), [PB[0]], [bf])
            CP(P, 'dve', f3[:, 1, :], P.bank(1), [PB[1]], [bf])
            P.dma(S['fT'][:, :, CT + kb * 512:CT + (kb + 1) * 512].rearrange('c p t -> p c t'), f3, bf, False)
        if l < L - 1:
            Zc = P.alloc(2 * 512, BF16)
            bZc = Buf()
            Zc3 = r3(Zc, 2, 512)
            P.dma(Zc3, S['ZCS'][0:CT, :].rearrange('(tt p) n -> p tt n', p=128), bZc, True)
            dc = P.alloc(2 * 2 * 256, BF16)
            bdc = Buf()
            dc4 = r4(dc, 2, 2, 256)
            for cs in range(2):
                P.dma(dc4[:, cs, :, :], I['dftC'][cs].rearrange('(tt p) k -> p tt k', p=128), bdc, True)
            for mc in range(2):
                n = 0
                for cs in range(2):
                    for tt in range(2):
                        MM(P, P.bank(2 + mc)[:, 0:256], Zc3[:, tt, mc * 256 + cs * 128: mc * 256 + cs * 128 + 128],
                           dc4[:, cs, tt, :], n == 0, n == 3, [bZc, bdc], [PB[2 + mc]])
                        n += 1
            f_t, bf = fo[0], b_fo[0]
            f3 = r3(f_t, 2, 512)
            CP(P, 'act', f3[:, 0, 0:256], P.bank(2)[:, 0:256], [PB[2]], [bf])
            CP(P, 'dve', f3[:, 1, 0:256], P.bank(3)[:, 0:256], [PB[3]], [bf])
            P.dma(S['fT'][:, :, 0:CT].rearrange('c p t -> p c t'), f3[:, :, 0:256], bf, False)
        P.barrier()
        P.release()

    def phase_C2(l):
        P.mark()
        LB = 15 + CT + 15 + T + 15
        U = P.alloc(2 * LB)
        bU = Buf()
        U3 = r3(U, 2, LB)
        MEMSET(P, 'pool', U, 0.0, [bU])
        OC, OX = 15, 15 + CT + 15
        for c in range(2):
            P.dma(U3[:, c, OC:OC + CT], S['UT'][c, :, 0:CT], bU, True)
            P.dma(U3[:, c, OX:OX + T], S['UT'][c, :, CT:NT], bU, True)
        wd = P.alloc(62)
        bwd = Buf()
        P.dma(wd, I['wdwT'][l], bwd, True)
        wd3 = r3(wd, 2, 31)
        cp = P.alloc(6)
        bcp = Buf()
        P.dma(cp, I['cvp'][l], bcp, True)
        cp3 = r3(cp, 2, 3)
        A = P.alloc(2 * LB)
        bA = Buf()
        A3 = r3(A, 2, LB)
        NV = LB - 30
        for c in range(2):
            TS(P, 'dve', A3[:, c, 15:15 + NV], U3[:, c, 0:NV], wd3[:, c, 0:1], cp3[:, c, 0:1], ALU.mult, ALU.add,
               [bU, bwd, bcp], [bA])
            for w in range(1, 31):
                STT(P, A3[:, c, 15:15 + NV], U3[:, c, w:w + NV], wd3[:, c, w:w + 1], A3[:, c, 15:15 + NV], ALU.mult, ALU.add,
                    [bU, bwd, bA], [bA])
        sqt = P.alloc(2 * 512)
        bsq = Buf()
        sq3 = r3(sqt, 2, 512)
        mean = P.alloc(512)
        bmean = Buf()
        var = P.alloc(512)
        bvar = Buf()
        y = P.alloc(512)
        by = Buf()
        co = [P.alloc(2 * 512, BF16) for _ in range(2)]
        bco = [Buf(), Buf()]
        blocks = [(OX + i * 512, CT + i * 512, 512) for i in range(8)]
        if l < L - 1:
            blocks.append((OC, 0, 256))
        for bi, (o, t0, n) in enumerate(blocks):
            for c in range(2):
                ACT(P, sq3[:, c, 0:n], A3[:, c, o:o + n], AF.Square, [bA], [bsq])
            for c in range(2):
                MM(P, P.bank(0)[:, 0:n], ones256, A3[:, c, o:o + n], c == 0, c == 1, [b_ones, bA], [PB[0]])
            for c in range(2):
                MM(P, P.bank(1)[:, 0:n], ones256, sq3[:, c, 0:n], c == 0, c == 1, [b_ones, bsq], [PB[1]])
            CP(P, 'act', mean[:, 0:n], P.bank(0)[:, 0:n], [PB[0]], [bmean])
            TT(P, 'dve', var[:, 0:n], mean[:, 0:n], mean[:, 0:n], ALU.mult, [bmean], [bvar])
            TT(P, 'dve', var[:, 0:n], P.bank(1)[:, 0:n], var[:, 0:n], ALU.subtract, [PB[1], bvar], [bvar])
            TS(P, 'dve', var[:, 0:n], var[:, 0:n], EPS, None, ALU.add, None, [bvar], [bvar])
            ACT(P, var[:, 0:n], var[:, 0:n], AF.Sqrt, [bvar], [bvar])
            P.op('dve', lambda e, n=n: e.reciprocal(out=var[:, 0:n], in_=var[:, 0:n]), [bvar], [bvar])
            c_t, bc = co[bi % 2], bco[bi % 2]
            c3 = r3(c_t, 2, 512)
            for c in range(2):
                TT(P, 'dve', y[:, 0:n], A3[:, c, o:o + n], mean[:, 0:n], ALU.subtract, [bA, bmean], [by])
                TT(P, 'dve', y[:, 0:n], y[:, 0:n], var[:, 0:n], ALU.mult, [by, bvar], [by])
                ACT(P, c3[:, c, 0:n], y[:, 0:n], AF.Silu, [by, bcp], [bc], scale=cp3[:, c, 1:2], bias=cp3[:, c, 2:3])
            P.dma(S['cvT'][:, :, t0:t0 + n].rearrange('c p t -> p c t'), c3[:, :, 0:n], bc, False)
        P.barrier()
        P.release()

    def phase_D(l):
        P.mark()
        win = I['w_in'][l].rearrange('(k p) n -> p k n', p=128)
        stg = [P.alloc(1024) for _ in range(3)]
        bstg = [Buf() for _ in range(3)]
        wg = P.alloc(8 * 3072, BF16)
        b_wg = Buf()
        wg3 = r3(wg, 8, 3072)
        for part in range(3):
            load_w_bf16(wg3[:, :, part * 1024:(part + 1) * 1024], b_wg, win[:, :, 1184 + part * 1024:1184 + (part + 1) * 1024],
                        1024, stg, bstg, 8)
        wba = P.alloc(4 * 1024, BF16)
        wbf = P.alloc(2 * 1024, BF16)
        wbc = P.alloc(2 * 1024, BF16)
        wo = P.alloc(8 * 1024, BF16)
        b_wb = Buf()
        load_w_bf16(r3(wba, 4, 1024), b_wb, I['wb_attn'][l].rearrange('(k p) n -> p k n', p=128), 1024, stg, bstg, 4)
        load_w_bf16(r3(wbf, 2, 1024), b_wb, I['wb_fnet'][l].rearrange('(k p) n -> p k n', p=128), 1024, stg, bstg, 2)
        load_w_bf16(r3(wbc, 2, 1024), b_wb, I['wb_conv'][l].rearrange('(k p) n -> p k n', p=128), 1024, stg, bstg, 2)
        load_w_bf16(r3(wo, 8, 1024), b_wb, I['w_out'][l].rearrange('(k p) n -> p k n', p=128), 1024, stg, bstg, 8)
        wbr = [(r3(wba, 4, 1024), 4), (r3(wbf, 2, 1024), 2), (r3(wbc, 2, 1024), 2)]
        wo3 = r3(wo, 8, 1024)
        ht = HT()
        sgt = P.alloc(24 * TB)
        b_sgt = Buf()
        sg3 = r3(sgt, 24, TB)
        br = [P.alloc(8 * TB, BF16) for _ in range(2)]
        b_br = [Buf(), Buf()]
        mT = P.alloc(8 * TB, BF16)
        b_mT = Buf()
        mT3 = r3(mT, 8, TB)
        t1 = P.alloc(TB)
        t2 = P.alloc(TB)
        b_t1, b_t2 = Buf(), Buf()
        xo = [P.alloc(1024) for _ in range(2)]
        b_xo = [Buf(), Buf()]
        nblk = NBLK if l < L - 1 else NBLK
        for tb in range(nblk):
            if tb == 0 and l == L - 1:
                continue
            mi = 1 if tb == 0 else 0
            hT3, bh, xs, bxs = ht.make(l, tb, 0, tbank=0)
            b_t, bb = br[tb % 2], b_br[tb % 2]
            b3 = r3(b_t, 8, TB)
            tsl = slice(tb * TB, (tb + 1) * TB)
            P.dma(b3[:, 0:4, :], S['aT'][:, :, tsl].rearrange('c p t -> p c t'), bb, True)
            P.dma(b3[:, 4:6, :], S['fT'][:, :, tsl].rearrange('c p t -> p c t'), bb, True)
            P.dma(b3[:, 6:8, :], S['cvT'][:, :, tsl].rearrange('c p t -> p c t'), bb, True)
            for gc in range(24):
                bk = 1 + (gc % 2)
                ps = P.bank(bk)[:, 0:TB]
                for k in range(8):
                    MM(P, ps, wg3[:, k, gc * 128:(gc + 1) * 128], hT3[:, k, :], k == 0, k == 7, [b_wg, bh], [PB[bk]])
                ACT(P, sg3[:, gc, :], ps, AF.Sigmoid, [PB[bk]], [b_sgt])
            for oc in range(8):
                koff = 0
                for bi, (w3, nk) in enumerate(wbr):
                    ps = P.bank(3 + bi)[:, 0:TB]
                    for kc in range(nk):
                        MM(P, ps, w3[:, kc, oc * 128:(oc + 1) * 128], b3[:, koff + kc, :], kc == 0, kc == nk - 1, [b_wb, bb], [PB[3 + bi]])
                    koff += nk
                TT(P, 'dve', t1, P.bank(3)[:, 0:TB], sg3[:, oc, :], ALU.mult, [PB[3], b_sgt], [b_t1])
                TT(P, 'dve', t2, P.bank(4)[:, 0:TB], sg3[:, 8 + oc, :], ALU.mult, [PB[4], b_sgt], [b_t2])
                TT(P, 'dve', t1, t1, t2, ALU.add, [b_t1, b_t2], [b_t1])
                TT(P, 'dve', t2, P.bank(5)[:, 0:TB], sg3[:, 16 + oc, :], ALU.mult, [PB[5], b_sgt], [b_t2])
                TT(P, 'dve', mT3[:, oc, :], t1, t2, ALU.add, [b_t1, b_t2], [b_mT])
            for tt in range(2):
                x_o, bxo = xo[tt], b_xo[tt]
                for nh in range(2):
                    ps = P.bank(6 + nh)
                    for k in range(8):
                        MM(P, ps, mT3[:, k, tt * 128:(tt + 1) * 128], wo3[:, k, nh * 512:(nh + 1) * 512], k == 0, k == 7,
                           [b_mT, b_wb], [PB[6 + nh]])
                    TT(P, 'dve', x_o[:, nh * 512:(nh + 1) * 512], ps, gtb[mi][:, nh * 512:(nh + 1) * 512], ALU.mult,
                       [PB[6 + nh], b_gtb[mi]], [bxo])
                TT(P, 'pool', x_o, x_o, xs[tt], ALU.add, [bxo, bxs[tt]], [bxo])
                P.dma(S['xres'][tb * TB + tt * 128: tb * TB + (tt + 1) * 128, :], x_o, bxo, False)
        P.barrier()
        P.release()

    def phase_T(l):
        P.mark()
        us = [P.alloc(1024) for _ in range(2)]
        b_us = [Buf(), Buf()]
        vsl = [P.alloc(1024) for _ in range(2)]
        b_vs = [Buf(), Buf()]
        ub = [P.alloc(1024, BF16) for _ in range(2)]
        b_ub = [Buf(), Buf()]
        uo = [P.alloc(1024, BF16) for _ in range(2)]
        b_uo = [Buf(), Buf()]
        vb = [P.alloc(1024, BF16) for _ in range(2)]
        b_vb = [Buf(), Buf()]
        usrc = I['u_tab'][l].rearrange('(i j) d -> j i d', j=NCH)
        vsrc = I['v_tab'][l].rearrange('(i j) d -> j i d', j=NCH)
        for j in range(NCH):
            s = j % 2
            P.dma(us[s], usrc[j], b_us[s], True)
            P.dma(vsl[s], vsrc[j], b_vs[s], True)
            CP(P, 'pool', vb[s], vsl[s], [b_vs[s]], [b_vb[s]])
            P.dma(S['vs'][j], vb[s], b_vb[s], False)
            CP(P, 'dve', ub[s], us[s], [b_us[s]], [b_ub[s]])
            pt = r3(P.bank(s).bitcast(BF16), 8, 128)
            for k in range(8):
                TR(P, pt[:, k, :], ub[s][:, k * 128:(k + 1) * 128], identB, [b_ub[s], b_idB], [PB[s]])
            CP(P, 'act', uo[s], P.bank(s).bitcast(BF16), [PB[s]], [b_uo[s]])
            P.dma(S['uTs'][j], uo[s], b_uo[s], False)
        P.barrier()
        P.release()

    def phase_E(l):
        P.mark()
        last = (l == L - 1)
        wq = P.alloc(8 * 1024, BF16)
        b_wq = Buf()
        wq3 = r3(wq, 8, 1024)
        skb = P.alloc(8 * 256, BF16)
        b_sk = Buf()
        P.mark()
        stg = [P.alloc(1024) for _ in range(2)]
        bstg = [Buf(), Buf()]
        load_w_bf16(wq3, b_wq, I['w_query'][l].rearrange('(k p) n -> p k n', p=128), 1024, stg, bstg, 8)
        for hf in range(2):
            P.dma(stg[hf], I['skbd'][l][:, hf * 1024:(hf + 1) * 1024], bstg[hf], True)
            CP(P, 'dve', skb[:, hf * 1024:(hf + 1) * 1024], stg[hf], [bstg[hf]], [b_sk])
        P.barrier()
        P.release()
        skb3 = r3(skb, 8, 256)
        ht = HT(nx=4, nh=2)
        qT = P.alloc(8 * TB, BF16)
        b_qT = Buf()
        qT3 = r3(qT, 8, TB)
        sc = P.alloc(2048)
        b_sc = Buf()
        sc4 = r4(sc, 8, 2, 128)
        tmp = P.alloc(128)
        b_tmp = Buf()
        val = P.alloc(256)
        b_val = Buf()
        val4 = r4(val, 8, 2, 16)
        idx = P.alloc(256).bitcast(U32)
        b_idx = Buf()
        idx4 = r4(idx, 8, 2, 16)
        idxf = P.alloc(256)
        b_idxf = Buf()
        idxf4 = r4(idxf, 8, 2, 16)
        cand = P.alloc(2048)
        b_cand = Buf()
        cand4 = r4(cand, 8, 16, 16)
        ctmp = P.alloc(256)
        b_ctmp = Buf()
        best = P.alloc(128)
        b_best = Buf()
        best3 = r3(best, 8, 16)
        pos = P.alloc(128).bitcast(U32)
        b_pos = Buf()
        pos3 = r3(pos, 8, 16)
        pa = P.alloc(128).bitcast(U32)
        pb_ = P.alloc(128).bitcast(U32)
        paf = P.alloc(128)
        pbf = P.alloc(128)
        b_pab = Buf()
        oh = cand
        b_oh = b_cand
        oh4 = r4(oh, 8, 16, 16)
        tok3 = P.alloc(3 * 128)
        b_tok = Buf()
        tok33 = r3(tok3, 3, 128)
        gsum = P.alloc(8)
        b_gsum = Buf()
        ijw = [P.alloc(3 * TB, BF16) for _ in range(2)]
        b_ijwl = [Buf(), Buf()]
        GT = 16
        Aoh = [P.alloc(GT * 128, BF16) for _ in range(2)]
        Boh = [P.alloc(GT * 128, BF16) for _ in range(2)]
        b_A = [Buf(), Buf()]
        b_B = [Buf(), Buf()]
        G = P.alloc(TB * 128, BF16)
        b_G = Buf()
        G3 = r3(G, TB, 128)
        NS = 3
        utr = [P.alloc(1024, BF16) for _ in range(NS)]
        b_utr = [Buf() for _ in range(NS)]
        vtr = [P.alloc(1024, BF16) for _ in range(NS)]
        b_vtr = [Buf() for _ in range(NS)]
        gs = [P.alloc(TB, BF16) for _ in range(2)]
        b_gs = [Buf(), Buf()]
        av = [P.alloc(TB, BF16) for _ in range(2)]
        b_av = [Buf(), Buf()]
        xo = [P.alloc(1024) for _ in range(2)]
        b_xo = [Buf(), Buf()]
        iota128b = iota128h.unsqueeze(1).to_broadcast([128, GT, 128])
        state = {}

        def prep(tb, slot):
            ht.begin(l, tb, from_res=True)
            yield
            yield
            yield
            hT3, bh, xs, bxs = ht.finish(2, tbank=0)
            yield
            yield
            for h in range(8):
                ps = P.bank(1)[:, 0:TB]
                for k in range(8):
                    MM(P, ps, wq3[:, k, h * 128:(h + 1) * 128], hT3[:, k, :], k == 0, k == 7, [b_wq, bh], [PB[1]])
                yield
                CP(P, 'act', qT3[:, h, :], ps, [PB[1]], [b_qT])
            ijwT3 = r3(ijw[slot], 3, TB)
            b_ijw = b_ijwl[slot]
            for tt in range(2):
                tsl = slice(tt * 128, (tt + 1) * 128)
                yield
                for h in range(8):
                    ps = P.bank(1)[:, 0:256]
                    MM(P, ps, qT3[:, h, tsl], skb3[:, h, :], True, True, [b_qT, b_sk], [PB[1]])
                    yield
                    CP(P, 'act', sc[:, h * 256:(h + 1) * 256], ps, [PB[1]], [b_sc])
                yield
                for h in range(8):
                    for p in range(2):
                        s_hp = sc4[:, h, p, :]
                        v16 = val4[:, h, p, :]
                        i16 = idx4[:, h, p, :]
                        P.op('dve', lambda e, o=v16[:, 0:8], i=s_hp: e.max(out=o, in_=i), [b_sc], [b_val])
                        P.op('dve', lambda e, o=i16[:, 0:8], m=v16[:, 0:8], i=s_hp: e.max_index(out=o, in_max=m, in_values=i),
                             [b_sc, b_val], [b_idx])
                        P.op('dve', lambda e, o=tmp[:, 0:128], m=v16[:, 0:8], i=s_hp: e.match_replace(
                            out=o, in_to_replace=m, in_values=i, imm_value=-1e30), [b_sc, b_val], [b_tmp])
                        yield
                        P.op('dve', lambda e, o=v16[:, 8:16], i=tmp[:, 0:128]: e.max(out=o, in_=i), [b_tmp], [b_val])
                        P.op('dve', lambda e, o=i16[:, 8:16], m=v16[:, 8:16], i=tmp[:, 0:128]: e.max_index(
                            out=o, in_max=m, in_values=i), [b_tmp, b_val], [b_idx])
                        yield
                CP(P, 'dve', idxf, idx, [b_idx], [b_idxf])
                TT(P, 'dve', cand4, val4[:, :, 0, :].unsqueeze(3).to_broadcast([128, 8, 16, 16]),
                   val4[:, :, 1, :].unsqueeze(2).to_broadcast([128, 8, 16, 16]), ALU.add, [b_val], [b_cand])
                yield
                for h in range(8):
                    c_h = cand[:, h * 256:(h + 1) * 256]
                    b16 = best3[:, h, :]
                    p16 = pos3[:, h, :]
                    P.op('dve', lambda e, o=b16[:, 0:8], i=c_h: e.max(out=o, in_=i), [b_cand], [b_best])
                    P.op('dve', lambda e, o=p16[:, 0:8], m=b16[:, 0:8], i=c_h: e.max_index(out=o, in_max=m, in_values=i),
                         [b_cand, b_best], [b_pos])
                    P.op('dve', lambda e, o=ctmp, m=b16[:, 0:8], i=c_h: e.match_replace(
                        out=o, in_to_replace=m, in_values=i, imm_value=-1e30), [b_cand, b_best], [b_ctmp])
                    yield
                    P.op('dve', lambda e, o=b16[:, 8:16], i=ctmp: e.max(out=o, in_=i), [b_ctmp], [b_best])
                    P.op('dve', lambda e, o=p16[:, 8:16], m=b16[:, 8:16], i=ctmp: e.max_index(out=o, in_max=m, in_values=i),
                         [b_ctmp, b_best], [b_pos])
                    yield
                P.op('dve', lambda e: e.tensor_single_scalar(out=pa, in_=pos, scalar=4, op=ALU.logical_shift_right),
                     [b_pos], [b_pab])
                P.op('dve', lambda e: e.tensor_single_scalar(out=pb_, in_=pos, scalar=15, op=ALU.bitwise_and),
                     [b_pos], [b_pab])
                yield
                CP(P, 'dve', paf, pa, [b_pab], [b_pab])
                CP(P, 'dve', pbf, pb_, [b_pab], [b_pab])
                yield
                for which, pf in enumerate((paf, pbf)):
                    pf3 = r3(pf, 8, 16)
                    TT(P, 'dve', oh4, iota16.unsqueeze(1).unsqueeze(1).to_broadcast([128, 8, 16, 16]),
                       pf3.unsqueeze(3).to_broadcast([128, 8, 16, 16]), ALU.is_equal, [b_iota, b_pab], [b_oh])
                    yield
                    TT(P, 'dve', oh4, oh4, idxf4[:, :, which, :].unsqueeze(2).to_broadcast([128, 8, 16, 16]), ALU.mult,
                       [b_oh, b_idxf], [b_oh])
                    yield
                    RED(P, tok33[:, which, :], r3(oh, 128, 16), ALU.add, [b_oh], [b_tok])
                    yield
                w3 = r3(tok33[:, 2, :], 8, 16)
                TT(P, 'dve', w3, best3, best3[:, :, 0:1].to_broadcast([128, 8, 16]), ALU.subtract, [b_best], [b_tok])
                yield
                ACT(P, tok33[:, 2, :], tok33[:, 2, :], AF.Exp, [b_tok], [b_tok])
                yield
                RED(P, gsum, w3, ALU.add, [b_tok], [b_gsum])
                P.op('dve', lambda e: e.reciprocal(out=gsum, in_=gsum), [b_gsum], [b_gsum])
                TT(P, 'dve', w3, w3, gsum.unsqueeze(2).to_broadcast([128, 8, 16]), ALU.mult, [b_tok, b_gsum], [b_tok])
                yield
                yield
                pT = r3(P.bank(1)[:, 0:384], 3, 128)
                for c in range(3):
                    TR(P, pT[:, c, :], tok33[:, c, :], identF, [b_tok, b_idF], [PB[1]])
                yield
                CP(P, 'act', ijwT3[:, :, tsl], pT, [PB[1]], [b_ijw])
            state[tb] = (hT3, bh, xs, bxs, ijwT3, b_ijw)

        blocks = [tb for tb in range(NBLK) if not (tb == 0 and last)]
        for _ in prep(blocks[0], 0):
            pass
        nchunk = 0
        for bi, tb in enumerate(blocks):
            mi = 1 if tb == 0 else 0
            hT3, bh, xs, bxs, ijwT3, b_ijw = state.pop(tb)
            for g in range(TB // GT):
                A_t, bA = Aoh[g % 2], b_A[g % 2]
                B_t, bB = Boh[g % 2], b_B[g % 2]
                A3, B3 = r3(A_t, GT, 128), r3(B_t, GT, 128)
                gsl = slice(g * GT, (g + 1) * GT)
                TT(P, 'dve', A3, iota128b, ijwT3[:, 0, gsl].unsqueeze(2).to_broadcast([128, GT, 128]), ALU.is_equal,
                   [b_iota, b_ijw], [bA])
                TT(P, 'dve', B3, iota128b, ijwT3[:, 1, gsl].unsqueeze(2).to_broadcast([128, GT, 128]), ALU.is_equal,
                   [b_iota, b_ijw], [bB])
                TT(P, 'dve', B3, B3, ijwT3[:, 2, gsl].unsqueeze(2).to_broadcast([128, GT, 128]), ALU.mult,
                   [bB, b_ijw], [bB])
                for t4 in range(GT // 4):
                    bk = t4 % 2
                    for ti in range(4):
                        tloc = t4 * 4 + ti
                        MM(P, P.bank(bk)[:, ti * 128:(ti + 1) * 128], A3[:, tloc, :], B3[:, tloc, :], True, True, [bA, bB], [PB[bk]])
                    tg0 = g * GT + t4 * 4
                    CP(P, 'act', G[:, tg0 * 128:(tg0 + 4) * 128], P.bank(bk), [PB[bk]], [b_G])
            nxt = prep(blocks[bi + 1], (bi + 1) % 2) if bi + 1 < len(blocks) else None

            def load(j):
                s_ = (nchunk + j) % NS
                P.dma(utr[s_], S['uTs'][j], b_utr[s_], True, q='sp')
                P.dma(vtr[s_], S['vs'][j], b_vtr[s_], True, q='sp')

            def mm1(j):
                s_ = (nchunk + j) % NS
                u3 = r3(utr[s_], 8, 128)
                sb = 2 + ((nchunk + j) % 2)
                for k in range(8):
                    MM(P, P.bank(sb)[:, 0:TB], u3[:, k, :], hT3[:, k, :], k == 0, k == 7, [b_utr[s_], bh], [PB[sb]])

            for j in range(min(NS, NCH)):
                load(j)
            mm1(0)
            for j in range(NCH):
                if j + 1 < NCH:
                    mm1(j + 1)
                s_ = (nchunk + j) % NS
                sb = 2 + ((nchunk + j) % 2)
                g_t, bg = gs[(nchunk + j) % 2], b_gs[(nchunk + j) % 2]
                ACT(P, g_t, P.bank(sb)[:, 0:TB], AF.Gelu_apprx_tanh, [PB[sb]], [bg])
                a_t, ba = av[(nchunk + j) % 2], b_av[(nchunk + j) % 2]
                TT(P, 'dve', a_t, g_t, G3[:, :, j], ALU.mult, [bg, b_G], [ba])
                for tt in range(2):
                    for nh in range(2):
                        ob = 4 + tt * 2 + nh
                        MM(P, P.bank(ob), a_t[:, tt * 128:(tt + 1) * 128], vtr[s_][:, nh * 512:(nh + 1) * 512],
                           j == 0, j == NCH - 1, [ba, b_vtr[s_]], [PB[ob]])
                if j + NS < NCH:
                    load(j + NS)
                if nxt is not None:
                    next(nxt, None)
                    if j % 4 == 0:
                        next(nxt, None)
            nchunk += NCH
            if nxt is not None:
                for _ in nxt:
                    pass
            for tt in range(2):
                x_o, bxo = xo[tt], b_xo[tt]
                for nh in range(2):
                    ob = 4 + tt * 2 + nh
                    TT(P, 'dve', x_o[:, nh * 512:(nh + 1) * 512], P.bank(ob), gtb[2 + mi][:, nh * 512:(nh + 1) * 512], ALU.mult,
                       [PB[ob], b_gtb[2 + mi]], [bxo])
                TT(P, 'pool', x_o, x_o, xs[tt], ALU.add, [bxo, bxs[tt]], [bxo])
                r0 = tb * TB + tt * 128
                if last:
                    P.dma(out_d[r0 - CT:r0 - CT + 128, :], x_o, bxo, False)
                else:
                    P.dma(S['xres'][r0:r0 + 128, :], x_o, bxo, False)
        print('phase E sbuf words', P.sb_off, 'of', P.sb_words)
        P.barrier()
        P.release()

    phases = []
    for l in range(n_layers):
        phases += [('adaln', phase_adaln, l), ('A', phase_A, l), ('B', phase_B, l), ('C1', phase_C1, l),
                   ('C2', phase_C2, l), ('D', phase_D, l), ('T', phase_T, l), ('E', phase_E, l)]
    for name, fn, l in phases:
        fn(l)
        if stop_after is not None and stop_after == (name, l):
            break
    P.emit()
    return nc


def _consts():
    t = np.arange(T, dtype=np.float64)
    row = np.repeat(np.arange(T // 64, dtype=np.float32), 64)
    col = np.tile(np.arange(64, dtype=np.float32), T // 64)
    inv = (np.float32(10000.0) ** (-np.arange(0, 16, 2, dtype=np.float32) / np.float32(16))).astype(np.float32)
    ar = row[:, None] * inv
    ac = col[:, None] * inv
    ang = np.concatenate([ar, ar, ac, ac], axis=-1).astype(np.float32)
    ropeC = np.cos(ang).astype(np.float32)
    sn = np.sin(ang).astype(np.float32)
    sign = np.tile(np.concatenate([-np.ones(8, np.float32), np.ones(8, np.float32)]), 2)
    ropeS = (sn * sign[None, :]).astype(np.float32)
    n = (np.outer(np.arange(T), np.arange(T)) % T).astype(np.float64)
    dftP = np.stack([np.cos(2 * np.pi * n / T) / 64.0, -np.sin(2 * np.pi * n / T) / 64.0]).astype(ml_dtypes.bfloat16)
    n2 = (np.outer(np.arange(CT), np.arange(CT)) % CT).astype(np.float64)
    dftC = np.stack([np.cos(2 * np.pi * n2 / CT) / 16.0, -np.sin(2 * np.pi * n2 / CT) / 16.0]).astype(ml_dtypes.bfloat16)
    c = np.arange(128)
    m = np.arange(128)
    same = (c[:, None] // 64) == (m[None, :] // 64)
    ph = 2 * np.pi * ((c[:, None] % 64) * (m[None, :] % 64) % 64) / 64.0
    csbd = np.concatenate([np.where(same, np.cos(ph) / 8.0, 0.0), np.where(same, np.sin(ph) / 8.0, 0.0)], axis=1).astype(np.float32)
    ident = np.eye(128, dtype=np.float32)
    sel = np.zeros((2, 256), np.float32)
    sel[0, 0:128] = 1.0
    sel[1, 128:256] = 1.0
    return dict(ropeC=ropeC, ropeS=ropeS, dftP=dftP, dftC=dftC, csbd=csbd, ident=ident, sel=sel)


_CONSTS = None
_NC_CACHE = {}


def make_in_maps(inp, cores):
    global _CONSTS
    if _CONSTS is None:
        _CONSTS = _consts()
    f = lambda a: np.ascontiguousarray(np.asarray(a, dtype=np.float32))
    shared = dict(_CONSTS)
    shared['w_mod'] = f(inp['w_mod'])
    bm = f(inp['b_mod'])
    shared['bmodT'] = np.ascontiguousarray(bm.reshape(L, 48, 128).transpose(0, 2, 1))
    shared['bmod2'] = np.ascontiguousarray(np.repeat(bm[:, None, :], 2, axis=1))
    shared['g1T'] = np.ascontiguousarray(f(inp['g_norm1']).reshape(L, 8, 128).transpose(0, 2, 1))
    shared['g2T'] = np.ascontiguousarray(f(inp['g_norm2']).reshape(L, 8, 128).transpose(0, 2, 1))
    shared['w_in'] = f(inp['w_in'])
    shared['g_ckv'] = f(inp['g_ckv']).reshape(L, 128, 1)
    shared['w_ukv'] = f(inp['w_ukv'])
    shared['g_cqT'] = np.ascontiguousarray(f(inp['g_cq']).reshape(L, 2, 128).transpose(0, 2, 1))
    shared['w_uq'] = f(inp['w_uq'])
    shared['g_qn'] = f(inp['g_qn']).reshape(L, 1, 96)
    shared['g_kn'] = f(inp['g_kn']).reshape(L, 1, 96)
    wdw = f(inp['w_dw']).reshape(L, 31, 2, 128)
    shared['wdwT'] = np.ascontiguousarray(wdw.transpose(0, 3, 2, 1)).reshape(L, 128, 62)
    cvp = np.stack([f(inp['b_dw']), f(inp['g_cln']), f(inp['b_cln'])], axis=-1)
    shared['cvp'] = np.ascontiguousarray(cvp.reshape(L, 2, 128, 3).transpose(0, 2, 1, 3)).reshape(L, 128, 6)
    for k in ('wb_attn', 'wb_fnet', 'wb_conv', 'w_out', 'w_query', 'u_tab', 'v_tab'):
        shared[k] = f(inp[k])
    sk = f(inp['sub_keys'])
    skbd = np.zeros((L, 128, 8, 256), np.float32)
    for p in range(2):
        skbd[:, p * 64:(p + 1) * 64, :, p * 128:(p + 1) * 128] = sk[:, :, p].transpose(0, 3, 1, 2)
    shared['skbd'] = skbd.reshape(L, 128, 2048)
    x = f(inp['x'])
    ctx = f(inp['ctx'])
    c = f(inp['c'])
    cc = f(inp['c_ctx'])
    maps = []
    for b in cores:
        m = dict(shared)
        m['x'] = x[b]
        m['ctx'] = ctx[b]
        cv = np.stack([c[b].reshape(8, 128).T, cc.reshape(8, 128).T], axis=-1)
        m['cvec'] = np.ascontiguousarray(cv).reshape(128, 16)
        maps.append(m)
    return maps


def kernel(**inputs):
    if 'full' not in _NC_CACHE:
        _NC_CACHE['full'] = build_program()
    nc = _NC_CACHE['full']
    maps = make_in_maps(inputs, list(range(8)))
    res = run_bass_kernel_spmd(nc, maps, core_ids=list(range(8)))
    return np.stack([np.asarray(r['out'], dtype=np.float32) for r in res.results], axis=0)
```

```python
import numpy as np
import ml_dtypes
import concourse.bass as bass
import concourse.mybir as mybir
from concourse.bass_utils import run_bass_kernel_spmd

F32 = mybir.dt.float32
BF16 = mybir.dt.bfloat16
U32 = mybir.dt.uint32
AF = mybir.ActivationFunctionType
ALU = mybir.AluOpType
AX = mybir.AxisListType

L = 2
D = 1024
T = 4096
CT = 256
NT = T + CT
TB = 256
NBLK = NT // TB
EPS = 1e-6
NCH = 128

CE = ('pe', 'act', 'dve', 'pool')
ENG = ('pe', 'act', 'dve', 'pool', 'sp')


class Buf:
    __slots__ = ('name', 'w', 'r', 'dkey', 'dcnt')

    def __init__(self, name=''):
        self.name = name
        self.w = None
        self.r = {}
        self.dkey = None
        self.dcnt = 0


class Prog:
    def __init__(self, nc, sbuf_words=207 * 256):
        self.nc = nc
        self.q = {e: [] for e in ENG}
        self.semh = {e: nc.alloc_semaphore('s_' + e) for e in CE}
        self.cnt = {e: 0 for e in CE}
        self.known = {e: {} for e in ENG}
        self.dtot = {}
        self.nd = 0
        self.sb = nc.alloc_sbuf_tensor('sb_all', [128, sbuf_words], F32)
        self.sb_words = sbuf_words
        self.sb_off = 0
        self.ps = nc.alloc_psum_tensor('ps_all', [128, 4096], F32)
        self.marks = []
        self.free_keys = []
        self.scope_keys = []
        self.pbuf = [Buf('bank%d' % i) for i in range(8)]

    def alloc(self, nelem, dtype=F32):
        nbytes = nelem * (2 if dtype == BF16 else 4)
        words = (nbytes + 3) // 4
        words = (words + 7) // 8 * 8
        assert self.sb_off + words <= self.sb_words, ('SBUF overflow', self.sb_off, words)
        ap = self.sb[:, self.sb_off:self.sb_off + words]
        self.sb_off += words
        if dtype != F32:
            ap = ap.bitcast(dtype)
        return ap[:, 0:nelem]

    def mark(self):
        self.marks.append((self.sb_off, len(self.scope_keys)))

    def release(self):
        self.sb_off, nk = self.marks.pop()
        self.free_keys += self.scope_keys[nk:]
        del self.scope_keys[nk:]

    def bank(self, b, n=1):
        return self.ps[:, b * 512:(b + n) * 512]

    def _wait(self, e, key, count):
        if key == e and e == 'pe':
            return
        if self.known[e].get(key, 0) >= count:
            return
        self.known[e][key] = count
        h = self.semh[key]
        self.q[e].append(lambda eng, h=h, c=count: eng.wait_ge(h, c))

    def _deps(self, e, r, w):
        for b in r:
            if b.w is not None:
                self._wait(e, *b.w)
        for b in w:
            if b.w is not None:
                self._wait(e, *b.w)
            for k, c in b.r.items():
                self._wait(e, k, c)

    def op(self, e, fn, r=(), w=()):
        self._deps(e, r, w)
        self.cnt[e] += 1
        c = self.cnt[e]
        h = self.semh[e]
        self.q[e].append(lambda eng, fn=fn, h=h: fn(eng).then_inc(h, 1))
        for b in r:
            b.r[e] = c
        for b in w:
            b.w = (e, c)
            b.r = {}

    def dma(self, out, in_, buf, load, q='sp', extra_r=(), **kw):
        if buf.dkey is None:
            if self.free_keys:
                buf.dkey = self.free_keys.pop()
            else:
                buf.dkey = 'd%d' % self.nd
                self.nd += 1
                self.semh[buf.dkey] = self.nc.alloc_semaphore(buf.dkey)
            if self.marks:
                self.scope_keys.append(buf.dkey)
            buf.dcnt = self.dtot.get(buf.dkey, 0)
        if load:
            self._deps(q, list(extra_r), [buf])
        else:
            self._deps(q, [buf] + list(extra_r), [])
        buf.dcnt += 16
        c = buf.dcnt
        self.dtot[buf.dkey] = c
        h = self.semh[buf.dkey]
        self.q[q].append(lambda eng, h=h, out=out, in_=in_, kw=kw: eng.dma_start(out=out, in_=in_, **kw).then_inc(h, 16))
        if load:
            buf.w = (buf.dkey, c)
            buf.r = {}
        else:
            buf.r[buf.dkey] = c

    def barrier(self, engines=ENG):
        for e in engines:
            for k in CE:
                if k != e and self.cnt[k] > 0:
                    self._wait(e, k, self.cnt[k])
            for k, c in self.dtot.items():
                self._wait(e, k, c)

    def emit(self):
        nc = self.nc
        self.barrier(ENG)
        with nc.Block() as block:
            @block.tensor
            def _(eng):
                for f in self.q['pe']:
                    f(eng)

            @block.scalar
            def _(eng):
                for f in self.q['act']:
                    f(eng)

            @block.vector
            def _(eng):
                for f in self.q['dve']:
                    f(eng)

            @block.gpsimd
            def _(eng):
                for f in self.q['pool']:
                    f(eng)

            @block.sync
            def _(eng):
                for f in self.q['sp']:
                    f(eng)


def MM(P, out, lhsT, rhs, start, stop, r, w, skip=False):
    P.op('pe', lambda e: e.matmul(out, lhsT=lhsT, rhs=rhs, start=start, stop=stop, skip_group_check=skip), r, w)


def TR(P, out, in_, ident, r, w):
    P.op('pe', lambda e: e.transpose(out=out, in_=in_, identity=ident), r, w)


def ACT(P, out, in_, func, r, w, scale=None, bias=None, accum=None):
    kw = {}
    if scale is not None:
        kw['scale'] = scale
    if bias is not None:
        kw['bias'] = bias
    if accum is not None:
        kw['accum_out'] = accum
    P.op('act', lambda e: e.activation(out=out, in_=in_, func=func, **kw), r, w)


def TT(P, eng, out, in0, in1, op, r, w):
    P.op(eng, lambda e: e.tensor_tensor(out=out, in0=in0, in1=in1, op=op), r, w)


def TS(P, eng, out, in0, s1, s2, op0, op1, r, w):
    if op1 is None:
        P.op(eng, lambda e: e.tensor_scalar(out=out, in0=in0, scalar1=s1, scalar2=None, op0=op0), r, w)
    else:
        P.op(eng, lambda e: e.tensor_scalar(out=out, in0=in0, scalar1=s1, scalar2=s2, op0=op0, op1=op1), r, w)


def STT(P, out, in0, scalar, in1, op0, op1, r, w):
    P.op('dve', lambda e: e.scalar_tensor_tensor(out=out, in0=in0, scalar=scalar, in1=in1, op0=op0, op1=op1), r, w)


def CP(P, eng, out, in_, r, w):
    if eng == 'act':
        P.op('act', lambda e: e.copy(out=out, in_=in_), r, w)
    else:
        P.op(eng, lambda e: e.tensor_copy(out=out, in_=in_), r, w)


def RED(P, out, in_, op, r, w):
    P.op('dve', lambda e: e.tensor_reduce(out=out, in_=in_, axis=AX.X, op=op), r, w)


def TTR(P, out, in0, in1, accum, r, w):
    P.op('act', lambda e: e.activation(out=out, in_=in0, func=AF.Square, accum_out=accum), r, w)


def MEMSET(P, eng, ap, val, w):
    P.op(eng, lambda e: e.memset(ap, val), (), w)


def RSTD(P, out, ss, n, r, w):
    TS(P, 'dve', out, ss, 1.0 / n, EPS, ALU.mult, ALU.add, r, w)
    ACT(P, out, out, AF.Sqrt, w, w)
    P.op('dve', lambda e: e.reciprocal(out=out, in_=out), w, w)


def r3(ap, a, b):
    return ap.rearrange('p (a b) -> p a b', a=a, b=b)


def r4(ap, a, b, c):
    return ap.rearrange('p (a b c) -> p a b c', a=a, b=b, c=c)


def build_program(debug=False, n_layers=L, stop_after=None):
    nc = bass.Bass('TRN2', target_bir_lowering=False)

    def din(name, shape, dt=F32):
        return nc.dram_tensor(name, list(shape), dt, kind='ExternalInput').ap()

    skind = 'ExternalOutput' if debug else 'Internal'

    def dsc(name, shape, dt):
        return nc.dram_tensor(name, list(shape), dt, kind=skind).ap()

    I = {}
    I['x'] = din('x', [T, D])
    I['ctx'] = din('ctx', [CT, D])
    I['cvec'] = din('cvec', [128, 16])
    I['w_mod'] = din('w_mod', [L, D, 6 * D])
    I['bmodT'] = din('bmodT', [L, 128, 48])
    I['bmod2'] = din('bmod2', [L, 2, 6 * D])
    I['g1T'] = din('g1T', [L, 128, 8])
    I['g2T'] = din('g2T', [L, 128, 8])
    I['w_in'] = din('w_in', [L, D, 4256])
    I['g_ckv'] = din('g_ckv', [L, 128, 1])
    I['w_ukv'] = din('w_ukv', [L, 128, 1024])
    I['g_cqT'] = din('g_cqT', [L, 128, 2])
    I['w_uq'] = din('w_uq', [L, 256, 768])
    I['g_qn'] = din('g_qn', [L, 1, 96])
    I['g_kn'] = din('g_kn', [L, 1, 96])
    I['wdwT'] = din('wdwT', [L, 128, 2 * 31])
    I['cvp'] = din('cvp', [L, 128, 6])
    I['wb_attn'] = din('wb_attn', [L, 512, D])
    I['wb_fnet'] = din('wb_fnet', [L, 256, D])
    I['wb_conv'] = din('wb_conv', [L, 256, D])
    I['w_out'] = din('w_out', [L, D, D])
    I['w_query'] = din('w_query', [L, D, D])
    I['skbd'] = din('skbd', [L, 128, 8 * 256])
    I['u_tab'] = din('u_tab', [L, 16384, D])
    I['v_tab'] = din('v_tab', [L, 16384, D])
    I['ropeC'] = din('ropeC', [T, 32])
    I['ropeS'] = din('ropeS', [T, 32])
    I['dftP'] = din('dftP', [2, T, T], BF16)
    I['dftC'] = din('dftC', [2, CT, CT], BF16)
    I['csbd'] = din('csbd', [128, 256])
    I['ident'] = din('ident', [128, 128])
    I['sel'] = din('sel', [2, 256])
    out_d = nc.dram_tensor('out', [T, D], F32, kind='ExternalOutput').ap()

    S = {}
    S['xres'] = dsc('xres', [NT, D], F32)
    S['QT'] = dsc('QT', [8, 96, NT], BF16)
    S['ZCS'] = dsc('ZCS', [NT, 512], BF16)
    S['UT'] = dsc('UT', [2, 128, NT], F32)
    S['aT'] = dsc('aT', [4, 128, NT], BF16)
    S['fT'] = dsc('fT', [2, 128, NT], BF16)
    S['cvT'] = dsc('cvT', [2, 128, NT], BF16)
    S['uTs'] = nc.dram_tensor('uTs', [NCH, 128, 1024], BF16, kind='Internal').ap()
    S['vs'] = nc.dram_tensor('vs', [NCH, 128, 1024], BF16, kind='Internal').ap()

    P = Prog(nc)
    PB = P.pbuf

    identF = P.alloc(128)
    b_idF = Buf()
    P.dma(identF, I['ident'], b_idF, True)
    identB = P.alloc(128, BF16)
    b_idB = Buf()
    CP(P, 'dve', identB, identF, [b_idF], [b_idB])
    sel = P.alloc(256)
    b_sel = Buf()
    P.dma(sel[0:2, :], I['sel'], b_sel, True)
    cvec = P.alloc(16)
    b_cvec = Buf()
    P.dma(cvec, I['cvec'], b_cvec, True)
    scv = P.alloc(16)
    b_scv = Buf()
    ACT(P, scv, cvec, AF.Silu, [b_cvec], [b_scv])
    ones256 = P.alloc(128)
    b_ones = Buf()
    MEMSET(P, 'dve', ones256, 1.0 / 256, [b_ones])
    iota16 = P.alloc(16)
    iota128 = P.alloc(128)
    b_iota = Buf()
    P.op('pool', lambda e: e.iota(iota16, pattern=[[1, 16]], base=0, channel_multiplier=0,
                                  allow_small_or_imprecise_dtypes=True), (), [b_iota])
    P.op('pool', lambda e: e.iota(iota128, pattern=[[1, 128]], base=0, channel_multiplier=0,
                                  allow_small_or_imprecise_dtypes=True), (), [b_iota])
    iota128h = P.alloc(128, BF16)
    CP(P, 'dve', iota128h, iota128, [b_iota], [b_iota])
    csbdF = P.alloc(256)
    b_csF = Buf()
    P.dma(csbdF, I['csbd'], b_csF, True)
    csbd = P.alloc(256, BF16)
    b_cs = Buf()
    CP(P, 'dve', csbd, csbdF, [b_csF], [b_cs])

    modF = P.alloc(4 * 8 * 2)
    b_modF = Buf()
    gtb = [P.alloc(1024) for _ in range(4)]
    b_gtb = [Buf() for _ in range(4)]

    def xsrc(l, r0, n):
        if l == 0:
            if r0 < CT:
                return I['ctx'][r0:r0 + n, :]
            return I['x'][r0 - CT:r0 - CT + n, :]
        return S['xres'][r0:r0 + n, :]

    def phase_adaln(l):
        P.mark()
        gT = [P.alloc(8), P.alloc(8)]
        b_g = Buf()
        P.dma(gT[0], I['g1T'][l], b_g, True)
        b_g2 = Buf()
        P.dma(gT[1], I['g2T'][l], b_g2, True)
        bmT = P.alloc(48)
        b_bm = Buf()
        P.dma(bmT, I['bmodT'][l], b_bm, True)
        bm2 = P.alloc(6 * D)
        b_bm2 = Buf()
        P.dma(bm2[0:2, :], I['bmod2'][l], b_bm2, True)
        rows = P.alloc(2048)
        b_rows = Buf()
        wslot = [P.alloc(8 * 512) for _ in range(2)]
        b_ws = [Buf(), Buf()]
        wsrc = I['w_mod'][l].rearrange('(k p) n -> p k n', p=128)
        modF4 = r4(modF, 4, 8, 2)
        kindmap = {0: 0, 1: 1, 3: 2, 4: 3}
        for nb in range(12):
            ws, bw = wslot[nb % 2], b_ws[nb % 2]
            ws3 = r3(ws, 8, 512)
            P.dma(ws3, wsrc[:, :, nb * 512:(nb + 1) * 512], bw, True)
            kind = nb // 2
            if kind in (2, 5):
                ps = P.bank(0)
                for k in range(8):
                    MM(P, ps[0:2, :], r3(scv, 8, 2)[:, k, :], ws3[:, k, :], k == 0, k == 7, [bw, b_scv], [PB[0]])
                gi = 0 if kind == 2 else 1
                c0 = gi * 1024 + (nb % 2) * 512
                TT(P, 'dve', rows[0:2, c0:c0 + 512], ps[0:2, :], bm2[0:2, nb * 512:(nb + 1) * 512], ALU.add,
                   [PB[0], b_bm2], [b_rows])
            else:
                ps = P.bank(1)
                for cc in range(4):
                    for k in range(8):
                        MM(P, ps[:, cc * 2:cc * 2 + 2], ws3[:, k, cc * 128:(cc + 1) * 128], r3(scv, 8, 2)[:, k, :],
                           k == 0, k == 7, [bw, b_scv], [PB[1]])
                for cc in range(4):
                    gc = nb * 4 + cc
                    kk = gc % 8
                    dst = modF4[:, kindmap[kind], kk, :]
                    if kind in (0, 3):
                        TS(P, 'dve', dst, ps[:, cc * 2:cc * 2 + 2], bmT[:, gc:gc + 1], None, ALU.add, None,
                           [PB[1], b_bm], [b_modF])
                    else:
                        g = gT[0] if kind == 1 else gT[1]
                        TS(P, 'dve', dst, ps[:, cc * 2:cc * 2 + 2], bmT[:, gc:gc + 1], 1.0, ALU.add, ALU.add,
                           [PB[1], b_bm], [b_modF])
                        TS(P, 'dve', dst, dst, g[:, kk:kk + 1], None, ALU.mult, None, [b_modF, b_g, b_g2], [b_modF])
        for gi in range(2):
            for mi in range(2):
                for hf in range(2):
                    ps = P.bank(2 + hf)
                    MM(P, ps, sel[0:2, mi * 128:(mi + 1) * 128], rows[0:2, gi * 1024 + hf * 512: gi * 1024 + hf * 512 + 512],
                       True, True, [b_sel, b_rows], [PB[2 + hf]])
                    CP(P, 'act', gtb[gi * 2 + mi][:, hf * 512:(hf + 1) * 512], ps, [PB[2 + hf]], [b_gtb[gi * 2 + mi]])
        P.barrier()
        P.release()

    class HT:
        def __init__(self, nx=4, nh=2):
            self.nx, self.nh = nx, nh
            self.x = [P.alloc(1024) for _ in range(nx)]
            self.bx = [Buf() for _ in range(nx)]
            self.xn = [P.alloc(1024, BF16) for _ in range(2)]
            self.bxn = [Buf(), Buf()]
            self.junk = P.alloc(1024)
            self.bj = Buf()
            self.ss = P.alloc(2)
            self.bss = Buf()
            self.hT = [P.alloc(8 * TB, BF16) for _ in range(nh)]
            self.bh = [Buf() for _ in range(nh)]
            self.n = 0

        def begin(self, l, tb, from_res=False):
            xs, bxs = [], []
            for tt in range(2):
                xi = (self.n * 2 + tt) % self.nx
                xt, bx = self.x[xi], self.bx[xi]
                r0_ = tb * TB + tt * 128
                P.dma(xt, S['xres'][r0_:r0_ + 128, :] if from_res else xsrc(l, r0_, 128), bx, True)
                TTR(P, self.junk, xt, xt, self.ss[:, tt:tt + 1], [bx], [self.bj, self.bss])
                xs.append(xt)
                bxs.append(bx)
            RSTD(P, self.ss, self.ss, D, [self.bss], [self.bss])
            for tt in range(2):
                ACT(P, self.xn[tt], xs[tt], AF.Copy, [bxs[tt], self.bss], [self.bxn[tt]], scale=self.ss[:, tt:tt + 1])
            self.cur = (tb, xs, bxs)

        def finish(self, kind, tbank=0):
            tb, xs, bxs = self.cur
            mi = 1 if tb == 0 else 0
            modF4 = r4(modF, 4, 8, 2)
            slot = self.n % self.nh
            hT, bh = self.hT[slot], self.bh[slot]
            hT3 = r3(hT, 8, TB)
            for tt in range(2):
                xn, bxn = self.xn[tt], self.bxn[tt]
                pt = r3(P.bank(tbank).bitcast(BF16), 8, 128)
                for k in range(8):
                    TR(P, pt[:, k, :], xn[:, k * 128:(k + 1) * 128], identB, [bxn, b_idB], [PB[tbank]])
                for k in range(8):
                    if k % 2 == 1:
                        ACT(P, hT3[:, k, tt * 128:(tt + 1) * 128], pt[:, k, :], AF.Identity, [PB[tbank], b_modF], [bh],
                            scale=modF4[:, kind + 1, k, mi:mi + 1], bias=modF4[:, kind, k, mi:mi + 1])
                    else:
                        TS(P, 'dve', hT3[:, k, tt * 128:(tt + 1) * 128], pt[:, k, :], modF4[:, kind + 1, k, mi:mi + 1],
                           modF4[:, kind, k, mi:mi + 1], ALU.mult, ALU.add, [PB[tbank], b_modF], [bh])
            self.n += 1
            return hT3, bh, xs, bxs

        def make(self, l, tb, kind, tbank=0, from_res=False):
            self.begin(l, tb, from_res)
            return self.finish(kind, tbank)

    def load_w_bf16(dst3, bdst, src3, ncols, stg, bstg, k_n):
        for k in range(k_n):
            s, bs = stg[k % len(stg)], bstg[k % len(stg)]
            P.dma(s[:, 0:ncols], src3[:, k, :], bs, True)
            if k % 2 == 0:
                CP(P, 'act', dst3[:, k, :], s[:, 0:ncols], [bs], [bdst])
            else:
                CP(P, 'pool', dst3[:, k, :], s[:, 0:ncols], [bs], [bdst])

    AT = {}

    def alloc_attn():
        P.mark()
        KT = P.alloc(8 * NT, BF16)
        Vs = P.alloc(34 * 8 * 65, BF16)
        AT['b_KT'] = Buf()
        AT['b_V'] = Buf()
        AT['KT3'] = r3(KT, 8, NT)
        AT['V4'] = r4(Vs, 34, 8, 65)
        MEMSET(P, 'pool', Vs, 1.0, [AT['b_V']])

    def phase_A(l):
        alloc_attn()
        KT3, V4, b_KT, b_V = AT['KT3'], AT['V4'], AT['b_KT'], AT['b_V']
        P.mark()
        win = I['w_in'][l].rearrange('(k p) n -> p k n', p=128)
        w_tm = P.alloc(8 * 416, BF16)
        w_fm = P.alloc(8 * 768, BF16)
        b_wtm, b_wfm = Buf(), Buf()
        stg = [P.alloc(1184)] * 2
        bstg = [Buf()] * 2
        w_tm3, w_fm3 = r3(w_tm, 8, 416), r3(w_fm, 8, 768)
        for k in range(8):
            s, bs = stg[k % 2], bstg[k % 2]
            P.dma(s, win[:, k, 0:1184], bs, True)
            CP(P, 'act', w_tm3[:, k, :], s[:, 0:416], [bs], [b_wtm])
            CP(P, 'pool', w_fm3[:, k, :], s[:, 416:1184], [bs], [b_wfm])
        gck = P.alloc(1)
        gcq = P.alloc(2)
        b_gc = Buf()
        P.dma(gck, I['g_ckv'][l], b_gc, True)
        b_gq2 = Buf()
        P.dma(gcq, I['g_cqT'][l], b_gq2, True)
        wukv = P.alloc(1024, BF16)
        b_wukv = Buf()
        P.dma(stg[0][:, 0:1024], I['w_ukv'][l], bstg[0], True)
        TS(P, 'dve', wukv, stg[0][:, 0:1024], gck[:, 0:1], None, ALU.mult, None, [bstg[0], b_gc], [b_wukv])
        wuq = P.alloc(2 * 768, BF16)
        b_wuq = Buf()
        wuq3 = r3(wuq, 2, 768)
        for kk in range(2):
            s, bs = stg[1 - kk], bstg[1 - kk]
            P.dma(s[:, 0:768], I['w_uq'][l][kk * 128:(kk + 1) * 128, :], bs, True)
            TS(P, 'dve', wuq3[:, kk, :], s[:, 0:768], gcq[:, kk:kk + 1], None, ALU.mult, None, [bs, b_gq2], [b_wuq])
        gqb = P.alloc(96)
        gkb = P.alloc(96)
        b_gqk = Buf()
        P.dma(gqb, I['g_qn'][l].partition_broadcast(128), b_gqk, True)
        b_gqk2 = Buf()
        P.dma(gkb, I['g_kn'][l].partition_broadcast(128), b_gqk2, True)
        gvec = [b_gqk, b_gqk2]

        ht = HT(nx=2, nh=1)
        tmz = P.alloc(416)
        b_tmz = Buf()
        zfT = P.alloc(2 * TB, BF16)
        b_zfT = Buf()
        zfT3 = r3(zfT, 2, TB)
        sg = P.alloc(TB)
        b_sg = Buf()
        ut = [P.alloc(2 * TB) for _ in range(2)]
        b_ut = [Buf(), Buf()]
        zcs = [P.alloc(512, BF16) for _ in range(2)]
        b_zcs = [Buf(), Buf()]
        st = P.alloc(24)
        b_st = Buf()
        cn = P.alloc(384, BF16)
        b_cn = Buf()
        cT = P.alloc(3 * 128, BF16)
        b_cT = Buf()
        cT3 = r3(cT, 3, 128)
        kvs = P.alloc(1024)
        b_kvs = Buf()
        qs = P.alloc(768)
        b_qs = Buf()
        sq = P.alloc(768)
        b_sq = Buf()
        ktm = P.alloc(768, BF16)
        b_ktm = Buf()
        qtm = P.alloc(768, BF16)
        b_qtm = Buf()
        krr = P.alloc(32)
        b_krr = Buf()
        rtmp = P.alloc(256)
        b_rtmp = Buf()
        rc = P.alloc(32)
        rs = P.alloc(32)
        b_rope = Buf()
        qTt = [P.alloc(8 * 128, BF16) for _ in range(2)]
        b_qTt = [Buf(), Buf()]

        for tb in range(NBLK):
            hT3, bh, xs, bxs = ht.make(l, tb, 0, tbank=0)
            is_ctx = (tb == 0)
            u_t, b_u = ut[tb % 2], b_ut[tb % 2]
            u3 = r3(u_t, 2, TB)
            for cc in range(2):
                bk = 2 + (cc % 2)
                ps = P.bank(bk)[:, 0:TB]
                for k in range(8):
                    MM(P, ps, w_fm3[:, k, cc * 128:(cc + 1) * 128], hT3[:, k, :], k == 0, k == 7, [b_wfm, bh], [PB[bk]])
                CP(P, 'act', zfT3[:, cc, :], ps, [PB[bk]], [b_zfT])
            for c2 in range(2):
                pa = P.bank(2)[:, 0:TB]
                pg = P.bank(3)[:, 0:TB]
                for k in range(8):
                    MM(P, pa, w_fm3[:, k, (2 + c2) * 128:(3 + c2) * 128], hT3[:, k, :], k == 0, k == 7, [b_wfm, bh], [PB[2]])
                for k in range(8):
                    MM(P, pg, w_fm3[:, k, (4 + c2) * 128:(5 + c2) * 128], hT3[:, k, :], k == 0, k == 7, [b_wfm, bh], [PB[3]])
                ACT(P, sg, pg, AF.Sigmoid, [PB[3]], [b_sg])
                TT(P, 'dve', u3[:, c2, :], pa, sg, ALU.mult, [PB[2], b_sg], [b_u])
            P.dma(S['UT'][:, :, tb * TB:(tb + 1) * TB].rearrange('c p t -> p c t'), u3, b_u, False)
            for tt in range(2):
                t0 = tb * TB + tt * 128
                ti = t0 // 128
                tsl = slice(tt * 128, (tt + 1) * 128)
                z, bz = zcs[tt], b_zcs[tt]
                pz = P.bank(4)
                for kc in range(2):
                    MM(P, pz[:, kc * 256:(kc + 1) * 256], zfT3[:, kc, tsl], csbd, True, True, [b_zfT, b_cs], [PB[4]])
                CP(P, 'act', z, pz, [PB[4]], [bz])
                P.dma(S['ZCS'][t0:t0 + 128, :], z, bz, False)
                pp = P.bank(1)[:, 0:416]
                for k in range(8):
                    MM(P, pp, hT3[:, k, tsl], w_tm3[:, k, :], k == 0, k == 7, [bh, b_wtm], [PB[1]])
                CP(P, 'act', tmz, pp, [PB[1]], [b_tmz])
                TTR(P, sq[:, 0:128], tmz[:, 0:128], tmz[:, 0:128], st[:, 0:1], [b_tmz], [b_sq, b_st])
                TTR(P, sq[:, 0:256], tmz[:, 160:416], tmz[:, 160:416], st[:, 1:2], [b_tmz], [b_sq, b_st])
                TS(P, 'dve', st[:, 1:2], st[:, 1:2], 0.5, None, ALU.mult, None, [b_st], [b_st])
                RSTD(P, st[:, 0:2], st[:, 0:2], 128, [b_st], [b_st])
                ACT(P, cn[:, 0:128], tmz[:, 0:128], AF.Copy, [b_tmz, b_st], [b_cn], scale=st[:, 0:1])
                ACT(P, cn[:, 128:384], tmz[:, 160:416], AF.Copy, [b_tmz, b_st], [b_cn], scale=st[:, 1:2])
                pt = r3(P.bank(0).bitcast(BF16), 8, 128)
                for j in range(3):
                    TR(P, pt[:, j, :], cn[:, j * 128:(j + 1) * 128], identB, [b_cn, b_idB], [PB[0]])
                CP(P, 'dve', cT3, pt[:, 0:3, :], [PB[0]], [b_cT])
                for hf in range(2):
                    MM(P, P.bank(5 + hf), cT3[:, 0, :], wukv[:, hf * 512:(hf + 1) * 512], True, True, [b_cT, b_wukv], [PB[5 + hf]])
                for hf in range(2):
                    CP(P, 'act', kvs[:, hf * 512:(hf + 1) * 512], P.bank(5 + hf), [PB[5 + hf]], [b_kvs])
                pq = P.bank(6, 2)
                for nh in range(2):
                    for kk in range(2):
                        MM(P, pq[:, nh * 512:nh * 512 + 384], cT3[:, 1 + kk, :], wuq3[:, kk, nh * 384:(nh + 1) * 384],
                           kk == 0, kk == 1, [b_cT, b_wuq], [PB[6], PB[7]])
                kv4 = r3(kvs, 8, 128)
                CP(P, 'pool', V4[:, ti, :, 0:64], kv4[:, :, 64:128], [b_kvs], [b_V])
                sq3 = r3(sq[:, 0:512], 8, 64)
                TT(P, 'dve', sq3, kv4[:, :, 0:64], kv4[:, :, 0:64], ALU.mult, [b_kvs], [b_sq])
                RED(P, st[:, 8:16], sq3, ALU.add, [b_sq], [b_st])
                TTR(P, sq[:, 512:544], tmz[:, 128:160], tmz[:, 128:160], st[:, 2:3], [b_tmz], [b_sq, b_st])
                TS(P, 'dve', st[:, 8:16], st[:, 8:16], st[:, 2:3], None, ALU.add, None, [b_st], [b_st])
                RSTD(P, st[:, 8:16], st[:, 8:16], 96, [b_st], [b_st])
                ktm3 = r3(ktm, 8, 96)
                TT(P, 'dve', sq3, kv4[:, :, 0:64], st[:, 8:16].unsqueeze(2).to_broadcast([128, 8, 64]), ALU.mult,
                   [b_kvs, b_st], [b_sq])
                TT(P, 'dve', ktm3[:, :, 0:64], sq3, gkb[:, 0:64].unsqueeze(1).to_broadcast([128, 8, 64]), ALU.mult,
                   [b_sq] + gvec, [b_ktm])
                TT(P, 'dve', krr, tmz[:, 128:160], gkb[:, 64:96], ALU.mult, [b_tmz] + gvec, [b_krr])
                if not is_ctx:
                    P.dma(rc, I['ropeC'][t0 - CT:t0 - CT + 128, :], b_rope, True)
                    P.dma(rs, I['ropeS'][t0 - CT:t0 - CT + 128, :], b_rope, True)
                    kr4 = r3(krr, 4, 8)
                    rt4 = r3(rtmp[:, 0:32], 4, 8)
                    for hb in range(2):
                        for blk in range(2):
                            TT(P, 'dve', rt4[:, hb * 2 + blk, :], kr4[:, hb * 2 + (1 - blk), :],
                               r3(rs, 4, 8)[:, hb * 2 + blk, :], ALU.mult, [b_krr, b_rope], [b_rtmp])
                    TT(P, 'dve', krr, krr, rc, ALU.mult, [b_krr, b_rope], [b_krr])
                    TT(P, 'dve', krr, krr, rtmp[:, 0:32], ALU.add, [b_krr, b_rtmp], [b_krr])
                TT(P, 'dve', ktm3[:, :, 64:96], krr.unsqueeze(1).to_broadcast([128, 8, 32]),
                   st[:, 8:16].unsqueeze(2).to_broadcast([128, 8, 32]), ALU.mult, [b_krr, b_st], [b_ktm])
                pk = r3(P.bank(0).bitcast(BF16), 8, 128)
                for h in range(8):
                    TR(P, pk[0:96, h, :], ktm3[:, h, :], identB, [b_ktm, b_idB], [PB[0]])
                CP(P, 'act', KT3[0:96, :, t0:t0 + 128], pk[0:96, :, :], [PB[0]], [b_KT])
                for nh in range(2):
                    CP(P, 'act', qs[:, nh * 384:(nh + 1) * 384], pq[:, nh * 512:nh * 512 + 384], [PB[6], PB[7]], [b_qs])
                q3 = r3(qs, 8, 96)
                s3 = r3(sq, 8, 96)
                TT(P, 'dve', s3, q3, q3, ALU.mult, [b_qs], [b_sq])
                RED(P, st[:, 16:24], s3, ALU.add, [b_sq], [b_st])
                RSTD(P, st[:, 16:24], st[:, 16:24], 96, [b_st], [b_st])
                TT(P, 'dve', s3, q3, st[:, 16:24].unsqueeze(2).to_broadcast([128, 8, 96]), ALU.mult, [b_qs, b_st], [b_sq])
                TT(P, 'dve', q3, s3, gqb.unsqueeze(1).to_broadcast([128, 8, 96]), ALU.mult, [b_sq] + gvec, [b_qs])
                qtm3 = r3(qtm, 8, 96)
                CP(P, 'pool', qtm3[:, :, 0:64], q3[:, :, 0:64], [b_qs], [b_qtm])
                if not is_ctx:
                    q5 = qs.rearrange('p (h c) -> p h c', h=8, c=96)[:, :, 64:96].rearrange('p h (a b) -> p h a b', a=4, b=8)
                    rt5 = rtmp.rearrange('p (h a b) -> p h a b', h=8, a=4, b=8)
                    rs4 = r3(rs, 4, 8)
                    for hb in range(2):
                        for blk in range(2):
                            TT(P, 'dve', rt5[:, :, hb * 2 + blk, :], q5[:, :, hb * 2 + (1 - blk), :],
                               rs4[:, hb * 2 + blk, :].unsqueeze(1).to_broadcast([128, 8, 8]), ALU.mult,
                               [b_qs, b_rope], [b_rtmp])
                    qr = q3[:, :, 64:96]
                    TT(P, 'dve', s3[:, :, 0:32], qr, rc.unsqueeze(1).to_broadcast([128, 8, 32]), ALU.mult,
                       [b_qs, b_rope], [b_sq])
                    TT(P, 'dve', qtm3[:, :, 64:96], s3[:, :, 0:32], r3(rtmp, 8, 32), ALU.add, [b_sq, b_rtmp], [b_qtm])
                else:
                    CP(P, 'pool', qtm3[:, :, 64:96], q3[:, :, 64:96], [b_qs], [b_qtm])
                pqT = r3(P.bank(4).bitcast(BF16), 8, 128)
                for h in range(8):
                    TR(P, pqT[0:96, h, :], qtm3[:, h, :], identB, [b_qtm, b_idB], [PB[4]])
                qq, bqq = qTt[tt], b_qTt[tt]
                CP(P, 'act', r3(qq, 8, 128)[0:96, :, :], pqT[0:96, :, :], [PB[4]], [bqq])
                P.dma(S['QT'][:, :, t0:t0 + 128].rearrange('h p t -> p h t'), r3(qq, 8, 128)[0:96, :, :], bqq, False)
        if debug:
            kd = nc.dram_tensor('KTd%d' % l, [128, 8 * NT], BF16, kind='ExternalOutput').ap()
            vd = nc.dram_tensor('Vd%d' % l, [128, 34 * 8 * 65], BF16, kind='ExternalOutput').ap()
            P.dma(kd, KT3.rearrange('p a b -> p (a b)'), b_KT, False)
            P.dma(vd, V4.rearrange('p a b c -> p (a b c)'), b_V, False)
        P.barrier()
        P.release()

    def phase_B(l):
        KT3, V4, b_KT, b_V = AT['KT3'], AT['V4'], AT['b_KT'], AT['b_V']
        P.mark()
        qt = [P.alloc(8 * 512, BF16) for _ in range(2)]
        b_qt = [Buf(), Buf()]
        E = [P.alloc(512, BF16) for _ in range(3)]
        b_E = [Buf() for _ in range(3)]
        atm = P.alloc(4 * 512)
        b_atm = Buf()
        atb = P.alloc(4 * 512, BF16)
        b_atb = Buf()
        rcp = P.alloc(4)
        b_rcp = Buf()
        aTt = [P.alloc(4 * 512, BF16) for _ in range(2)]
        b_aTt = [Buf(), Buf()]
        scale = 96.0 ** -0.5
        tgen = gen_T(l, 3)
        blocks = [(CT + i * 512, 512, list(range(34))) for i in range(8)]
        if l < L - 1:
            blocks.append((0, 256, [0, 1]))
        ne = 0
        for bi, (q0, nq, kts) in enumerate(blocks):
            nqi = nq // 128
            q_t, bq = qt[bi % 2], b_qt[bi % 2]
            q3 = r3(q_t, 8, 512)
            P.dma(q3[0:96, :, 0:nq], S['QT'][:, :, q0:q0 + nq].rearrange('h p t -> p h t'), bq, True)
            atm3 = r3(atm, 4, 512)
            steps = [(h, ki, kt) for h in range(8) for ki, kt in enumerate(kts)]

            def qk(i):
                h, ki, kt = steps[i]
                sb = (ne + i) % 3
                MM(P, P.bank(sb)[:, 0:nq], KT3[0:96, h, kt * 128:(kt + 1) * 128], q3[0:96, h, 0:nq], True, True,
                   [b_KT, bq], [PB[sb]])

            qk(0)
            for i, (h, ki, kt) in enumerate(steps):
                if i + 1 < len(steps):
                    qk(i + 1)
                if i % 3 == 0:
                    next(tgen, None)
                sb = (ne + i) % 3
                accb = 4 + (h % 2)
                acc = r3(P.bank(accb)[:, 0:4 * 65], 4, 65)
                e_t, be = E[sb], b_E[sb]
                ACT(P, e_t[:, 0:nq], P.bank(sb)[:, 0:nq], AF.Exp, [PB[sb]], [be], scale=scale)
                for qi in range(nqi):
                    MM(P, acc[:, qi, :], e_t[:, qi * 128:(qi + 1) * 128], V4[:, kt, h, :], ki == 0 and qi == 0,
                       ki == len(kts) - 1, [be, b_V], [PB[accb]], skip=True)
                if ki == len(kts) - 1:
                    P.op('dve', lambda e, acc=acc, nqi=nqi: e.reciprocal(out=rcp[:, 0:nqi], in_=acc[:, 0:nqi, 64]),
                         [PB[accb]], [b_rcp])
                    TT(P, 'dve', atm3[:, 0:nqi, h * 64:(h + 1) * 64], acc[:, 0:nqi, 0:64],
                       rcp[:, 0:nqi].unsqueeze(2).to_broadcast([128, nqi, 64]), ALU.mult, [PB[accb], b_rcp], [b_atm])
            ne += len(steps)
            CP(P, 'pool', atb, atm, [b_atm], [b_atb])
            atb3 = r3(atb, 4, 512)
            a_t, ba = aTt[bi % 2], b_aTt[bi % 2]
            a3 = r3(a_t, 4, 512)
            for qi in range(nqi):
                pt = r3(P.bank(6 + (qi % 2)).bitcast(BF16)[:, 0:512], 4, 128)
                for c in range(4):
                    TR(P, pt[:, c, :], atb3[:, qi, c * 128:(c + 1) * 128], identB, [b_atb, b_idB], [PB[6 + (qi % 2)]])
                CP(P, 'act', a3[:, :, qi * 128:(qi + 1) * 128], pt, [PB[6 + (qi % 2)]], [ba])
            P.dma(S['aT'][:, :, q0:q0 + nq].rearrange('c p t -> p c t'), a3[:, :, 0:nq], ba, False)
        for _ in tgen:
            pass
        P.barrier()
        P.release()
        P.release()

    def phase_C1(l):
        P.mark()
        Z = P.alloc(32 * 512, BF16)
        bZ = Buf()
        Z3 = r3(Z, 32, 512)
        P.dma(Z3, S['ZCS'][CT:NT, :].rearrange('(tt p) n -> p tt n', p=128), bZ, True)
        slab = [P.alloc(8 * 512, BF16) for _ in range(4)]
        b_slab = [Buf() for _ in range(4)]
        fo = [P.alloc(2 * 512, BF16) for _ in range(2)]
        b_fo = [Buf(), Buf()]
        ns = 0
        for kb in range(8):
            for cs in range(2):
                src = I['dftP'][cs].rearrange('(tt p) k -> p tt k', p=128)
                for tg in range(4):
                    sl, bs = slab[ns % 4], b_slab[ns % 4]
                    sl3 = r3(sl, 8, 512)
                    P.dma(sl3, src[:, tg * 8:(tg + 1) * 8, kb * 512:(kb + 1) * 512], bs, True)
                    for t8 in range(8):
                        tt = tg * 8 + t8
                        first = (cs == 0 and tt == 0)
                        last = (cs == 1 and tt == 31)
                        for mc in range(2):
                            MM(P, P.bank(mc), Z3[:, tt, mc * 256 + cs * 128: mc * 256 + cs * 128 + 128], sl3[:, t8, :],
                               first, last, [bZ, bs], [PB[mc]])
                    ns += 1
            f_t, bf = fo[kb % 2], b_fo[kb % 2]
            f3 = r3(f_t, 2, 512)
            CP(P, 'act', f3[:, 0, :], P.bank(0), [PB[0]], [bf])
            CP(P, 'dve', f3[:, 1, :], P.bank(1), [PB[1]], [bf])
            P.dma(S['fT'][:, :, CT + kb * 512:CT + (kb + 1) * 512].rearrange('c p t -> p c t'), f3, bf, False)
        if l < L - 1:
            Zc = P.alloc(2 * 512, BF16)
            bZc = Buf()
            Zc3 = r3(Zc, 2, 512)
            P.dma(Zc3, S['ZCS'][0:CT, :].rearrange('(tt p) n -> p tt n', p=128), bZc, True)
            dc = P.alloc(2 * 2 * 256, BF16)
            bdc = Buf()
            dc4 = r4(dc, 2, 2, 256)
            for cs in range(2):
                P.dma(dc4[:, cs, :, :], I['dftC'][cs].rearrange('(tt p) k -> p tt k', p=128), bdc, True)
            for mc in range(2):
                n = 0
                for cs in range(2):
                    for tt in range(2):
                        MM(P, P.bank(2 + mc)[:, 0:256], Zc3[:, tt, mc * 256 + cs * 128: mc * 256 + cs * 128 + 128],
                           dc4[:, cs, tt, :], n == 0, n == 3, [bZc, bdc], [PB[2 + mc]])
                        n += 1
            f_t, bf = fo[0], b_fo[0]
            f3 = r3(f_t, 2, 512)
            CP(P, 'act', f3[:, 0, 0:256], P.bank(2)[:, 0:256], [PB[2]], [bf])
            CP(P, 'dve', f3[:, 1, 0:256], P.bank(3)[:, 0:256], [PB[3]], [bf])
            P.dma(S['fT'][:, :, 0:CT].rearrange('c p t -> p c t'), f3[:, :, 0:256], bf, False)
        P.barrier()
        P.release()

    def phase_C2(l):
        P.mark()
        LB = 15 + CT + 15 + T + 15
        U = P.alloc(2 * LB)
        bU = Buf()
        U3 = r3(U, 2, LB)
        MEMSET(P, 'pool', U, 0.0, [bU])
        OC, OX = 15, 15 + CT + 15
        for c in range(2):
            P.dma(U3[:, c, OC:OC + CT], S['UT'][c, :, 0:CT], bU, True)
            P.dma(U3[:, c, OX:OX + T], S['UT'][c, :, CT:NT], bU, True)
        wd = P.alloc(62)
        bwd = Buf()
        P.dma(wd, I['wdwT'][l], bwd, True)
        wd3 = r3(wd, 2, 31)
        cp = P.alloc(6)
        bcp = Buf()
        P.dma(cp, I['cvp'][l], bcp, True)
        cp3 = r3(cp, 2, 3)
        A = P.alloc(2 * LB)
        bA = Buf()
        A3 = r3(A, 2, LB)
        NV = LB - 30
        for c in range(2):
            TS(P, 'dve', A3[:, c, 15:15 + NV], U3[:, c, 0:NV], wd3[:, c, 0:1], cp3[:, c, 0:1], ALU.mult, ALU.add,
               [bU, bwd, bcp], [bA])
            for w in range(1, 31):
                STT(P, A3[:, c, 15:15 + NV], U3[:, c, w:w + NV], wd3[:, c, w:w + 1], A3[:, c, 15:15 + NV], ALU.mult, ALU.add,
                    [bU, bwd, bA], [bA])
        sqt = P.alloc(2 * 512)
        bsq = Buf()
        sq3 = r3(sqt, 2, 512)
        mean = P.alloc(512)
        bmean = Buf()
        var = P.alloc(512)
        bvar = Buf()
        y = P.alloc(512)
        by = Buf()
        co = [P.alloc(2 * 512, BF16) for _ in range(2)]
        bco = [Buf(), Buf()]
        blocks = [(OX + i * 512, CT + i * 512, 512) for i in range(8)]
        if l < L - 1:
            blocks.append((OC, 0, 256))
        for bi, (o, t0, n) in enumerate(blocks):
            for c in range(2):
                ACT(P, sq3[:, c, 0:n], A3[:, c, o:o + n], AF.Square, [bA], [bsq])
            for c in range(2):
                MM(P, P.bank(0)[:, 0:n], ones256, A3[:, c, o:o + n], c == 0, c == 1, [b_ones, bA], [PB[0]])
            for c in range(2):
                MM(P, P.bank(1)[:, 0:n], ones256, sq3[:, c, 0:n], c == 0, c == 1, [b_ones, bsq], [PB[1]])
            CP(P, 'act', mean[:, 0:n], P.bank(0)[:, 0:n], [PB[0]], [bmean])
            TT(P, 'dve', var[:, 0:n], mean[:, 0:n], mean[:, 0:n], ALU.mult, [bmean], [bvar])
            TT(P, 'dve', var[:, 0:n], P.bank(1)[:, 0:n], var[:, 0:n], ALU.subtract, [PB[1], bvar], [bvar])
            TS(P, 'dve', var[:, 0:n], var[:, 0:n], EPS, None, ALU.add, None, [bvar], [bvar])
            ACT(P, var[:, 0:n], var[:, 0:n], AF.Sqrt, [bvar], [bvar])
            P.op('dve', lambda e, n=n: e.reciprocal(out=var[:, 0:n], in_=var[:, 0:n]), [bvar], [bvar])
            c_t, bc = co[bi % 2], bco[bi % 2]
            c3 = r3(c_t, 2, 512)
            for c in range(2):
                TT(P, 'dve', y[:, 0:n], A3[:, c, o:o + n], mean[:, 0:n], ALU.subtract, [bA, bmean], [by])
                TT(P, 'dve', y[:, 0:n], y[:, 0:n], var[:, 0:n], ALU.mult, [by, bvar], [by])
                ACT(P, c3[:, c, 0:n], y[:, 0:n], AF.Silu, [by, bcp], [bc], scale=cp3[:, c, 1:2], bias=cp3[:, c, 2:3])
            P.dma(S['cvT'][:, :, t0:t0 + n].rearrange('c p t -> p c t'), c3[:, :, 0:n], bc, False)
        P.barrier()
        P.release()

    def phase_D(l):
        P.mark()
        win = I['w_in'][l].rearrange('(k p) n -> p k n', p=128)
        stg = [P.alloc(1024) for _ in range(3)]
        bstg = [Buf() for _ in range(3)]
        wg = P.alloc(8 * 3072, BF16)
        b_wg = Buf()
        wg3 = r3(wg, 8, 3072)
        for part in range(3):
            load_w_bf16(wg3[:, :, part * 1024:(part + 1) * 1024], b_wg, win[:, :, 1184 + part * 1024:1184 + (part + 1) * 1024],
                        1024, stg, bstg, 8)
        wba = P.alloc(4 * 1024, BF16)
        wbf = P.alloc(2 * 1024, BF16)
        wbc = P.alloc(2 * 1024, BF16)
        wo = P.alloc(8 * 1024, BF16)
        b_wb = Buf()
        load_w_bf16(r3(wba, 4, 1024), b_wb, I['wb_attn'][l].rearrange('(k p) n -> p k n', p=128), 1024, stg, bstg, 4)
        load_w_bf16(r3(wbf, 2, 1024), b_wb, I['wb_fnet'][l].rearrange('(k p) n -> p k n', p=128), 1024, stg, bstg, 2)
        load_w_bf16(r3(wbc, 2, 1024), b_wb, I['wb_conv'][l].rearrange('(k p) n -> p k n', p=128), 1024, stg, bstg, 2)
        load_w_bf16(r3(wo, 8, 1024), b_wb, I['w_out'][l].rearrange('(k p) n -> p k n', p=128), 1024, stg, bstg, 8)
        wbr = [(r3(wba, 4, 1024), 4), (r3(wbf, 2, 1024), 2), (r3(wbc, 2, 1024), 2)]
        wo3 = r3(wo, 8, 1024)
        ht = HT()
        sgt = [P.alloc(24 * TB, BF16) for _ in range(2)]
        b_sgt = [Buf(), Buf()]
        br = [P.alloc(8 * TB, BF16) for _ in range(2)]
        b_br = [Buf(), Buf()]
        mT = P.alloc(8 * TB, BF16)
        b_mT = Buf()
        mT3 = r3(mT, 8, TB)
        t1 = P.alloc(TB)
        t2 = P.alloc(TB)
        b_t1, b_t2 = Buf(), Buf()
        xo = [P.alloc(1024) for _ in range(2)]
        b_xo = [Buf(), Buf()]
        blocks = [tb for tb in range(NBLK) if not (tb == 0 and l == L - 1)]
        st_ = {}

        def front(n):
            tb = blocks[n]
            hT3, bh, xs, bxs = ht.make(l, tb, 0, tbank=0)
            b_t, bb = br[n % 2], b_br[n % 2]
            b3 = r3(b_t, 8, TB)
            tsl = slice(tb * TB, (tb + 1) * TB)
            P.dma(b3[:, 0:4, :], S['aT'][:, :, tsl].rearrange('c p t -> p c t'), bb, True)
            P.dma(b3[:, 4:6, :], S['fT'][:, :, tsl].rearrange('c p t -> p c t'), bb, True)
            P.dma(b3[:, 6:8, :], S['cvT'][:, :, tsl].rearrange('c p t -> p c t'), bb, True)
            sg3 = r3(sgt[n % 2], 24, TB)
            bsg = b_sgt[n % 2]
            for gc in range(24):
                bk = 1 + (gc % 2)
                ps = P.bank(bk)[:, 0:TB]
                for k in range(8):
                    MM(P, ps, wg3[:, k, gc * 128:(gc + 1) * 128], hT3[:, k, :], k == 0, k == 7, [b_wg, bh], [PB[bk]])
                ACT(P, sg3[:, gc, :], ps, AF.Sigmoid, [PB[bk]], [bsg])
            st_[n] = (xs, bxs, b3, bb, sg3, bsg)

        def back_a(n):
            xs, bxs, b3, bb, sg3, bsg = st_[n]
            for oc in range(8):
                koff = 0
                for bi, (w3, nk) in enumerate(wbr):
                    ps = P.bank(3 + bi)[:, 0:TB]
                    for kc in range(nk):
                        MM(P, ps, w3[:, kc, oc * 128:(oc + 1) * 128], b3[:, koff + kc, :], kc == 0, kc == nk - 1, [b_wb, bb], [PB[3 + bi]])
                    koff += nk
                TT(P, 'dve', t1, P.bank(3)[:, 0:TB], sg3[:, oc, :], ALU.mult, [PB[3], bsg], [b_t1])
                TT(P, 'dve', t2, P.bank(4)[:, 0:TB], sg3[:, 8 + oc, :], ALU.mult, [PB[4], bsg], [b_t2])
                TT(P, 'dve', t1, t1, t2, ALU.add, [b_t1, b_t2], [b_t1])
                TT(P, 'dve', t2, P.bank(5)[:, 0:TB], sg3[:, 16 + oc, :], ALU.mult, [PB[5], bsg], [b_t2])
                TT(P, 'dve', mT3[:, oc, :], t1, t2, ALU.add, [b_t1, b_t2], [b_mT])

        def back_b(n):
            tb = blocks[n]
            mi = 1 if tb == 0 else 0
            xs, bxs, b3, bb, sg3, bsg = st_.pop(n)
            for tt in range(2):
                x_o, bxo = xo[tt], b_xo[tt]
                for nh in range(2):
                    ps = P.bank(6 + nh)
                    for k in range(8):
                        MM(P, ps, mT3[:, k, tt * 128:(tt + 1) * 128], wo3[:, k, nh * 512:(nh + 1) * 512], k == 0, k == 7,
                           [b_mT, b_wb], [PB[6 + nh]])
                    TT(P, 'dve', x_o[:, nh * 512:(nh + 1) * 512], ps, gtb[mi][:, nh * 512:(nh + 1) * 512], ALU.mult,
                       [PB[6 + nh], b_gtb[mi]], [bxo])
                TT(P, 'pool', x_o, x_o, xs[tt], ALU.add, [bxo, bxs[tt]], [bxo])
                P.dma(S['xres'][tb * TB + tt * 128: tb * TB + (tt + 1) * 128, :], x_o, bxo, False)

        front(0)
        for n in range(len(blocks)):
            back_a(n)
            if n + 1 < len(blocks):
                front(n + 1)
            back_b(n)
        P.barrier()
        P.release()

    def gen_T(l, bank):
        us = [P.alloc(1024) for _ in range(2)]
        b_us = [Buf(), Buf()]
        vsl = [P.alloc(1024) for _ in range(2)]
        b_vs = [Buf(), Buf()]
        ub = [P.alloc(1024, BF16) for _ in range(2)]
        b_ub = [Buf(), Buf()]
        uo = [P.alloc(1024, BF16) for _ in range(2)]
        b_uo = [Buf(), Buf()]
        vb = [P.alloc(1024, BF16) for _ in range(2)]
        b_vb = [Buf(), Buf()]
        usrc = I['u_tab'][l].rearrange('(i j) d -> j i d', j=NCH)
        vsrc = I['v_tab'][l].rearrange('(i j) d -> j i d', j=NCH)
        for j in range(NCH):
            s = j % 2
            P.dma(us[s], usrc[j], b_us[s], True)
            P.dma(vsl[s], vsrc[j], b_vs[s], True)
            yield
            yield
            CP(P, 'pool', vb[s], vsl[s], [b_vs[s]], [b_vb[s]])
            CP(P, 'dve', ub[s], us[s], [b_us[s]], [b_ub[s]])
            yield
            P.dma(S['vs'][j], vb[s], b_vb[s], False)
            pt = r3(P.bank(bank).bitcast(BF16), 8, 128)
            for k in range(8):
                TR(P, pt[:, k, :], ub[s][:, k * 128:(k + 1) * 128], identB, [b_ub[s], b_idB], [PB[bank]])
            yield
            CP(P, 'dve', uo[s], P.bank(bank).bitcast(BF16), [PB[bank]], [b_uo[s]])
            yield
            P.dma(S['uTs'][j], uo[s], b_uo[s], False)

    def phase_E(l):
        P.mark()
        last = (l == L - 1)
        wq = P.alloc(8 * 1024, BF16)
        b_wq = Buf()
        wq3 = r3(wq, 8, 1024)
        skb = P.alloc(8 * 256, BF16)
        b_sk = Buf()
        P.mark()
        stg = [P.alloc(1024) for _ in range(2)]
        bstg = [Buf(), Buf()]
        load_w_bf16(wq3, b_wq, I['w_query'][l].rearrange('(k p) n -> p k n', p=128), 1024, stg, bstg, 8)
        for hf in range(2):
            P.dma(stg[hf], I['skbd'][l][:, hf * 1024:(hf + 1) * 1024], bstg[hf], True)
            CP(P, 'dve', skb[:, hf * 1024:(hf + 1) * 1024], stg[hf], [bstg[hf]], [b_sk])
        P.barrier()
        P.release()
        skb3 = r3(skb, 8, 256)
        ht = HT(nx=4, nh=2)
        qT = P.alloc(8 * TB, BF16)
        b_qT = Buf()
        qT3 = r3(qT, 8, TB)
        sc = P.alloc(2048)
        b_sc = Buf()
        sc4 = r4(sc, 8, 2, 128)
        tmp = P.alloc(128)
        b_tmp = Buf()
        val = P.alloc(256)
        b_val = Buf()
        val4 = r4(val, 8, 2, 16)
        idx = P.alloc(256).bitcast(U32)
        b_idx = Buf()
        idx4 = r4(idx, 8, 2, 16)
        idxf = P.alloc(256)
        b_idxf = Buf()
        idxf4 = r4(idxf, 8, 2, 16)
        cand = P.alloc(2048)
        b_cand = Buf()
        cand4 = r4(cand, 8, 16, 16)
        ctmp = P.alloc(256)
        b_ctmp = Buf()
        best = P.alloc(128)
        b_best = Buf()
        best3 = r3(best, 8, 16)
        pos = P.alloc(128).bitcast(U32)
        b_pos = Buf()
        pos3 = r3(pos, 8, 16)
        pa = P.alloc(128).bitcast(U32)
        pb_ = P.alloc(128).bitcast(U32)
        paf = P.alloc(128)
        pbf = P.alloc(128)
        b_pab = Buf()
        oh = cand
        b_oh = b_cand
        oh4 = r4(oh, 8, 16, 16)
        tok3 = P.alloc(3 * 128)
        b_tok = Buf()
        tok33 = r3(tok3, 3, 128)
        gsum = P.alloc(8)
        b_gsum = Buf()
        ijw = [P.alloc(3 * TB, BF16) for _ in range(2)]
        b_ijwl = [Buf(), Buf()]
        GT = 16
        Aoh = [P.alloc(GT * 128, BF16) for _ in range(2)]
        Boh = [P.alloc(GT * 128, BF16) for _ in range(2)]
        b_A = [Buf(), Buf()]
        b_B = [Buf(), Buf()]
        G = P.alloc(TB * 128, BF16)
        b_G = Buf()
        G3 = r3(G, TB, 128)
        NS = 3
        utr = [P.alloc(1024, BF16) for _ in range(NS)]
        b_utr = [Buf() for _ in range(NS)]
        vtr = [P.alloc(1024, BF16) for _ in range(NS)]
        b_vtr = [Buf() for _ in range(NS)]
        gs = [P.alloc(TB, BF16) for _ in range(2)]
        b_gs = [Buf(), Buf()]
        av = [P.alloc(TB, BF16) for _ in range(2)]
        b_av = [Buf(), Buf()]
        xo = [P.alloc(1024) for _ in range(2)]
        b_xo = [Buf(), Buf()]
        iota128b = iota128h.unsqueeze(1).to_broadcast([128, GT, 128])
        state = {}

        def prep(tb, slot):
            ht.begin(l, tb, from_res=True)
            yield
            yield
            yield
            hT3, bh, xs, bxs = ht.finish(2, tbank=0)
            yield
            yield
            for h in range(8):
                ps = P.bank(1)[:, 0:TB]
                for k in range(8):
                    MM(P, ps, wq3[:, k, h * 128:(h + 1) * 128], hT3[:, k, :], k == 0, k == 7, [b_wq, bh], [PB[1]])
                yield
                CP(P, 'act', qT3[:, h, :], ps, [PB[1]], [b_qT])
            ijwT3 = r3(ijw[slot], 3, TB)
            b_ijw = b_ijwl[slot]
            for tt in range(2):
                tsl = slice(tt * 128, (tt + 1) * 128)
                yield
                for h in range(8):
                    ps = P.bank(1)[:, 0:256]
                    MM(P, ps, qT3[:, h, tsl], skb3[:, h, :], True, True, [b_qT, b_sk], [PB[1]])
                    yield
                    CP(P, 'act', sc[:, h * 256:(h + 1) * 256], ps, [PB[1]], [b_sc])
                yield
                for h in range(8):
                    for p in range(2):
                        s_hp = sc4[:, h, p, :]
                        v16 = val4[:, h, p, :]
                        i16 = idx4[:, h, p, :]
                        P.op('dve', lambda e, o=v16[:, 0:8], i=s_hp: e.max(out=o, in_=i), [b_sc], [b_val])
                        P.op('dve', lambda e, o=i16[:, 0:8], m=v16[:, 0:8], i=s_hp: e.max_index(out=o, in_max=m, in_values=i),
                             [b_sc, b_val], [b_idx])
                        P.op('dve', lambda e, o=tmp[:, 0:128], m=v16[:, 0:8], i=s_hp: e.match_replace(
                            out=o, in_to_replace=m, in_values=i, imm_value=-1e30), [b_sc, b_val], [b_tmp])
                        yield
                        P.op('dve', lambda e, o=v16[:, 8:16], i=tmp[:, 0:128]: e.max(out=o, in_=i), [b_tmp], [b_val])
                        P.op('dve', lambda e, o=i16[:, 8:16], m=v16[:, 8:16], i=tmp[:, 0:128]: e.max_index(
                            out=o, in_max=m, in_values=i), [b_tmp, b_val], [b_idx])
                        yield
                CP(P, 'dve', idxf, idx, [b_idx], [b_idxf])
                TT(P, 'dve', cand4, val4[:, :, 0, :].unsqueeze(3).to_broadcast([128, 8, 16, 16]),
                   val4[:, :, 1, :].unsqueeze(2).to_broadcast([128, 8, 16, 16]), ALU.add, [b_val], [b_cand])
                yield
                for h in range(8):
                    c_h = cand[:, h * 256:(h + 1) * 256]
                    b16 = best3[:, h, :]
                    p16 = pos3[:, h, :]
                    P.op('dve', lambda e, o=b16[:, 0:8], i=c_h: e.max(out=o, in_=i), [b_cand], [b_best])
                    P.op('dve', lambda e, o=p16[:, 0:8], m=b16[:, 0:8], i=c_h: e.max_index(out=o, in_max=m, in_values=i),
                         [b_cand, b_best], [b_pos])
                    P.op('dve', lambda e, o=ctmp, m=b16[:, 0:8], i=c_h: e.match_replace(
                        out=o, in_to_replace=m, in_values=i, imm_value=-1e30), [b_cand, b_best], [b_ctmp])
                    yield
                    P.op('dve', lambda e, o=b16[:, 8:16], i=ctmp: e.max(out=o, in_=i), [b_ctmp], [b_best])
                    P.op('dve', lambda e, o=p16[:, 8:16], m=b16[:, 8:16], i=ctmp: e.max_index(out=o, in_max=m, in_values=i),
                         [b_ctmp, b_best], [b_pos])
                    yield
                P.op('dve', lambda e: e.tensor_single_scalar(out=pa, in_=pos, scalar=4, op=ALU.logical_shift_right),
                     [b_pos], [b_pab])
                P.op('dve', lambda e: e.tensor_single_scalar(out=pb_, in_=pos, scalar=15, op=ALU.bitwise_and),
                     [b_pos], [b_pab])
                yield
                CP(P, 'dve', paf, pa, [b_pab], [b_pab])
                CP(P, 'dve', pbf, pb_, [b_pab], [b_pab])
                yield
                for which, pf in enumerate((paf, pbf)):
                    pf3 = r3(pf, 8, 16)
                    TT(P, 'dve', oh4, iota16.unsqueeze(1).unsqueeze(1).to_broadcast([128, 8, 16, 16]),
                       pf3.unsqueeze(3).to_broadcast([128, 8, 16, 16]), ALU.is_equal, [b_iota, b_pab], [b_oh])
                    yield
                    TT(P, 'dve', oh4, oh4, idxf4[:, :, which, :].unsqueeze(2).to_broadcast([128, 8, 16, 16]), ALU.mult,
                       [b_oh, b_idxf], [b_oh])
                    yield
                    RED(P, tok33[:, which, :], r3(oh, 128, 16), ALU.add, [b_oh], [b_tok])
                    yield
                w3 = r3(tok33[:, 2, :], 8, 16)
                TT(P, 'dve', w3, best3, best3[:, :, 0:1].to_broadcast([128, 8, 16]), ALU.subtract, [b_best], [b_tok])
                yield
                ACT(P, tok33[:, 2, :], tok33[:, 2, :], AF.Exp, [b_tok], [b_tok])
                yield
                RED(P, gsum, w3, ALU.add, [b_tok], [b_gsum])
                P.op('dve', lambda e: e.reciprocal(out=gsum, in_=gsum), [b_gsum], [b_gsum])
                TT(P, 'dve', w3, w3, gsum.unsqueeze(2).to_broadcast([128, 8, 16]), ALU.mult, [b_tok, b_gsum], [b_tok])
                yield
                yield
                pT = r3(P.bank(1)[:, 0:384], 3, 128)
                for c in range(3):
                    TR(P, pT[:, c, :], tok33[:, c, :], identF, [b_tok, b_idF], [PB[1]])
                yield
                CP(P, 'act', ijwT3[:, :, tsl], pT, [PB[1]], [b_ijw])
            state[tb] = (hT3, bh, xs, bxs, ijwT3, b_ijw)

        blocks = [tb for tb in range(NBLK) if not (tb == 0 and last)]
        for _ in prep(blocks[0], 0):
            pass
        nchunk = 0
        for bi, tb in enumerate(blocks):
            mi = 1 if tb == 0 else 0
            hT3, bh, xs, bxs, ijwT3, b_ijw = state.pop(tb)
            for g in range(TB // GT):
                A_t, bA = Aoh[g % 2], b_A[g % 2]
                B_t, bB = Boh[g % 2], b_B[g % 2]
                A3, B3 = r3(A_t, GT, 128), r3(B_t, GT, 128)
                gsl = slice(g * GT, (g + 1) * GT)
                TT(P, 'dve', A3, iota128b, ijwT3[:, 0, gsl].unsqueeze(2).to_broadcast([128, GT, 128]), ALU.is_equal,
                   [b_iota, b_ijw], [bA])
                TT(P, 'dve', B3, iota128b, ijwT3[:, 1, gsl].unsqueeze(2).to_broadcast([128, GT, 128]), ALU.is_equal,
                   [b_iota, b_ijw], [bB])
                TT(P, 'pool', B3, B3, ijwT3[:, 2, gsl].unsqueeze(2).to_broadcast([128, GT, 128]), ALU.mult,
                   [bB, b_ijw], [bB])
                for t4 in range(GT // 4):
                    bk = t4 % 2
                    for ti in range(4):
                        tloc = t4 * 4 + ti
                        MM(P, P.bank(bk)[:, ti * 128:(ti + 1) * 128], A3[:, tloc, :], B3[:, tloc, :], True, True, [bA, bB], [PB[bk]])
                    tg0 = g * GT + t4 * 4
                    CP(P, 'act', G[:, tg0 * 128:(tg0 + 4) * 128], P.bank(bk), [PB[bk]], [b_G])
            nxt = prep(blocks[bi + 1], (bi + 1) % 2) if bi + 1 < len(blocks) else None

            def load(j):
                s_ = (nchunk + j) % NS
                P.dma(utr[s_], S['uTs'][j], b_utr[s_], True, q='sp')
                P.dma(vtr[s_], S['vs'][j], b_vtr[s_], True, q='sp')

            def mm1(j):
                s_ = (nchunk + j) % NS
                u3 = r3(utr[s_], 8, 128)
                sb = 2 + ((nchunk + j) % 2)
                for k in range(8):
                    MM(P, P.bank(sb)[:, 0:TB], u3[:, k, :], hT3[:, k, :], k == 0, k == 7, [b_utr[s_], bh], [PB[sb]])

            for j in range(min(NS, NCH)):
                load(j)
            mm1(0)
            for j in range(NCH):
                if j + 1 < NCH:
                    mm1(j + 1)
                s_ = (nchunk + j) % NS
                sb = 2 + ((nchunk + j) % 2)
                g_t, bg = gs[(nchunk + j) % 2], b_gs[(nchunk + j) % 2]
                ACT(P, g_t, P.bank(sb)[:, 0:TB], AF.Gelu_apprx_tanh, [PB[sb]], [bg])
                a_t, ba = av[(nchunk + j) % 2], b_av[(nchunk + j) % 2]
                TT(P, 'pool', a_t, g_t, G3[:, :, j], ALU.mult, [bg, b_G], [ba])
                for tt in range(2):
                    for nh in range(2):
                        ob = 4 + tt * 2 + nh
                        MM(P, P.bank(ob), a_t[:, tt * 128:(tt + 1) * 128], vtr[s_][:, nh * 512:(nh + 1) * 512],
                           j == 0, j == NCH - 1, [ba, b_vtr[s_]], [PB[ob]])
                if j + NS < NCH:
                    load(j + NS)
                if nxt is not None:
                    next(nxt, None)
                    if j % 4 == 0:
                        next(nxt, None)
            nchunk += NCH
            if nxt is not None:
                for _ in nxt:
                    pass
            for tt in range(2):
                x_o, bxo = xo[tt], b_xo[tt]
                for nh in range(2):
                    ob = 4 + tt * 2 + nh
                    TT(P, 'dve', x_o[:, nh * 512:(nh + 1) * 512], P.bank(ob), gtb[2 + mi][:, nh * 512:(nh + 1) * 512], ALU.mult,
                       [PB[ob], b_gtb[2 + mi]], [bxo])
                TT(P, 'pool', x_o, x_o, xs[tt], ALU.add, [bxo, bxs[tt]], [bxo])
                r0 = tb * TB + tt * 128
                if last:
                    P.dma(out_d[r0 - CT:r0 - CT + 128, :], x_o, bxo, False)
                else:
                    P.dma(S['xres'][r0:r0 + 128, :], x_o, bxo, False)
        print('phase E sbuf words', P.sb_off, 'of', P.sb_words)
        P.barrier()
        P.release()

    phases = []
    for l in range(n_layers):
        phases += [('adaln', phase_adaln, l), ('A', phase_A, l), ('B', phase_B, l), ('C1', phase_C1, l),
                   ('C2', phase_C2, l), ('D', phase_D, l), ('E', phase_E, l)]
    for name, fn, l in phases:
        fn(l)
        if stop_after is not None and stop_after == (name, l):
            break
    P.emit()
    return nc


def _consts():
    t = np.arange(T, dtype=np.float64)
    row = np.repeat(np.arange(T // 64, dtype=np.float32), 64)
    col = np.tile(np.arange(64, dtype=np.float32), T // 64)
    inv = (np.float32(10000.0) ** (-np.arange(0, 16, 2, dtype=np.float32) / np.float32(16))).astype(np.float32)
    ar = row[:, None] * inv
    ac = col[:, None] * inv
    ang = np.concatenate([ar, ar, ac, ac], axis=-1).astype(np.float32)
    ropeC = np.cos(ang).astype(np.float32)
    sn = np.sin(ang).astype(np.float32)
    sign = np.tile(np.concatenate([-np.ones(8, np.float32), np.ones(8, np.float32)]), 2)
    ropeS = (sn * sign[None, :]).astype(np.float32)
    n = (np.outer(np.arange(T), np.arange(T)) % T).astype(np.float64)
    dftP = np.stack([np.cos(2 * np.pi * n / T) / 64.0, -np.sin(2 * np.pi * n / T) / 64.0]).astype(ml_dtypes.bfloat16)
    n2 = (np.outer(np.arange(CT), np.arange(CT)) % CT).astype(np.float64)
    dftC = np.stack([np.cos(2 * np.pi * n2 / CT) / 16.0, -np.sin(2 * np.pi * n2 / CT) / 16.0]).astype(ml_dtypes.bfloat16)
    c = np.arange(128)
    m = np.arange(128)
    same = (c[:, None] // 64) == (m[None, :] // 64)
    ph = 2 * np.pi * ((c[:, None] % 64) * (m[None, :] % 64) % 64) / 64.0
    csbd = np.concatenate([np.where(same, np.cos(ph) / 8.0, 0.0), np.where(same, np.sin(ph) / 8.0, 0.0)], axis=1).astype(np.float32)
    ident = np.eye(128, dtype=np.float32)
    sel = np.zeros((2, 256), np.float32)
    sel[0, 0:128] = 1.0
    sel[1, 128:256] = 1.0
    return dict(ropeC=ropeC, ropeS=ropeS, dftP=dftP, dftC=dftC, csbd=csbd, ident=ident, sel=sel)


_CONSTS = None
_NC_CACHE = {}


def make_in_maps(inp, cores):
    global _CONSTS
    if _CONSTS is None:
        _CONSTS = _consts()
    f = lambda a: np.ascontiguousarray(np.asarray(a, dtype=np.float32))
    shared = dict(_CONSTS)
    shared['w_mod'] = f(inp['w_mod'])
    bm = f(inp['b_mod'])
    shared['bmodT'] = np.ascontiguousarray(bm.reshape(L, 48, 128).transpose(0, 2, 1))
    shared['bmod2'] = np.ascontiguousarray(np.repeat(bm[:, None, :], 2, axis=1))
    shared['g1T'] = np.ascontiguousarray(f(inp['g_norm1']).reshape(L, 8, 128).transpose(0, 2, 1))
    shared['g2T'] = np.ascontiguousarray(f(inp['g_norm2']).reshape(L, 8, 128).transpose(0, 2, 1))
    shared['w_in'] = f(inp['w_in'])
    shared['g_ckv'] = f(inp['g_ckv']).reshape(L, 128, 1)
    shared['w_ukv'] = f(inp['w_ukv'])
    shared['g_cqT'] = np.ascontiguousarray(f(inp['g_cq']).reshape(L, 2, 128).transpose(0, 2, 1))
    shared['w_uq'] = f(inp['w_uq'])
    shared['g_qn'] = f(inp['g_qn']).reshape(L, 1, 96)
    shared['g_kn'] = f(inp['g_kn']).reshape(L, 1, 96)
    wdw = f(inp['w_dw']).reshape(L, 31, 2, 128)
    shared['wdwT'] = np.ascontiguousarray(wdw.transpose(0, 3, 2, 1)).reshape(L, 128, 62)
    cvp = np.stack([f(inp['b_dw']), f(inp['g_cln']), f(inp['b_cln'])], axis=-1)
    shared['cvp'] = np.ascontiguousarray(cvp.reshape(L, 2, 128, 3).transpose(0, 2, 1, 3)).reshape(L, 128, 6)
    for k in ('wb_attn', 'wb_fnet', 'wb_conv', 'w_out', 'w_query', 'u_tab', 'v_tab'):
        shared[k] = f(inp[k])
    sk = f(inp['sub_keys'])
    skbd = np.zeros((L, 128, 8, 256), np.float32)
    for p in range(2):
        skbd[:, p * 64:(p + 1) * 64, :, p * 128:(p + 1) * 128] = sk[:, :, p].transpose(0, 3, 1, 2)
    shared['skbd'] = skbd.reshape(L, 128, 2048)
    x = f(inp['x'])
    ctx = f(inp['ctx'])
    c = f(inp['c'])
    cc = f(inp['c_ctx'])
    maps = []
    for b in cores:
        m = dict(shared)
        m['x'] = x[b]
        m['ctx'] = ctx[b]
        cv = np.stack([c[b].reshape(8, 128).T, cc.reshape(8, 128).T], axis=-1)
        m['cvec'] = np.ascontiguousarray(cv).reshape(128, 16)
        maps.append(m)
    return maps


def kernel(**inputs):
    if 'full' not in _NC_CACHE:
        _NC_CACHE['full'] = build_program()
    nc = _NC_CACHE['full']
    maps = make_in_maps(inputs, list(range(8)))
    res = run_bass_kernel_spmd(nc, maps, core_ids=list(range(8)))
    return np.stack([np.asarray(r['out'], dtype=np.float32) for r in res.results], axis=0)
```

```python
import numpy as np
import ml_dtypes
import concourse.bass as bass
import concourse.mybir as mybir
from concourse.bass_utils import run_bass_kernel_spmd

F32 = mybir.dt.float32
BF16 = mybir.dt.bfloat16
U32 = mybir.dt.uint32
AF = mybir.ActivationFunctionType
ALU = mybir.AluOpType
AX = mybir.AxisListType

L = 2
D = 1024
T = 4096
CT = 256
NT = T + CT
TB = 256
NBLK = NT // TB
EPS = 1e-6
NCH = 128

CE = ('pe', 'act', 'dve', 'pool')
ENG = ('pe', 'act', 'dve', 'pool', 'sp')


class Buf:
    __slots__ = ('name', 'w', 'r', 'dkey', 'dcnt')

    def __init__(self, name=''):
        self.name = name
        self.w = None
        self.r = {}
        self.dkey = None
        self.dcnt = 0


class Prog:
    def __init__(self, nc, sbuf_words=207 * 256):
        self.nc = nc
        self.q = {e: [] for e in ENG}
        self.semh = {e: nc.alloc_semaphore('s_' + e) for e in CE}
        self.cnt = {e: 0 for e in CE}
        self.known = {e: {} for e in ENG}
        self.dtot = {}
        self.nd = 0
        self.sb = nc.alloc_sbuf_tensor('sb_all', [128, sbuf_words], F32)
        self.sb_words = sbuf_words
        self.sb_off = 0
        self.ps = nc.alloc_psum_tensor('ps_all', [128, 4096], F32)
        self.marks = []
        self.free_keys = []
        self.scope_keys = []
        self.pbuf = [Buf('bank%d' % i) for i in range(8)]

    def alloc(self, nelem, dtype=F32):
        nbytes = nelem * (2 if dtype == BF16 else 4)
        words = (nbytes + 3) // 4
        words = (words + 7) // 8 * 8
        assert self.sb_off + words <= self.sb_words, ('SBUF overflow', self.sb_off, words)
        ap = self.sb[:, self.sb_off:self.sb_off + words]
        self.sb_off += words
        if dtype != F32:
            ap = ap.bitcast(dtype)
        return ap[:, 0:nelem]

    def mark(self):
        self.marks.append((self.sb_off, len(self.scope_keys)))

    def release(self):
        self.sb_off, nk = self.marks.pop()
        self.free_keys += self.scope_keys[nk:]
        del self.scope_keys[nk:]

    def bank(self, b, n=1):
        return self.ps[:, b * 512:(b + n) * 512]

    def _wait(self, e, key, count):
        if key == e and e == 'pe':
            return
        if self.known[e].get(key, 0) >= count:
            return
        self.known[e][key] = count
        h = self.semh[key]
        self.q[e].append(lambda eng, h=h, c=count: eng.wait_ge(h, c))

    def _deps(self, e, r, w):
        for b in r:
            if b.w is not None:
                self._wait(e, *b.w)
        for b in w:
            if b.w is not None:
                self._wait(e, *b.w)
            for k, c in b.r.items():
                self._wait(e, k, c)

    def op(self, e, fn, r=(), w=()):
        self._deps(e, r, w)
        self.cnt[e] += 1
        c = self.cnt[e]
        h = self.semh[e]
        self.q[e].append(lambda eng, fn=fn, h=h: fn(eng).then_inc(h, 1))
        for b in r:
            b.r[e] = c
        for b in w:
            b.w = (e, c)
            b.r = {}

    def dma(self, out, in_, buf, load, q='sp', extra_r=(), **kw):
        if buf.dkey is None:
            if self.free_keys:
                buf.dkey = self.free_keys.pop()
            else:
                buf.dkey = 'd%d' % self.nd
                self.nd += 1
                self.semh[buf.dkey] = self.nc.alloc_semaphore(buf.dkey)
            if self.marks:
                self.scope_keys.append(buf.dkey)
            buf.dcnt = self.dtot.get(buf.dkey, 0)
        if load:
            self._deps(q, list(extra_r), [buf])
        else:
            self._deps(q, [buf] + list(extra_r), [])
        buf.dcnt += 16
        c = buf.dcnt
        self.dtot[buf.dkey] = c
        h = self.semh[buf.dkey]
        self.q[q].append(lambda eng, h=h, out=out, in_=in_, kw=kw: eng.dma_start(out=out, in_=in_, **kw).then_inc(h, 16))
        if load:
            buf.w = (buf.dkey, c)
            buf.r = {}
        else:
            buf.r[buf.dkey] = c

    def barrier(self, engines=ENG):
        for e in engines:
            for k in CE:
                if k != e and self.cnt[k] > 0:
                    self._wait(e, k, self.cnt[k])
            for k, c in self.dtot.items():
                self._wait(e, k, c)

    def emit(self):
        nc = self.nc
        self.barrier(ENG)
        with nc.Block() as block:
            @block.tensor
            def _(eng):
                for f in self.q['pe']:
                    f(eng)

            @block.scalar
            def _(eng):
                for f in self.q['act']:
                    f(eng)

            @block.vector
            def _(eng):
                for f in self.q['dve']:
                    f(eng)

            @block.gpsimd
            def _(eng):
                for f in self.q['pool']:
                    f(eng)

            @block.sync
            def _(eng):
                for f in self.q['sp']:
                    f(eng)


def MM(P, out, lhsT, rhs, start, stop, r, w, skip=False):
    P.op('pe', lambda e: e.matmul(out, lhsT=lhsT, rhs=rhs, start=start, stop=stop, skip_group_check=skip), r, w)


def TR(P, out, in_, ident, r, w):
    P.op('pe', lambda e: e.transpose(out=out, in_=in_, identity=ident), r, w)


def ACT(P, out, in_, func, r, w, scale=None, bias=None, accum=None):
    kw = {}
    if scale is not None:
        kw['scale'] = scale
    if bias is not None:
        kw['bias'] = bias
    if accum is not None:
        kw['accum_out'] = accum
    P.op('act', lambda e: e.activation(out=out, in_=in_, func=func, **kw), r, w)


def TT(P, eng, out, in0, in1, op, r, w):
    P.op(eng, lambda e: e.tensor_tensor(out=out, in0=in0, in1=in1, op=op), r, w)


def TS(P, eng, out, in0, s1, s2, op0, op1, r, w):
    if op1 is None:
        P.op(eng, lambda e: e.tensor_scalar(out=out, in0=in0, scalar1=s1, scalar2=None, op0=op0), r, w)
    else:
        P.op(eng, lambda e: e.tensor_scalar(out=out, in0=in0, scalar1=s1, scalar2=s2, op0=op0, op1=op1), r, w)


def STT(P, out, in0, scalar, in1, op0, op1, r, w):
    P.op('dve', lambda e: e.scalar_tensor_tensor(out=out, in0=in0, scalar=scalar, in1=in1, op0=op0, op1=op1), r, w)


def CP(P, eng, out, in_, r, w):
    if eng == 'act':
        P.op('act', lambda e: e.copy(out=out, in_=in_), r, w)
    else:
        P.op(eng, lambda e: e.tensor_copy(out=out, in_=in_), r, w)


def RED(P, out, in_, op, r, w):
    P.op('dve', lambda e: e.tensor_reduce(out=out, in_=in_, axis=AX.X, op=op), r, w)


def TTR(P, out, in0, in1, accum, r, w):
    P.op('act', lambda e: e.activation(out=out, in_=in0, func=AF.Square, accum_out=accum), r, w)


def MEMSET(P, eng, ap, val, w):
    P.op(eng, lambda e: e.memset(ap, val), (), w)


def RSTD(P, out, ss, n, r, w):
    TS(P, 'dve', out, ss, 1.0 / n, EPS, ALU.mult, ALU.add, r, w)
    ACT(P, out, out, AF.Sqrt, w, w)
    P.op('dve', lambda e: e.reciprocal(out=out, in_=out), w, w)


def r3(ap, a, b):
    return ap.rearrange('p (a b) -> p a b', a=a, b=b)


def r4(ap, a, b, c):
    return ap.rearrange('p (a b c) -> p a b c', a=a, b=b, c=c)


def build_program(debug=False, n_layers=L, stop_after=None):
    nc = bass.Bass('TRN2', target_bir_lowering=False)

    def din(name, shape, dt=F32):
        return nc.dram_tensor(name, list(shape), dt, kind='ExternalInput').ap()

    skind = 'ExternalOutput' if debug else 'Internal'

    def dsc(name, shape, dt):
        return nc.dram_tensor(name, list(shape), dt, kind=skind).ap()

    I = {}
    I['x'] = din('x', [T, D])
    I['ctx'] = din('ctx', [CT, D])
    I['cvec'] = din('cvec', [128, 16])
    I['w_mod'] = din('w_mod', [L, D, 6 * D])
    I['bmodT'] = din('bmodT', [L, 128, 48])
    I['bmod2'] = din('bmod2', [L, 2, 6 * D])
    I['g1T'] = din('g1T', [L, 128, 8])
    I['g2T'] = din('g2T', [L, 128, 8])
    I['w_in'] = din('w_in', [L, D, 4256])
    I['g_ckv'] = din('g_ckv', [L, 128, 1])
    I['w_ukv'] = din('w_ukv', [L, 128, 1024])
    I['g_cqT'] = din('g_cqT', [L, 128, 2])
    I['w_uq'] = din('w_uq', [L, 256, 768])
    I['g_qn'] = din('g_qn', [L, 1, 96])
    I['g_kn'] = din('g_kn', [L, 1, 96])
    I['wdwT'] = din('wdwT', [L, 128, 2 * 31])
    I['cvp'] = din('cvp', [L, 128, 6])
    I['wb_attn'] = din('wb_attn', [L, 512, D])
    I['wb_fnet'] = din('wb_fnet', [L, 256, D])
    I['wb_conv'] = din('wb_conv', [L, 256, D])
    I['w_out'] = din('w_out', [L, D, D])
    I['w_query'] = din('w_query', [L, D, D])
    I['skbd'] = din('skbd', [L, 128, 8 * 256])
    I['u_tab'] = din('u_tab', [L, 16384, D])
    I['v_tab'] = din('v_tab', [L, 16384, D])
    I['ropeC'] = din('ropeC', [T, 32])
    I['ropeS'] = din('ropeS', [T, 32])
    I['dftP'] = din('dftP', [2, T, T], BF16)
    I['dftC'] = din('dftC', [2, CT, CT], BF16)
    I['csbd'] = din('csbd', [128, 256])
    I['ident'] = din('ident', [128, 128])
    I['sel'] = din('sel', [2, 256])
    out_d = nc.dram_tensor('out', [T, D], F32, kind='ExternalOutput').ap()

    S = {}
    S['xres'] = dsc('xres', [NT, D], F32)
    S['QT'] = dsc('QT', [8, 96, NT], BF16)
    S['ZCS'] = dsc('ZCS', [NT, 512], BF16)
    S['UT'] = dsc('UT', [2, 128, NT], F32)
    S['Vd'] = dsc('Vd', [34, 128, 8 * 65], BF16)
    S['aT'] = dsc('aT', [4, 128, NT], BF16)
    S['fT'] = dsc('fT', [2, 128, NT], BF16)
    S['cvT'] = dsc('cvT', [2, 128, NT], BF16)
    S['uTs'] = nc.dram_tensor('uTs', [NCH, 128, 1024], BF16, kind='Internal').ap()
    S['vs'] = nc.dram_tensor('vs', [NCH, 128, 1024], BF16, kind='Internal').ap()

    P = Prog(nc)
    PB = P.pbuf

    identF = P.alloc(128)
    b_idF = Buf()
    P.dma(identF, I['ident'], b_idF, True)
    identB = P.alloc(128, BF16)
    b_idB = Buf()
    CP(P, 'dve', identB, identF, [b_idF], [b_idB])
    sel = P.alloc(256)
    b_sel = Buf()
    P.dma(sel[0:2, :], I['sel'], b_sel, True)
    cvec = P.alloc(16)
    b_cvec = Buf()
    P.dma(cvec, I['cvec'], b_cvec, True)
    scv = P.alloc(16)
    b_scv = Buf()
    ACT(P, scv, cvec, AF.Silu, [b_cvec], [b_scv])
    ones256 = P.alloc(128)
    b_ones = Buf()
    MEMSET(P, 'dve', ones256, 1.0 / 256, [b_ones])
    iota16 = P.alloc(16)
    iota128 = P.alloc(128)
    b_iota = Buf()
    P.op('pool', lambda e: e.iota(iota16, pattern=[[1, 16]], base=0, channel_multiplier=0,
                                  allow_small_or_imprecise_dtypes=True), (), [b_iota])
    P.op('pool', lambda e: e.iota(iota128, pattern=[[1, 128]], base=0, channel_multiplier=0,
                                  allow_small_or_imprecise_dtypes=True), (), [b_iota])
    iota128h = P.alloc(128, BF16)
    CP(P, 'dve', iota128h, iota128, [b_iota], [b_iota])
    csbdF = P.alloc(256)
    b_csF = Buf()
    P.dma(csbdF, I['csbd'], b_csF, True)
    csbd = P.alloc(256, BF16)
    b_cs = Buf()
    CP(P, 'dve', csbd, csbdF, [b_csF], [b_cs])

    modF = P.alloc(4 * 8 * 2)
    b_modF = Buf()
    gtb = [P.alloc(1024) for _ in range(4)]
    b_gtb = [Buf() for _ in range(4)]

    def xsrc(l, r0, n):
        if l == 0:
            if r0 < CT:
                return I['ctx'][r0:r0 + n, :]
            return I['x'][r0 - CT:r0 - CT + n, :]
        return S['xres'][r0:r0 + n, :]

    def phase_adaln(l):
        P.mark()
        gT = [P.alloc(8), P.alloc(8)]
        b_g = Buf()
        P.dma(gT[0], I['g1T'][l], b_g, True)
        b_g2 = Buf()
        P.dma(gT[1], I['g2T'][l], b_g2, True)
        bmT = P.alloc(48)
        b_bm = Buf()
        P.dma(bmT, I['bmodT'][l], b_bm, True)
        bm2 = P.alloc(6 * D)
        b_bm2 = Buf()
        P.dma(bm2[0:2, :], I['bmod2'][l], b_bm2, True)
        rows = P.alloc(2048)
        b_rows = Buf()
        wslot = [P.alloc(8 * 512) for _ in range(2)]
        b_ws = [Buf(), Buf()]
        wsrc = I['w_mod'][l].rearrange('(k p) n -> p k n', p=128)
        modF4 = r4(modF, 4, 8, 2)
        kindmap = {0: 0, 1: 1, 3: 2, 4: 3}
        for nb in range(12):
            ws, bw = wslot[nb % 2], b_ws[nb % 2]
            ws3 = r3(ws, 8, 512)
            P.dma(ws3, wsrc[:, :, nb * 512:(nb + 1) * 512], bw, True)
            kind = nb // 2
            if kind in (2, 5):
                ps = P.bank(0)
                for k in range(8):
                    MM(P, ps[0:2, :], r3(scv, 8, 2)[:, k, :], ws3[:, k, :], k == 0, k == 7, [bw, b_scv], [PB[0]])
                gi = 0 if kind == 2 else 1
                c0 = gi * 1024 + (nb % 2) * 512
                TT(P, 'dve', rows[0:2, c0:c0 + 512], ps[0:2, :], bm2[0:2, nb * 512:(nb + 1) * 512], ALU.add,
                   [PB[0], b_bm2], [b_rows])
            else:
                ps = P.bank(1)
                for cc in range(4):
                    for k in range(8):
                        MM(P, ps[:, cc * 2:cc * 2 + 2], ws3[:, k, cc * 128:(cc + 1) * 128], r3(scv, 8, 2)[:, k, :],
                           k == 0, k == 7, [bw, b_scv], [PB[1]])
                for cc in range(4):
                    gc = nb * 4 + cc
                    kk = gc % 8
                    dst = modF4[:, kindmap[kind], kk, :]
                    if kind in (0, 3):
                        TS(P, 'dve', dst, ps[:, cc * 2:cc * 2 + 2], bmT[:, gc:gc + 1], None, ALU.add, None,
                           [PB[1], b_bm], [b_modF])
                    else:
                        g = gT[0] if kind == 1 else gT[1]
                        TS(P, 'dve', dst, ps[:, cc * 2:cc * 2 + 2], bmT[:, gc:gc + 1], 1.0, ALU.add, ALU.add,
                           [PB[1], b_bm], [b_modF])
                        TS(P, 'dve', dst, dst, g[:, kk:kk + 1], None, ALU.mult, None, [b_modF, b_g, b_g2], [b_modF])
        for gi in range(2):
            for mi in range(2):
                for hf in range(2):
                    ps = P.bank(2 + hf)
                    MM(P, ps, sel[0:2, mi * 128:(mi + 1) * 128], rows[0:2, gi * 1024 + hf * 512: gi * 1024 + hf * 512 + 512],
                       True, True, [b_sel, b_rows], [PB[2 + hf]])
                    CP(P, 'act', gtb[gi * 2 + mi][:, hf * 512:(hf + 1) * 512], ps, [PB[2 + hf]], [b_gtb[gi * 2 + mi]])
        P.barrier()
        P.release()

    class HT:
        def __init__(self, nx=4, nh=2):
            self.nx, self.nh = nx, nh
            self.x = [P.alloc(1024) for _ in range(nx)]
            self.bx = [Buf() for _ in range(nx)]
            self.xn = [P.alloc(1024, BF16) for _ in range(2)]
            self.bxn = [Buf(), Buf()]
            self.junk = P.alloc(1024)
            self.bj = Buf()
            self.ss = P.alloc(2)
            self.bss = Buf()
            self.hT = [P.alloc(8 * TB, BF16) for _ in range(nh)]
            self.bh = [Buf() for _ in range(nh)]
            self.n = 0

        def begin(self, l, tb, from_res=False):
            xs, bxs = [], []
            for tt in range(2):
                xi = (self.n * 2 + tt) % self.nx
                xt, bx = self.x[xi], self.bx[xi]
                r0_ = tb * TB + tt * 128
                P.dma(xt, S['xres'][r0_:r0_ + 128, :] if from_res else xsrc(l, r0_, 128), bx, True)
                TTR(P, self.junk, xt, xt, self.ss[:, tt:tt + 1], [bx], [self.bj, self.bss])
                xs.append(xt)
                bxs.append(bx)
            RSTD(P, self.ss, self.ss, D, [self.bss], [self.bss])
            for tt in range(2):
                ACT(P, self.xn[tt], xs[tt], AF.Copy, [bxs[tt], self.bss], [self.bxn[tt]], scale=self.ss[:, tt:tt + 1])
            self.cur = (tb, xs, bxs)

        def finish(self, kind, tbank=0):
            tb, xs, bxs = self.cur
            mi = 1 if tb == 0 else 0
            modF4 = r4(modF, 4, 8, 2)
            slot = self.n % self.nh
            hT, bh = self.hT[slot], self.bh[slot]
            hT3 = r3(hT, 8, TB)
            for tt in range(2):
                xn, bxn = self.xn[tt], self.bxn[tt]
                pt = r3(P.bank(tbank).bitcast(BF16), 8, 128)
                for k in range(8):
                    TR(P, pt[:, k, :], xn[:, k * 128:(k + 1) * 128], identB, [bxn, b_idB], [PB[tbank]])
                for k in range(8):
                    if k % 2 == 1:
                        ACT(P, hT3[:, k, tt * 128:(tt + 1) * 128], pt[:, k, :], AF.Identity, [PB[tbank], b_modF], [bh],
                            scale=modF4[:, kind + 1, k, mi:mi + 1], bias=modF4[:, kind, k, mi:mi + 1])
                    else:
                        TS(P, 'dve', hT3[:, k, tt * 128:(tt + 1) * 128], pt[:, k, :], modF4[:, kind + 1, k, mi:mi + 1],
                           modF4[:, kind, k, mi:mi + 1], ALU.mult, ALU.add, [PB[tbank], b_modF], [bh])
            self.n += 1
            return hT3, bh, xs, bxs

        def make(self, l, tb, kind, tbank=0, from_res=False):
            self.begin(l, tb, from_res)
            return self.finish(kind, tbank)

    def load_w_bf16(dst3, bdst, src3, ncols, stg, bstg, k_n):
        for k in range(k_n):
            s, bs = stg[k % len(stg)], bstg[k % len(stg)]
            P.dma(s[:, 0:ncols], src3[:, k, :], bs, True)
            if k % 2 == 0:
                CP(P, 'act', dst3[:, k, :], s[:, 0:ncols], [bs], [bdst])
            else:
                CP(P, 'pool', dst3[:, k, :], s[:, 0:ncols], [bs], [bdst])

    AT = {}

    def alloc_attn():
        P.mark()
        KT = P.alloc(8 * NT, BF16)
        AT['b_KT'] = Buf()
        AT['KT3'] = r3(KT, 8, NT)

    def phase_A(l):
        alloc_attn()
        KT3, b_KT = AT['KT3'], AT['b_KT']
        P.mark()
        win = I['w_in'][l].rearrange('(k p) n -> p k n', p=128)
        w_tm = P.alloc(8 * 416, BF16)
        w_fm = P.alloc(8 * 768, BF16)
        b_wtm, b_wfm = Buf(), Buf()
        stg = [P.alloc(1184)] * 2
        bstg = [Buf()] * 2
        w_tm3, w_fm3 = r3(w_tm, 8, 416), r3(w_fm, 8, 768)
        for k in range(8):
            s, bs = stg[k % 2], bstg[k % 2]
            P.dma(s, win[:, k, 0:1184], bs, True)
            CP(P, 'act', w_tm3[:, k, :], s[:, 0:416], [bs], [b_wtm])
            CP(P, 'pool', w_fm3[:, k, :], s[:, 416:1184], [bs], [b_wfm])
        gck = P.alloc(1)
        gcq = P.alloc(2)
        b_gc = Buf()
        P.dma(gck, I['g_ckv'][l], b_gc, True)
        b_gq2 = Buf()
        P.dma(gcq, I['g_cqT'][l], b_gq2, True)
        wukv = P.alloc(1024, BF16)
        b_wukv = Buf()
        P.dma(stg[0][:, 0:1024], I['w_ukv'][l], bstg[0], True)
        TS(P, 'dve', wukv, stg[0][:, 0:1024], gck[:, 0:1], None, ALU.mult, None, [bstg[0], b_gc], [b_wukv])
        wuq = P.alloc(2 * 768, BF16)
        b_wuq = Buf()
        wuq3 = r3(wuq, 2, 768)
        for kk in range(2):
            s, bs = stg[1 - kk], bstg[1 - kk]
            P.dma(s[:, 0:768], I['w_uq'][l][kk * 128:(kk + 1) * 128, :], bs, True)
            TS(P, 'dve', wuq3[:, kk, :], s[:, 0:768], gcq[:, kk:kk + 1], None, ALU.mult, None, [bs, b_gq2], [b_wuq])
        gqb = P.alloc(96)
        gkb = P.alloc(96)
        b_gqk = Buf()
        P.dma(gqb, I['g_qn'][l].partition_broadcast(128), b_gqk, True)
        b_gqk2 = Buf()
        P.dma(gkb, I['g_kn'][l].partition_broadcast(128), b_gqk2, True)
        gvec = [b_gqk, b_gqk2]

        ht = HT(nx=4, nh=2)
        zfT = P.alloc(2 * TB, BF16)
        b_zfT = Buf()
        zfT3 = r3(zfT, 2, TB)
        sg = P.alloc(TB)
        b_sg = Buf()
        ut = [P.alloc(2 * TB) for _ in range(2)]
        b_ut = [Buf(), Buf()]
        zcs = [P.alloc(512, BF16) for _ in range(2)]
        b_zcs = [Buf(), Buf()]
        sets = []
        for _ in range(2):
            d_ = {}
            for nm, n_, dt_ in (('tmz', 416, F32), ('st', 24, F32), ('cn', 384, BF16), ('cT', 384, BF16), ('kvs', 1024, F32),
                                ('qs', 768, F32), ('sq', 768, F32), ('ktm', 768, BF16), ('qtm', 768, BF16), ('krr', 32, F32),
                                ('rtmp', 256, F32), ('rc', 32, F32), ('rs', 32, F32), ('vt', 8 * 65, BF16)):
                d_[nm] = P.alloc(n_, dt_)
                d_['b_' + nm] = Buf()
            d_['b_rope'] = Buf()
            MEMSET(P, 'pool', d_['vt'], 1.0, [d_['b_vt']])
            sets.append(d_)
        qTt = [P.alloc(8 * 128, BF16) for _ in range(2)]
        b_qTt = [Buf(), Buf()]

        for tb in range(NBLK):
            hT3, bh, xs, bxs = ht.make(l, tb, 0, tbank=0)
            is_ctx = (tb == 0)
            u_t, b_u = ut[tb % 2], b_ut[tb % 2]
            u3 = r3(u_t, 2, TB)
            for cc in range(2):
                bk = 2 + (cc % 2)
                ps = P.bank(bk)[:, 0:TB]
                for k in range(8):
                    MM(P, ps, w_fm3[:, k, cc * 128:(cc + 1) * 128], hT3[:, k, :], k == 0, k == 7, [b_wfm, bh], [PB[bk]])
                CP(P, 'act', zfT3[:, cc, :], ps, [PB[bk]], [b_zfT])
            for c2 in range(2):
                pa = P.bank(2)[:, 0:TB]
                pg = P.bank(3)[:, 0:TB]
                for k in range(8):
                    MM(P, pa, w_fm3[:, k, (2 + c2) * 128:(3 + c2) * 128], hT3[:, k, :], k == 0, k == 7, [b_wfm, bh], [PB[2]])
                for k in range(8):
                    MM(P, pg, w_fm3[:, k, (4 + c2) * 128:(5 + c2) * 128], hT3[:, k, :], k == 0, k == 7, [b_wfm, bh], [PB[3]])
                ACT(P, sg, pg, AF.Sigmoid, [PB[3]], [b_sg])
                TT(P, 'dve', u3[:, c2, :], pa, sg, ALU.mult, [PB[2], b_sg], [b_u])
            P.dma(S['UT'][:, :, tb * TB:(tb + 1) * TB].rearrange('c p t -> p c t'), u3, b_u, False)
            for tt in range(2):
                t0 = tb * TB + tt * 128
                ti = t0 // 128
                tsl = slice(tt * 128, (tt + 1) * 128)
                d_ = sets[tt]
                tmz, st, cn, cT, kvs, qs, sq, ktm, qtm, krr, rtmp, rc, rs, vt = (d_[k] for k in (
                    'tmz', 'st', 'cn', 'cT', 'kvs', 'qs', 'sq', 'ktm', 'qtm', 'krr', 'rtmp', 'rc', 'rs', 'vt'))
                b_tmz, b_st, b_cn, b_cT, b_kvs, b_qs, b_sq, b_ktm, b_qtm, b_krr, b_rtmp, b_rope, b_vt = (d_[k] for k in (
                    'b_tmz', 'b_st', 'b_cn', 'b_cT', 'b_kvs', 'b_qs', 'b_sq', 'b_ktm', 'b_qtm', 'b_krr', 'b_rtmp', 'b_rope', 'b_vt'))
                cT3 = r3(cT, 3, 128)
                z, bz = zcs[tt], b_zcs[tt]
                pz = P.bank(2 + tt)
                for kc in range(2):
                    MM(P, pz[:, kc * 256:(kc + 1) * 256], zfT3[:, kc, tsl], csbd, True, True, [b_zfT, b_cs], [PB[2 + tt]])
                CP(P, 'act', z, pz, [PB[2 + tt]], [bz])
                P.dma(S['ZCS'][t0:t0 + 128, :], z, bz, False)
                pp = P.bank(0)[:, 0:416]
                for k in range(8):
                    MM(P, pp, hT3[:, k, tsl], w_tm3[:, k, :], k == 0, k == 7, [bh, b_wtm], [PB[0]])
                CP(P, 'act', tmz, pp, [PB[0]], [b_tmz])
                TTR(P, sq[:, 0:128], tmz[:, 0:128], tmz[:, 0:128], st[:, 0:1], [b_tmz], [b_sq, b_st])
                TTR(P, sq[:, 0:256], tmz[:, 160:416], tmz[:, 160:416], st[:, 1:2], [b_tmz], [b_sq, b_st])
                TS(P, 'dve', st[:, 1:2], st[:, 1:2], 0.5, None, ALU.mult, None, [b_st], [b_st])
                RSTD(P, st[:, 0:2], st[:, 0:2], 128, [b_st], [b_st])
                ACT(P, cn[:, 0:128], tmz[:, 0:128], AF.Copy, [b_tmz, b_st], [b_cn], scale=st[:, 0:1])
                ACT(P, cn[:, 128:384], tmz[:, 160:416], AF.Copy, [b_tmz, b_st], [b_cn], scale=st[:, 1:2])
                pt = r3(P.bank(1).bitcast(BF16)[:, tt * 384:(tt + 1) * 384], 3, 128)
                for j in range(3):
                    TR(P, pt[:, j, :], cn[:, j * 128:(j + 1) * 128], identB, [b_cn, b_idB], [PB[1]])
                CP(P, 'dve', cT3, pt, [PB[1]], [b_cT])
                for hf in range(2):
                    MM(P, P.bank(4 + hf), cT3[:, 0, :], wukv[:, hf * 512:(hf + 1) * 512], True, True, [b_cT, b_wukv], [PB[4 + hf]])
                for hf in range(2):
                    CP(P, 'act', kvs[:, hf * 512:(hf + 1) * 512], P.bank(4 + hf), [PB[4 + hf]], [b_kvs])
                pq = P.bank(6, 2)
                for nh in range(2):
                    for kk in range(2):
                        MM(P, pq[:, nh * 512:nh * 512 + 384], cT3[:, 1 + kk, :], wuq3[:, kk, nh * 384:(nh + 1) * 384],
                           kk == 0, kk == 1, [b_cT, b_wuq], [PB[6], PB[7]])
                kv4 = r3(kvs, 8, 128)
                CP(P, 'pool', r3(vt, 8, 65)[:, :, 0:64], kv4[:, :, 64:128], [b_kvs], [b_vt])
                P.dma(S['Vd'][ti], vt, b_vt, False)
                sq3 = r3(sq[:, 0:512], 8, 64)
                TT(P, 'dve', sq3, kv4[:, :, 0:64], kv4[:, :, 0:64], ALU.mult, [b_kvs], [b_sq])
                RED(P, st[:, 8:16], sq3, ALU.add, [b_sq], [b_st])
                TTR(P, sq[:, 512:544], tmz[:, 128:160], tmz[:, 128:160], st[:, 2:3], [b_tmz], [b_sq, b_st])
                TS(P, 'dve', st[:, 8:16], st[:, 8:16], st[:, 2:3], None, ALU.add, None, [b_st], [b_st])
                RSTD(P, st[:, 8:16], st[:, 8:16], 96, [b_st], [b_st])
                ktm3 = r3(ktm, 8, 96)
                TT(P, 'dve', sq3, kv4[:, :, 0:64], st[:, 8:16].unsqueeze(2).to_broadcast([128, 8, 64]), ALU.mult,
                   [b_kvs, b_st], [b_sq])
                TT(P, 'dve', ktm3[:, :, 0:64], sq3, gkb[:, 0:64].unsqueeze(1).to_broadcast([128, 8, 64]), ALU.mult,
                   [b_sq] + gvec, [b_ktm])
                TT(P, 'dve', krr, tmz[:, 128:160], gkb[:, 64:96], ALU.mult, [b_tmz] + gvec, [b_krr])
                if not is_ctx:
                    P.dma(rc, I['ropeC'][t0 - CT:t0 - CT + 128, :], b_rope, True)
                    P.dma(rs, I['ropeS'][t0 - CT:t0 - CT + 128, :], b_rope, True)
                    kr4 = r3(krr, 4, 8)
                    rt4 = r3(rtmp[:, 0:32], 4, 8)
                    for hb in range(2):
                        for blk in range(2):
                            TT(P, 'dve', rt4[:, hb * 2 + blk, :], kr4[:, hb * 2 + (1 - blk), :],
                               r3(rs, 4, 8)[:, hb * 2 + blk, :], ALU.mult, [b_krr, b_rope], [b_rtmp])
                    TT(P, 'dve', krr, krr, rc, ALU.mult, [b_krr, b_rope], [b_krr])
                    TT(P, 'dve', krr, krr, rtmp[:, 0:32], ALU.add, [b_krr, b_rtmp], [b_krr])
                TT(P, 'dve', ktm3[:, :, 64:96], krr.unsqueeze(1).to_broadcast([128, 8, 32]),
                   st[:, 8:16].unsqueeze(2).to_broadcast([128, 8, 32]), ALU.mult, [b_krr, b_st], [b_ktm])
                pk = r3(P.bank(2 + tt).bitcast(BF16), 8, 128)
                for h in range(8):
                    TR(P, pk[0:96, h, :], ktm3[:, h, :], identB, [b_ktm, b_idB], [PB[2 + tt]])
                CP(P, 'act', KT3[0:96, :, t0:t0 + 128], pk[0:96, :, :], [PB[2 + tt]], [b_KT])
                for nh in range(2):
                    CP(P, 'act', qs[:, nh * 384:(nh + 1) * 384], pq[:, nh * 512:nh * 512 + 384], [PB[6], PB[7]], [b_qs])
                q3 = r3(qs, 8, 96)
                s3 = r3(sq, 8, 96)
                TT(P, 'dve', s3, q3, q3, ALU.mult, [b_qs], [b_sq])
                RED(P, st[:, 16:24], s3, ALU.add, [b_sq], [b_st])
                RSTD(P, st[:, 16:24], st[:, 16:24], 96, [b_st], [b_st])
                TT(P, 'dve', s3, q3, st[:, 16:24].unsqueeze(2).to_broadcast([128, 8, 96]), ALU.mult, [b_qs, b_st], [b_sq])
                TT(P, 'dve', q3, s3, gqb.unsqueeze(1).to_broadcast([128, 8, 96]), ALU.mult, [b_sq] + gvec, [b_qs])
                qtm3 = r3(qtm, 8, 96)
                CP(P, 'pool', qtm3[:, :, 0:64], q3[:, :, 0:64], [b_qs], [b_qtm])
                if not is_ctx:
                    q5 = qs.rearrange('p (h c) -> p h c', h=8, c=96)[:, :, 64:96].rearrange('p h (a b) -> p h a b', a=4, b=8)
                    rt5 = rtmp.rearrange('p (h a b) -> p h a b', h=8, a=4, b=8)
                    rs4 = r3(rs, 4, 8)
                    for hb in range(2):
                        for blk in range(2):
                            TT(P, 'dve', rt5[:, :, hb * 2 + blk, :], q5[:, :, hb * 2 + (1 - blk), :],
                               rs4[:, hb * 2 + blk, :].unsqueeze(1).to_broadcast([128, 8, 8]), ALU.mult,
                               [b_qs, b_rope], [b_rtmp])
                    qr = q3[:, :, 64:96]
                    TT(P, 'dve', s3[:, :, 0:32], qr, rc.unsqueeze(1).to_broadcast([128, 8, 32]), ALU.mult,
                       [b_qs, b_rope], [b_sq])
                    TT(P, 'dve', qtm3[:, :, 64:96], s3[:, :, 0:32], r3(rtmp, 8, 32), ALU.add, [b_sq, b_rtmp], [b_qtm])
                else:
                    CP(P, 'pool', qtm3[:, :, 64:96], q3[:, :, 64:96], [b_qs], [b_qtm])
                pqT = r3(P.bank(2 + tt).bitcast(BF16), 8, 128)
                for h in range(8):
                    TR(P, pqT[0:96, h, :], qtm3[:, h, :], identB, [b_qtm, b_idB], [PB[2 + tt]])
                qq, bqq = qTt[tt], b_qTt[tt]
                CP(P, 'act', r3(qq, 8, 128)[0:96, :, :], pqT[0:96, :, :], [PB[2 + tt]], [bqq])
                P.dma(S['QT'][:, :, t0:t0 + 128].rearrange('h p t -> p h t'), r3(qq, 8, 128)[0:96, :, :], bqq, False)
        if debug:
            kd = nc.dram_tensor('KTd%d' % l, [128, 8 * NT], BF16, kind='ExternalOutput').ap()
            P.dma(kd, KT3.rearrange('p a b -> p (a b)'), b_KT, False)
        P.barrier()
        P.release()

    def phase_B(l):
        KT3, b_KT = AT['KT3'], AT['b_KT']
        P.mark()
        Vs = P.alloc(34 * 8 * 65, BF16)
        b_V = Buf()
        V4 = r4(Vs, 34, 8, 65)
        P.dma(r3(Vs, 34, 8 * 65), S['Vd'].rearrange('t p c -> p t c'), b_V, True)
        qt = [P.alloc(8 * 512, BF16) for _ in range(2)]
        b_qt = [Buf(), Buf()]
        E = [P.alloc(512, BF16) for _ in range(3)]
        b_E = [Buf() for _ in range(3)]
        atm = P.alloc(4 * 512)
        b_atm = Buf()
        atb = P.alloc(4 * 512, BF16)
        b_atb = Buf()
        rcp = P.alloc(4)
        b_rcp = Buf()
        aTt = [P.alloc(4 * 512, BF16) for _ in range(2)]
        b_aTt = [Buf(), Buf()]
        scale = 96.0 ** -0.5
        tgen = gen_T(l, 3)
        blocks = [(CT + i * 512, 512, list(range(34))) for i in range(8)]
        if l < L - 1:
            blocks.append((0, 256, [0, 1]))
        ne = 0
        for bi, (q0, nq, kts) in enumerate(blocks):
            nqi = nq // 128
            q_t, bq = qt[bi % 2], b_qt[bi % 2]
            q3 = r3(q_t, 8, 512)
            P.dma(q3[0:96, :, 0:nq], S['QT'][:, :, q0:q0 + nq].rearrange('h p t -> p h t'), bq, True)
            atm3 = r3(atm, 4, 512)
            steps = [(h, ki, kt) for h in range(8) for ki, kt in enumerate(kts)]

            def qk(i):
                h, ki, kt = steps[i]
                sb = (ne + i) % 3
                MM(P, P.bank(sb)[:, 0:nq], KT3[0:96, h, kt * 128:(kt + 1) * 128], q3[0:96, h, 0:nq], True, True,
                   [b_KT, bq], [PB[sb]])

            qk(0)
            for i, (h, ki, kt) in enumerate(steps):
                if i + 1 < len(steps):
                    qk(i + 1)
                if i % 3 == 0:
                    next(tgen, None)
                sb = (ne + i) % 3
                accb = 4 + (h % 2)
                acc = r3(P.bank(accb)[:, 0:4 * 65], 4, 65)
                e_t, be = E[sb], b_E[sb]
                ACT(P, e_t[:, 0:nq], P.bank(sb)[:, 0:nq], AF.Exp, [PB[sb]], [be], scale=scale)
                for qi in range(nqi):
                    MM(P, acc[:, qi, :], e_t[:, qi * 128:(qi + 1) * 128], V4[:, kt, h, :], ki == 0 and qi == 0,
                       ki == len(kts) - 1, [be, b_V], [PB[accb]], skip=True)
                if ki == len(kts) - 1:
                    P.op('dve', lambda e, acc=acc, nqi=nqi: e.reciprocal(out=rcp[:, 0:nqi], in_=acc[:, 0:nqi, 64]),
                         [PB[accb]], [b_rcp])
                    TT(P, 'dve', atm3[:, 0:nqi, h * 64:(h + 1) * 64], acc[:, 0:nqi, 0:64],
                       rcp[:, 0:nqi].unsqueeze(2).to_broadcast([128, nqi, 64]), ALU.mult, [PB[accb], b_rcp], [b_atm])
            ne += len(steps)
            CP(P, 'pool', atb, atm, [b_atm], [b_atb])
            atb3 = r3(atb, 4, 512)
            a_t, ba = aTt[bi % 2], b_aTt[bi % 2]
            a3 = r3(a_t, 4, 512)
            for qi in range(nqi):
                pt = r3(P.bank(6 + (qi % 2)).bitcast(BF16)[:, 0:512], 4, 128)
                for c in range(4):
                    TR(P, pt[:, c, :], atb3[:, qi, c * 128:(c + 1) * 128], identB, [b_atb, b_idB], [PB[6 + (qi % 2)]])
                CP(P, 'act', a3[:, :, qi * 128:(qi + 1) * 128], pt, [PB[6 + (qi % 2)]], [ba])
            P.dma(S['aT'][:, :, q0:q0 + nq].rearrange('c p t -> p c t'), a3[:, :, 0:nq], ba, False)
        for _ in tgen:
            pass
        P.barrier()
        P.release()
        P.release()

    def phase_C1(l):
        P.mark()
        Z = P.alloc(32 * 512, BF16)
        bZ = Buf()
        Z3 = r3(Z, 32, 512)
        P.dma(Z3, S['ZCS'][CT:NT, :].rearrange('(tt p) n -> p tt n', p=128), bZ, True)
        slab = [P.alloc(8 * 512, BF16) for _ in range(4)]
        b_slab = [Buf() for _ in range(4)]
        fo = [P.alloc(2 * 512, BF16) for _ in range(2)]
        b_fo = [Buf(), Buf()]
        ns = 0
        for kb in range(8):
            for cs in range(2):
                src = I['dftP'][cs].rearrange('(tt p) k -> p tt k', p=128)
                for tg in range(4):
                    sl, bs = slab[ns % 4], b_slab[ns % 4]
                    sl3 = r3(sl, 8, 512)
                    P.dma(sl3, src[:, tg * 8:(tg + 1) * 8, kb * 512:(kb + 1) * 512], bs, True)
                    for t8 in range(8):
                        tt = tg * 8 + t8
                        first = (cs == 0 and tt == 0)
                        last = (cs == 1 and tt == 31)
                        for mc in range(2):
                            MM(P, P.bank(mc), Z3[:, tt, mc * 256 + cs * 128: mc * 256 + cs * 128 + 128], sl3[:, t8, :],
                               first, last, [bZ, bs], [PB[mc]])
                    ns += 1
            f_t, bf = fo[kb % 2], b_fo[kb % 2]
            f3 = r3(f_t, 2, 512)
            CP(P, 'act', f3[:, 0, :], P.bank(0), [PB[0]], [bf])
            CP(P, 'dve', f3[:, 1, :], P.bank(1), [PB[1]], [bf])
            P.dma(S['fT'][:, :, CT + kb * 512:CT + (kb + 1) * 512].rearrange('c p t -> p c t'), f3, bf, False)
        if l < L - 1:
            Zc = P.alloc(2 * 512, BF16)
            bZc = Buf()
            Zc3 = r3(Zc, 2, 512)
            P.dma(Zc3, S['ZCS'][0:CT, :].rearrange('(tt p) n -> p tt n', p=128), bZc, True)
            dc = P.alloc(2 * 2 * 256, BF16)
            bdc = Buf()
            dc4 = r4(dc, 2, 2, 256)
            for cs in range(2):
                P.dma(dc4[:, cs, :, :], I['dftC'][cs].rearrange('(tt p) k -> p tt k', p=128), bdc, True)
            for mc in range(2):
                n = 0
                for cs in range(2):
                    for tt in range(2):
                        MM(P, P.bank(2 + mc)[:, 0:256], Zc3[:, tt, mc * 256 + cs * 128: mc * 256 + cs * 128 + 128],
                           dc4[:, cs, tt, :], n == 0, n == 3, [bZc, bdc], [PB[2 + mc]])
                        n += 1
            f_t, bf = fo[0], b_fo[0]
            f3 = r3(f_t, 2, 512)
            CP(P, 'act', f3[:, 0, 0:256], P.bank(2)[:, 0:256], [PB[2]], [bf])
            CP(P, 'dve', f3[:, 1, 0:256], P.bank(3)[:, 0:256], [PB[3]], [bf])
            P.dma(S['fT'][:, :, 0:CT].rearrange('c p t -> p c t'), f3[:, :, 0:256], bf, False)
        P.barrier()
        P.release()

    def phase_C2(l):
        P.mark()
        LB = 15 + CT + 15 + T + 15
        U = P.alloc(2 * LB)
        bU = Buf()
        U3 = r3(U, 2, LB)
        MEMSET(P, 'pool', U, 0.0, [bU])
        OC, OX = 15, 15 + CT + 15
        for c in range(2):
            P.dma(U3[:, c, OC:OC + CT], S['UT'][c, :, 0:CT], bU, True)
            P.dma(U3[:, c, OX:OX + T], S['UT'][c, :, CT:NT], bU, True)
        wd = P.alloc(62)
        bwd = Buf()
        P.dma(wd, I['wdwT'][l], bwd, True)
        wd3 = r3(wd, 2, 31)
        cp = P.alloc(6)
        bcp = Buf()
        P.dma(cp, I['cvp'][l], bcp, True)
        cp3 = r3(cp, 2, 3)
        A = P.alloc(2 * LB)
        bA = Buf()
        A3 = r3(A, 2, LB)
        NV = LB - 30
        for c in range(2):
            TS(P, 'dve', A3[:, c, 15:15 + NV], U3[:, c, 0:NV], wd3[:, c, 0:1], cp3[:, c, 0:1], ALU.mult, ALU.add,
               [bU, bwd, bcp], [bA])
            for w in range(1, 31):
                STT(P, A3[:, c, 15:15 + NV], U3[:, c, w:w + NV], wd3[:, c, w:w + 1], A3[:, c, 15:15 + NV], ALU.mult, ALU.add,
                    [bU, bwd, bA], [bA])
        sqt = P.alloc(2 * 512)
        bsq = Buf()
        sq3 = r3(sqt, 2, 512)
        mean = P.alloc(512)
        bmean = Buf()
        var = P.alloc(512)
        bvar = Buf()
        y = P.alloc(512)
        by = Buf()
        co = [P.alloc(2 * 512, BF16) for _ in range(2)]
        bco = [Buf(), Buf()]
        blocks = [(OX + i * 512, CT + i * 512, 512) for i in range(8)]
        if l < L - 1:
            blocks.append((OC, 0, 256))
        for bi, (o, t0, n) in enumerate(blocks):
            for c in range(2):
                ACT(P, sq3[:, c, 0:n], A3[:, c, o:o + n], AF.Square, [bA], [bsq])
            for c in range(2):
                MM(P, P.bank(0)[:, 0:n], ones256, A3[:, c, o:o + n], c == 0, c == 1, [b_ones, bA], [PB[0]])
            for c in range(2):
                MM(P, P.bank(1)[:, 0:n], ones256, sq3[:, c, 0:n], c == 0, c == 1, [b_ones, bsq], [PB[1]])
            CP(P, 'act', mean[:, 0:n], P.bank(0)[:, 0:n], [PB[0]], [bmean])
            TT(P, 'dve', var[:, 0:n], mean[:, 0:n], mean[:, 0:n], ALU.mult, [bmean], [bvar])
            TT(P, 'dve', var[:, 0:n], P.bank(1)[:, 0:n], var[:, 0:n], ALU.subtract, [PB[1], bvar], [bvar])
            TS(P, 'dve', var[:, 0:n], var[:, 0:n], EPS, None, ALU.add, None, [bvar], [bvar])
            ACT(P, var[:, 0:n], var[:, 0:n], AF.Sqrt, [bvar], [bvar])
            P.op('dve', lambda e, n=n: e.reciprocal(out=var[:, 0:n], in_=var[:, 0:n]), [bvar], [bvar])
            c_t, bc = co[bi % 2], bco[bi % 2]
            c3 = r3(c_t, 2, 512)
            for c in range(2):
                TT(P, 'dve', y[:, 0:n], A3[:, c, o:o + n], mean[:, 0:n], ALU.subtract, [bA, bmean], [by])
                TT(P, 'dve', y[:, 0:n], y[:, 0:n], var[:, 0:n], ALU.mult, [by, bvar], [by])
                ACT(P, c3[:, c, 0:n], y[:, 0:n], AF.Silu, [by, bcp], [bc], scale=cp3[:, c, 1:2], bias=cp3[:, c, 2:3])
            P.dma(S['cvT'][:, :, t0:t0 + n].rearrange('c p t -> p c t'), c3[:, :, 0:n], bc, False)
        P.barrier()
        P.release()

    def phase_D(l):
        P.mark()
        win = I['w_in'][l].rearrange('(k p) n -> p k n', p=128)
        stg = [P.alloc(1024) for _ in range(3)]
        bstg = [Buf() for _ in range(3)]
        wg = P.alloc(8 * 3072, BF16)
        b_wg = Buf()
        wg3 = r3(wg, 8, 3072)
        for part in range(3):
            load_w_bf16(wg3[:, :, part * 1024:(part + 1) * 1024], b_wg, win[:, :, 1184 + part * 1024:1184 + (part + 1) * 1024],
                        1024, stg, bstg, 8)
        wba = P.alloc(4 * 1024, BF16)
        wbf = P.alloc(2 * 1024, BF16)
        wbc = P.alloc(2 * 1024, BF16)
        wo = P.alloc(8 * 1024, BF16)
        b_wb = Buf()
        load_w_bf16(r3(wba, 4, 1024), b_wb, I['wb_attn'][l].rearrange('(k p) n -> p k n', p=128), 1024, stg, bstg, 4)
        load_w_bf16(r3(wbf, 2, 1024), b_wb, I['wb_fnet'][l].rearrange('(k p) n -> p k n', p=128), 1024, stg, bstg, 2)
        load_w_bf16(r3(wbc, 2, 1024), b_wb, I['wb_conv'][l].rearrange('(k p) n -> p k n', p=128), 1024, stg, bstg, 2)
        load_w_bf16(r3(wo, 8, 1024), b_wb, I['w_out'][l].rearrange('(k p) n -> p k n', p=128), 1024, stg, bstg, 8)
        wbr = [(r3(wba, 4, 1024), 4), (r3(wbf, 2, 1024), 2), (r3(wbc, 2, 1024), 2)]
        wo3 = r3(wo, 8, 1024)
        ht = HT()
        sgt = [P.alloc(24 * TB, BF16) for _ in range(2)]
        b_sgt = [Buf(), Buf()]
        br = [P.alloc(8 * TB, BF16) for _ in range(2)]
        b_br = [Buf(), Buf()]
        mT = P.alloc(8 * TB, BF16)
        b_mT = Buf()
        mT3 = r3(mT, 8, TB)
        t1 = P.alloc(TB)
        t2 = P.alloc(TB)
        b_t1, b_t2 = Buf(), Buf()
        xo = [P.alloc(1024) for _ in range(2)]
        b_xo = [Buf(), Buf()]
        blocks = [tb for tb in range(NBLK) if not (tb == 0 and l == L - 1)]
        st_ = {}

        def front(n):
            tb = blocks[n]
            hT3, bh, xs, bxs = ht.make(l, tb, 0, tbank=0)
            b_t, bb = br[n % 2], b_br[n % 2]
            b3 = r3(b_t, 8, TB)
            tsl = slice(tb * TB, (tb + 1) * TB)
            P.dma(b3[:, 0:4, :], S['aT'][:, :, tsl].rearrange('c p t -> p c t'), bb, True)
            P.dma(b3[:, 4:6, :], S['fT'][:, :, tsl].rearrange('c p t -> p c t'), bb, True)
            P.dma(b3[:, 6:8, :], S['cvT'][:, :, tsl].rearrange('c p t -> p c t'), bb, True)
            sg3 = r3(sgt[n % 2], 24, TB)
            bsg = b_sgt[n % 2]
            for gc in range(24):
                bk = 1 + (gc % 2)
                ps = P.bank(bk)[:, 0:TB]
                for k in range(8):
                    MM(P, ps, wg3[:, k, gc * 128:(gc + 1) * 128], hT3[:, k, :], k == 0, k == 7, [b_wg, bh], [PB[bk]])
                ACT(P, sg3[:, gc, :], ps, AF.Sigmoid, [PB[bk]], [bsg])
            st_[n] = (xs, bxs, b3, bb, sg3, bsg)

        def back_a(n):
            xs, bxs, b3, bb, sg3, bsg = st_[n]
            for oc in range(8):
                koff = 0
                for bi, (w3, nk) in enumerate(wbr):
                    ps = P.bank(3 + bi)[:, 0:TB]
                    for kc in range(nk):
                        MM(P, ps, w3[:, kc, oc * 128:(oc + 1) * 128], b3[:, koff + kc, :], kc == 0, kc == nk - 1, [b_wb, bb], [PB[3 + bi]])
                    koff += nk
                TT(P, 'dve', t1, P.bank(3)[:, 0:TB], sg3[:, oc, :], ALU.mult, [PB[3], bsg], [b_t1])
                TT(P, 'dve', t2, P.bank(4)[:, 0:TB], sg3[:, 8 + oc, :], ALU.mult, [PB[4], bsg], [b_t2])
                TT(P, 'dve', t1, t1, t2, ALU.add, [b_t1, b_t2], [b_t1])
                TT(P, 'dve', t2, P.bank(5)[:, 0:TB], sg3[:, 16 + oc, :], ALU.mult, [PB[5], bsg], [b_t2])
                TT(P, 'dve', mT3[:, oc, :], t1, t2, ALU.add, [b_t1, b_t2], [b_mT])

        def back_b(n):
            tb = blocks[n]
            mi = 1 if tb == 0 else 0
            xs, bxs, b3, bb, sg3, bsg = st_.pop(n)
            for tt in range(2):
                x_o, bxo = xo[tt], b_xo[tt]
                for nh in range(2):
                    ps = P.bank(6 + nh)
                    for k in range(8):
                        MM(P, ps, mT3[:, k, tt * 128:(tt + 1) * 128], wo3[:, k, nh * 512:(nh + 1) * 512], k == 0, k == 7,
                           [b_mT, b_wb], [PB[6 + nh]])
                    TT(P, 'dve', x_o[:, nh * 512:(nh + 1) * 512], ps, gtb[mi][:, nh * 512:(nh + 1) * 512], ALU.mult,
                       [PB[6 + nh], b_gtb[mi]], [bxo])
                TT(P, 'pool', x_o, x_o, xs[tt], ALU.add, [bxo, bxs[tt]], [bxo])
                P.dma(S['xres'][tb * TB + tt * 128: tb * TB + (tt + 1) * 128, :], x_o, bxo, False)

        front(0)
        for n in range(len(blocks)):
            back_a(n)
            if n + 1 < len(blocks):
                front(n + 1)
            back_b(n)
        P.barrier()
        P.release()

    def gen_T(l, bank):
        us = [P.alloc(1024) for _ in range(2)]
        b_us = [Buf(), Buf()]
        vsl = [P.alloc(1024) for _ in range(2)]
        b_vs = [Buf(), Buf()]
        ub = [P.alloc(1024, BF16) for _ in range(2)]
        b_ub = [Buf(), Buf()]
        uo = [P.alloc(1024, BF16) for _ in range(2)]
        b_uo = [Buf(), Buf()]
        vb = [P.alloc(1024, BF16) for _ in range(2)]
        b_vb = [Buf(), Buf()]
        usrc = I['u_tab'][l].rearrange('(i j) d -> j i d', j=NCH)
        vsrc = I['v_tab'][l].rearrange('(i j) d -> j i d', j=NCH)
        for j in range(NCH):
            s = j % 2
            P.dma(us[s], usrc[j], b_us[s], True)
            P.dma(vsl[s], vsrc[j], b_vs[s], True)
            yield
            yield
            CP(P, 'pool', vb[s], vsl[s], [b_vs[s]], [b_vb[s]])
            CP(P, 'dve', ub[s], us[s], [b_us[s]], [b_ub[s]])
            yield
            P.dma(S['vs'][j], vb[s], b_vb[s], False)
            pt = r3(P.bank(bank).bitcast(BF16), 8, 128)
            for k in range(8):
                TR(P, pt[:, k, :], ub[s][:, k * 128:(k + 1) * 128], identB, [b_ub[s], b_idB], [PB[bank]])
            yield
            CP(P, 'dve', uo[s], P.bank(bank).bitcast(BF16), [PB[bank]], [b_uo[s]])
            yield
            P.dma(S['uTs'][j], uo[s], b_uo[s], False)

    def phase_E(l):
        P.mark()
        last = (l == L - 1)
        wq = P.alloc(8 * 1024, BF16)
        b_wq = Buf()
        wq3 = r3(wq, 8, 1024)
        skb = P.alloc(8 * 256, BF16)
        b_sk = Buf()
        P.mark()
        stg = [P.alloc(1024) for _ in range(2)]
        bstg = [Buf(), Buf()]
        load_w_bf16(wq3, b_wq, I['w_query'][l].rearrange('(k p) n -> p k n', p=128), 1024, stg, bstg, 8)
        for hf in range(2):
            P.dma(stg[hf], I['skbd'][l][:, hf * 1024:(hf + 1) * 1024], bstg[hf], True)
            CP(P, 'dve', skb[:, hf * 1024:(hf + 1) * 1024], stg[hf], [bstg[hf]], [b_sk])
        P.barrier()
        P.release()
        skb3 = r3(skb, 8, 256)
        ht = HT(nx=4, nh=2)
        qT = P.alloc(8 * TB, BF16)
        b_qT = Buf()
        qT3 = r3(qT, 8, TB)
        sc = P.alloc(2048)
        b_sc = Buf()
        sc4 = r4(sc, 8, 2, 128)
        tmp = P.alloc(128)
        b_tmp = Buf()
        val = P.alloc(256)
        b_val = Buf()
        val4 = r4(val, 8, 2, 16)
        idx = P.alloc(256).bitcast(U32)
        b_idx = Buf()
        idx4 = r4(idx, 8, 2, 16)
        idxf = P.alloc(256)
        b_idxf = Buf()
        idxf4 = r4(idxf, 8, 2, 16)
        cand = P.alloc(2048)
        b_cand = Buf()
        cand4 = r4(cand, 8, 16, 16)
        ctmp = P.alloc(256)
        b_ctmp = Buf()
        best = P.alloc(128)
        b_best = Buf()
        best3 = r3(best, 8, 16)
        pos = P.alloc(128).bitcast(U32)
        b_pos = Buf()
        pos3 = r3(pos, 8, 16)
        pa = P.alloc(128).bitcast(U32)
        pb_ = P.alloc(128).bitcast(U32)
        paf = P.alloc(128)
        pbf = P.alloc(128)
        b_pab = Buf()
        oh = cand
        b_oh = b_cand
        oh4 = r4(oh, 8, 16, 16)
        tok3 = P.alloc(3 * 128)
        b_tok = Buf()
        tok33 = r3(tok3, 3, 128)
        gsum = P.alloc(8)
        b_gsum = Buf()
        ijw = [P.alloc(3 * TB, BF16) for _ in range(2)]
        b_ijwl = [Buf(), Buf()]
        GT = 16
        Aoh = [P.alloc(GT * 128, BF16) for _ in range(2)]
        Boh = [P.alloc(GT * 128, BF16) for _ in range(2)]
        b_A = [Buf(), Buf()]
        b_B = [Buf(), Buf()]
        G = P.alloc(TB * 128, BF16)
        b_G = Buf()
        G3 = r3(G, TB, 128)
        NS = 3
        utr = [P.alloc(1024, BF16) for _ in range(NS)]
        b_utr = [Buf() for _ in range(NS)]
        vtr = [P.alloc(1024, BF16) for _ in range(NS)]
        b_vtr = [Buf() for _ in range(NS)]
        gs = [P.alloc(TB, BF16) for _ in range(2)]
        b_gs = [Buf(), Buf()]
        av = [P.alloc(TB, BF16) for _ in range(2)]
        b_av = [Buf(), Buf()]
        xo = [P.alloc(1024) for _ in range(2)]
        b_xo = [Buf(), Buf()]
        iota128b = iota128h.unsqueeze(1).to_broadcast([128, GT, 128])
        state = {}

        def prep(tb, slot):
            ht.begin(l, tb, from_res=True)
            yield
            yield
            yield
            hT3, bh, xs, bxs = ht.finish(2, tbank=0)
            yield
            yield
            for h in range(8):
                ps = P.bank(1)[:, 0:TB]
                for k in range(8):
                    MM(P, ps, wq3[:, k, h * 128:(h + 1) * 128], hT3[:, k, :], k == 0, k == 7, [b_wq, bh], [PB[1]])
                yield
                CP(P, 'act', qT3[:, h, :], ps, [PB[1]], [b_qT])
            ijwT3 = r3(ijw[slot], 3, TB)
            b_ijw = b_ijwl[slot]
            for tt in range(2):
                tsl = slice(tt * 128, (tt + 1) * 128)
                yield
                for h in range(8):
                    ps = P.bank(1)[:, 0:256]
                    MM(P, ps, qT3[:, h, tsl], skb3[:, h, :], True, True, [b_qT, b_sk], [PB[1]])
                    yield
                    CP(P, 'act', sc[:, h * 256:(h + 1) * 256], ps, [PB[1]], [b_sc])
                yield
                for h in range(8):
                    for p in range(2):
                        s_hp = sc4[:, h, p, :]
                        v16 = val4[:, h, p, :]
                        i16 = idx4[:, h, p, :]
                        P.op('dve', lambda e, o=v16[:, 0:8], i=s_hp: e.max(out=o, in_=i), [b_sc], [b_val])
                        P.op('dve', lambda e, o=i16[:, 0:8], m=v16[:, 0:8], i=s_hp: e.max_index(out=o, in_max=m, in_values=i),
                             [b_sc, b_val], [b_idx])
                        P.op('dve', lambda e, o=tmp[:, 0:128], m=v16[:, 0:8], i=s_hp: e.match_replace(
                            out=o, in_to_replace=m, in_values=i, imm_value=-1e30), [b_sc, b_val], [b_tmp])
                        yield
                        P.op('dve', lambda e, o=v16[:, 8:16], i=tmp[:, 0:128]: e.max(out=o, in_=i), [b_tmp], [b_val])
                        P.op('dve', lambda e, o=i16[:, 8:16], m=v16[:, 8:16], i=tmp[:, 0:128]: e.max_index(
                            out=o, in_max=m, in_values=i), [b_tmp, b_val], [b_idx])
                        yield
                CP(P, 'dve', idxf, idx, [b_idx], [b_idxf])
                TT(P, 'dve', cand4, val4[:, :, 0, :].unsqueeze(3).to_broadcast([128, 8, 16, 16]),
                   val4[:, :, 1, :].unsqueeze(2).to_broadcast([128, 8, 16, 16]), ALU.add, [b_val], [b_cand])
                yield
                for h in range(8):
                    c_h = cand[:, h * 256:(h + 1) * 256]
                    b16 = best3[:, h, :]
                    p16 = pos3[:, h, :]
                    P.op('dve', lambda e, o=b16[:, 0:8], i=c_h: e.max(out=o, in_=i), [b_cand], [b_best])
                    P.op('dve', lambda e, o=p16[:, 0:8], m=b16[:, 0:8], i=c_h: e.max_index(out=o, in_max=m, in_values=i),
                         [b_cand, b_best], [b_pos])
                    P.op('dve', lambda e, o=ctmp, m=b16[:, 0:8], i=c_h: e.match_replace(
                        out=o, in_to_replace=m, in_values=i, imm_value=-1e30), [b_cand, b_best], [b_ctmp])
                    yield
                    P.op('dve', lambda e, o=b16[:, 8:16], i=ctmp: e.max(out=o, in_=i), [b_ctmp], [b_best])
                    P.op('dve', lambda e, o=p16[:, 8:16], m=b16[:, 8:16], i=ctmp: e.max_index(out=o, in_max=m, in_values=i),
                         [b_ctmp, b_best], [b_pos])
                    yield
                P.op('dve', lambda e: e.tensor_single_scalar(out=pa, in_=pos, scalar=4, op=ALU.logical_shift_right),
                     [b_pos], [b_pab])
                P.op('dve', lambda e: e.tensor_single_scalar(out=pb_, in_=pos, scalar=15, op=ALU.bitwise_and),
                     [b_pos], [b_pab])
                yield
                CP(P, 'dve', paf, pa, [b_pab], [b_pab])
                CP(P, 'dve', pbf, pb_, [b_pab], [b_pab])
                yield
                for which, pf in enumerate((paf, pbf)):
                    pf3 = r3(pf, 8, 16)
                    TT(P, 'dve', oh4, iota16.unsqueeze(1).unsqueeze(1).to_broadcast([128, 8, 16, 16]),
                       pf3.unsqueeze(3).to_broadcast([128, 8, 16, 16]), ALU.is_equal, [b_iota, b_pab], [b_oh])
                    yield
                    TT(P, 'dve', oh4, oh4, idxf4[:, :, which, :].unsqueeze(2).to_broadcast([128, 8, 16, 16]), ALU.mult,
                       [b_oh, b_idxf], [b_oh])
                    yield
                    RED(P, tok33[:, which, :], r3(oh, 128, 16), ALU.add, [b_oh], [b_tok])
                    yield
                w3 = r3(tok33[:, 2, :], 8, 16)
                TT(P, 'dve', w3, best3, best3[:, :, 0:1].to_broadcast([128, 8, 16]), ALU.subtract, [b_best], [b_tok])
                yield
                ACT(P, tok33[:, 2, :], tok33[:, 2, :], AF.Exp, [b_tok], [b_tok])
                yield
                RED(P, gsum, w3, ALU.add, [b_tok], [b_gsum])
                P.op('dve', lambda e: e.reciprocal(out=gsum, in_=gsum), [b_gsum], [b_gsum])
                TT(P, 'dve', w3, w3, gsum.unsqueeze(2).to_broadcast([128, 8, 16]), ALU.mult, [b_tok, b_gsum], [b_tok])
                yield
                yield
                pT = r3(P.bank(1)[:, 0:384], 3, 128)
                for c in range(3):
                    TR(P, pT[:, c, :], tok33[:, c, :], identF, [b_tok, b_idF], [PB[1]])
                yield
                CP(P, 'act', ijwT3[:, :, tsl], pT, [PB[1]], [b_ijw])
            state[tb] = (hT3, bh, xs, bxs, ijwT3, b_ijw)

        blocks = [tb for tb in range(NBLK) if not (tb == 0 and last)]
        for _ in prep(blocks[0], 0):
            pass
        nchunk = 0
        for bi, tb in enumerate(blocks):
            mi = 1 if tb == 0 else 0
            hT3, bh, xs, bxs, ijwT3, b_ijw = state.pop(tb)
            for g in range(TB // GT):
                A_t, bA = Aoh[g % 2], b_A[g % 2]
                B_t, bB = Boh[g % 2], b_B[g % 2]
                A3, B3 = r3(A_t, GT, 128), r3(B_t, GT, 128)
                gsl = slice(g * GT, (g + 1) * GT)
                TT(P, 'dve', A3, iota128b, ijwT3[:, 0, gsl].unsqueeze(2).to_broadcast([128, GT, 128]), ALU.is_equal,
                   [b_iota, b_ijw], [bA])
                TT(P, 'dve', B3, iota128b, ijwT3[:, 1, gsl].unsqueeze(2).to_broadcast([128, GT, 128]), ALU.is_equal,
                   [b_iota, b_ijw], [bB])
                TT(P, 'pool', B3, B3, ijwT3[:, 2, gsl].unsqueeze(2).to_broadcast([128, GT, 128]), ALU.mult,
                   [bB, b_ijw], [bB])
                for t4 in range(GT // 4):
                    bk = t4 % 2
                    for ti in range(4):
                        tloc = t4 * 4 + ti
                        MM(P, P.bank(bk)[:, ti * 128:(ti + 1) * 128], A3[:, tloc, :], B3[:, tloc, :], True, True, [bA, bB], [PB[bk]])
                    tg0 = g * GT + t4 * 4
                    CP(P, 'act', G[:, tg0 * 128:(tg0 + 4) * 128], P.bank(bk), [PB[bk]], [b_G])
            nxt = prep(blocks[bi + 1], (bi + 1) % 2) if bi + 1 < len(blocks) else None

            def load(j):
                s_ = (nchunk + j) % NS
                P.dma(utr[s_], S['uTs'][j], b_utr[s_], True, q='sp')
                P.dma(vtr[s_], S['vs'][j], b_vtr[s_], True, q='sp')

            def mm1(j):
                s_ = (nchunk + j) % NS
                u3 = r3(utr[s_], 8, 128)
                sb = 2 + ((nchunk + j) % 2)
                for k in range(8):
                    MM(P, P.bank(sb)[:, 0:TB], u3[:, k, :], hT3[:, k, :], k == 0, k == 7, [b_utr[s_], bh], [PB[sb]])

            for j in range(min(NS, NCH)):
                load(j)
            mm1(0)
            for j in range(NCH):
                if j + 1 < NCH:
                    mm1(j + 1)
                s_ = (nchunk + j) % NS
                sb = 2 + ((nchunk + j) % 2)
                g_t, bg = gs[(nchunk + j) % 2], b_gs[(nchunk + j) % 2]
                ACT(P, g_t, P.bank(sb)[:, 0:TB], AF.Gelu_apprx_tanh, [PB[sb]], [bg])
                a_t, ba = av[(nchunk + j) % 2], b_av[(nchunk + j) % 2]
                TT(P, 'pool', a_t, g_t, G3[:, :, j], ALU.mult, [bg, b_G], [ba])
                for tt in range(2):
                    for nh in range(2):
                        ob = 4 + tt * 2 + nh
                        MM(P, P.bank(ob), a_t[:, tt * 128:(tt + 1) * 128], vtr[s_][:, nh * 512:(nh + 1) * 512],
                           j == 0, j == NCH - 1, [ba, b_vtr[s_]], [PB[ob]])
                if j + NS < NCH:
                    load(j + NS)
                if nxt is not None:
                    next(nxt, None)
                    if j % 4 == 0:
                        next(nxt, None)
            nchunk += NCH
            if nxt is not None:
                for _ in nxt:
                    pass
            for tt in range(2):
                x_o, bxo = xo[tt], b_xo[tt]
                for nh in range(2):
                    ob = 4 + tt * 2 + nh
                    TT(P, 'dve', x_o[:, nh * 512:(nh + 1) * 512], P.bank(ob), gtb[2 + mi][:, nh * 512:(nh + 1) * 512], ALU.mult,
                       [PB[ob], b_gtb[2 + mi]], [bxo])
                TT(P, 'pool', x_o, x_o, xs[tt], ALU.add, [bxo, bxs[tt]], [bxo])
                r0 = tb * TB + tt * 128
                if last:
                    P.dma(out_d[r0 - CT:r0 - CT + 128, :], x_o, bxo, False)
                else:
                    P.dma(S['xres'][r0:r0 + 128, :], x_o, bxo, False)
        print('phase E sbuf words', P.sb_off, 'of', P.sb_words)
        P.barrier()
        P.release()

    phases = []
    for l in range(n_layers):
        phases += [('adaln', phase_adaln, l), ('A', phase_A, l), ('B', phase_B, l), ('C1', phase_C1, l),
                   ('C2', phase_C2, l), ('D', phase_D, l), ('E', phase_E, l)]
    for name, fn, l in phases:
        fn(l)
        if stop_after is not None and stop_after == (name, l):
            break
    P.emit()
    return nc


def _consts():
    t = np.arange(T, dtype=np.float64)
    row = np.repeat(np.arange(T // 64, dtype=np.float32), 64)
    col = np.tile(np.arange(64, dtype=np.float32), T // 64)
    inv = (np.float32(10000.0) ** (-np.arange(0, 16, 2, dtype=np.float32) / np.float32(16))).astype(np.float32)
    ar = row[:, None] * inv
    ac = col[:, None] * inv
    ang = np.concatenate([ar, ar, ac, ac], axis=-1).astype(np.float32)
    ropeC = np.cos(ang).astype(np.float32)
    sn = np.sin(ang).astype(np.float32)
    sign = np.tile(np.concatenate([-np.ones(8, np.float32), np.ones(8, np.float32)]), 2)
    ropeS = (sn * sign[None, :]).astype(np.float32)
    n = (np.outer(np.arange(T), np.arange(T)) % T).astype(np.float64)
    dftP = np.stack([np.cos(2 * np.pi * n / T) / 64.0, -np.sin(2 * np.pi * n / T) / 64.0]).astype(ml_dtypes.bfloat16)
    n2 = (np.outer(np.arange(CT), np.arange(CT)) % CT).astype(np.float64)
    dftC = np.stack([np.cos(2 * np.pi * n2 / CT) / 16.0, -np.sin(2 * np.pi * n2 / CT) / 16.0]).astype(ml_dtypes.bfloat16)
    c = np.arange(128)
    m = np.arange(128)
    same = (c[:, None] // 64) == (m[None, :] // 64)
    ph = 2 * np.pi * ((c[:, None] % 64) * (m[None, :] % 64) % 64) / 64.0
    csbd = np.concatenate([np.where(same, np.cos(ph) / 8.0, 0.0), np.where(same, np.sin(ph) / 8.0, 0.0)], axis=1).astype(np.float32)
    ident = np.eye(128, dtype=np.float32)
    sel = np.zeros((2, 256), np.float32)
    sel[0, 0:128] = 1.0
    sel[1, 128:256] = 1.0
    return dict(ropeC=ropeC, ropeS=ropeS, dftP=dftP, dftC=dftC, csbd=csbd, ident=ident, sel=sel)


_CONSTS = None
_NC_CACHE = {}


def make_in_maps(inp, cores):
    global _CONSTS
    if _CONSTS is None:
        _CONSTS = _consts()
    f = lambda a: np.ascontiguousarray(np.asarray(a, dtype=np.float32))
    shared = dict(_CONSTS)
    shared['w_mod'] = f(inp['w_mod'])
    bm = f(inp['b_mod'])
    shared['bmodT'] = np.ascontiguousarray(bm.reshape(L, 48, 128).transpose(0, 2, 1))
    shared['bmod2'] = np.ascontiguousarray(np.repeat(bm[:, None, :], 2, axis=1))
    shared['g1T'] = np.ascontiguousarray(f(inp['g_norm1']).reshape(L, 8, 128).transpose(0, 2, 1))
    shared['g2T'] = np.ascontiguousarray(f(inp['g_norm2']).reshape(L, 8, 128).transpose(0, 2, 1))
    shared['w_in'] = f(inp['w_in'])
    shared['g_ckv'] = f(inp['g_ckv']).reshape(L, 128, 1)
    shared['w_ukv'] = f(inp['w_ukv'])
    shared['g_cqT'] = np.ascontiguousarray(f(inp['g_cq']).reshape(L, 2, 128).transpose(0, 2, 1))
    shared['w_uq'] = f(inp['w_uq'])
    shared['g_qn'] = f(inp['g_qn']).reshape(L, 1, 96)
    shared['g_kn'] = f(inp['g_kn']).reshape(L, 1, 96)
    wdw = f(inp['w_dw']).reshape(L, 31, 2, 128)
    shared['wdwT'] = np.ascontiguousarray(wdw.transpose(0, 3, 2, 1)).reshape(L, 128, 62)
    cvp = np.stack([f(inp['b_dw']), f(inp['g_cln']), f(inp['b_cln'])], axis=-1)
    shared['cvp'] = np.ascontiguousarray(cvp.reshape(L, 2, 128, 3).transpose(0, 2, 1, 3)).reshape(L, 128, 6)
    for k in ('wb_attn', 'wb_fnet', 'wb_conv', 'w_out', 'w_query', 'u_tab', 'v_tab'):
        shared[k] = f(inp[k])
    sk = f(inp['sub_keys'])
    skbd = np.zeros((L, 128, 8, 256), np.float32)
    for p in range(2):
        skbd[:, p * 64:(p + 1) * 64, :, p * 128:(p + 1) * 128] = sk[:, :, p].transpose(0, 3, 1, 2)
    shared['skbd'] = skbd.reshape(L, 128, 2048)
    x = f(inp['x'])
    ctx = f(inp['ctx'])
    c = f(inp['c'])
    cc = f(inp['c_ctx'])
    maps = []
    for b in cores:
        m = dict(shared)
        m['x'] = x[b]
        m['ctx'] = ctx[b]
        cv = np.stack([c[b].reshape(8, 128).T, cc.reshape(8, 128).T], axis=-1)
        m['cvec'] = np.ascontiguousarray(cv).reshape(128, 16)
        maps.append(m)
    return maps


def kernel(**inputs):
    if 'full' not in _NC_CACHE:
        _NC_CACHE['full'] = build_program()
    nc = _NC_CACHE['full']
    maps = make_in_maps(inputs, list(range(8)))
    res = run_bass_kernel_spmd(nc, maps, core_ids=list(range(8)))
    return np.stack([np.asarray(r['out'], dtype=np.float32) for r in res.results], axis=0)
```

```python
import numpy as np
import ml_dtypes
import concourse.bass as bass
import concourse.mybir as mybir
from concourse.bass_utils import run_bass_kernel_spmd

F32 = mybir.dt.float32
BF16 = mybir.dt.bfloat16
U32 = mybir.dt.uint32
AF = mybir.ActivationFunctionType
ALU = mybir.AluOpType
AX = mybir.AxisListType

L = 2
D = 1024
T = 4096
CT = 256
NT = T + CT
TB = 256
NBLK = NT // TB
EPS = 1e-6
NCH = 128

CE = ('pe', 'act', 'dve', 'pool')
ENG = ('pe', 'act', 'dve', 'pool', 'sp')


class Buf:
    __slots__ = ('name', 'w', 'r', 'dkey', 'dcnt')

    def __init__(self, name=''):
        self.name = name
        self.w = None
        self.r = {}
        self.dkey = None
        self.dcnt = 0


class Prog:
    def __init__(self, nc, sbuf_words=207 * 256):
        self.nc = nc
        self.q = {e: [] for e in ENG}
        self.semh = {e: nc.alloc_semaphore('s_' + e) for e in CE}
        self.cnt = {e: 0 for e in CE}
        self.known = {e: {} for e in ENG}
        self.dtot = {}
        self.nd = 0
        self.sb = nc.alloc_sbuf_tensor('sb_all', [128, sbuf_words], F32)
        self.sb_words = sbuf_words
        self.sb_off = 0
        self.ps = nc.alloc_psum_tensor('ps_all', [128, 4096], F32)
        self.marks = []
        self.free_keys = []
        self.scope_keys = []
        self.pbuf = [Buf('bank%d' % i) for i in range(8)]

    def alloc(self, nelem, dtype=F32):
        nbytes = nelem * (2 if dtype == BF16 else 4)
        words = (nbytes + 3) // 4
        words = (words + 7) // 8 * 8
        assert self.sb_off + words <= self.sb_words, ('SBUF overflow', self.sb_off, words)
        ap = self.sb[:, self.sb_off:self.sb_off + words]
        self.sb_off += words
        if dtype != F32:
            ap = ap.bitcast(dtype)
        return ap[:, 0:nelem]

    def mark(self):
        self.marks.append((self.sb_off, len(self.scope_keys)))

    def release(self):
        self.sb_off, nk = self.marks.pop()
        self.free_keys += self.scope_keys[nk:]
        del self.scope_keys[nk:]

    def bank(self, b, n=1):
        return self.ps[:, b * 512:(b + n) * 512]

    def _wait(self, e, key, count):
        if key == e and e == 'pe':
            return
        if self.known[e].get(key, 0) >= count:
            return
        self.known[e][key] = count
        h = self.semh[key]
        self.q[e].append(lambda eng, h=h, c=count: eng.wait_ge(h, c))

    def _deps(self, e, r, w):
        for b in r:
            if b.w is not None:
                self._wait(e, *b.w)
        for b in w:
            if b.w is not None:
                self._wait(e, *b.w)
            for k, c in b.r.items():
                self._wait(e, k, c)

    def op(self, e, fn, r=(), w=()):
        self._deps(e, r, w)
        self.cnt[e] += 1
        c = self.cnt[e]
        h = self.semh[e]
        self.q[e].append(lambda eng, fn=fn, h=h: fn(eng).then_inc(h, 1))
        for b in r:
            b.r[e] = c
        for b in w:
            b.w = (e, c)
            b.r = {}

    def dma(self, out, in_, buf, load, q='sp', extra_r=(), **kw):
        if buf.dkey is None:
            if self.free_keys:
                buf.dkey = self.free_keys.pop()
            else:
                buf.dkey = 'd%d' % self.nd
                self.nd += 1
                self.semh[buf.dkey] = self.nc.alloc_semaphore(buf.dkey)
            if self.marks:
                self.scope_keys.append(buf.dkey)
            buf.dcnt = self.dtot.get(buf.dkey, 0)
        if load:
            self._deps(q, list(extra_r), [buf])
        else:
            self._deps(q, [buf] + list(extra_r), [])
        buf.dcnt += 16
        c = buf.dcnt
        self.dtot[buf.dkey] = c
        h = self.semh[buf.dkey]
        self.q[q].append(lambda eng, h=h, out=out, in_=in_, kw=kw: eng.dma_start(out=out, in_=in_, **kw).then_inc(h, 16))
        if load:
            buf.w = (buf.dkey, c)
            buf.r = {}
        else:
            buf.r[buf.dkey] = c

    def barrier(self, engines=ENG):
        for e in engines:
            for k in CE:
                if k != e and self.cnt[k] > 0:
                    self._wait(e, k, self.cnt[k])
            for k, c in self.dtot.items():
                self._wait(e, k, c)

    def emit(self):
        nc = self.nc
        self.barrier(ENG)
        with nc.Block() as block:
            @block.tensor
            def _(eng):
                for f in self.q['pe']:
                    f(eng)

            @block.scalar
            def _(eng):
                for f in self.q['act']:
                    f(eng)

            @block.vector
            def _(eng):
                for f in self.q['dve']:
                    f(eng)

            @block.gpsimd
            def _(eng):
                for f in self.q['pool']:
                    f(eng)

            @block.sync
            def _(eng):
                for f in self.q['sp']:
                    f(eng)


def MM(P, out, lhsT, rhs, start, stop, r, w, skip=False):
    P.op('pe', lambda e: e.matmul(out, lhsT=lhsT, rhs=rhs, start=start, stop=stop, skip_group_check=skip), r, w)


def TR(P, out, in_, ident, r, w):
    P.op('pe', lambda e: e.transpose(out=out, in_=in_, identity=ident), r, w)


def ACT(P, out, in_, func, r, w, scale=None, bias=None, accum=None):
    kw = {}
    if scale is not None:
        kw['scale'] = scale
    if bias is not None:
        kw['bias'] = bias
    if accum is not None:
        kw['accum_out'] = accum
    P.op('act', lambda e: e.activation(out=out, in_=in_, func=func, **kw), r, w)


def TT(P, eng, out, in0, in1, op, r, w):
    P.op(eng, lambda e: e.tensor_tensor(out=out, in0=in0, in1=in1, op=op), r, w)


def TS(P, eng, out, in0, s1, s2, op0, op1, r, w):
    if op1 is None:
        P.op(eng, lambda e: e.tensor_scalar(out=out, in0=in0, scalar1=s1, scalar2=None, op0=op0), r, w)
    else:
        P.op(eng, lambda e: e.tensor_scalar(out=out, in0=in0, scalar1=s1, scalar2=s2, op0=op0, op1=op1), r, w)


def STT(P, out, in0, scalar, in1, op0, op1, r, w):
    P.op('dve', lambda e: e.scalar_tensor_tensor(out=out, in0=in0, scalar=scalar, in1=in1, op0=op0, op1=op1), r, w)


def CP(P, eng, out, in_, r, w):
    if eng == 'act':
        P.op('act', lambda e: e.copy(out=out, in_=in_), r, w)
    else:
        P.op(eng, lambda e: e.tensor_copy(out=out, in_=in_), r, w)


def RED(P, out, in_, op, r, w):
    P.op('dve', lambda e: e.tensor_reduce(out=out, in_=in_, axis=AX.X, op=op), r, w)


def TTR(P, out, in0, in1, accum, r, w):
    P.op('act', lambda e: e.activation(out=out, in_=in0, func=AF.Square, accum_out=accum), r, w)


def MEMSET(P, eng, ap, val, w):
    P.op(eng, lambda e: e.memset(ap, val), (), w)


def RSTD(P, out, ss, n, r, w):
    TS(P, 'dve', out, ss, 1.0 / n, EPS, ALU.mult, ALU.add, r, w)
    ACT(P, out, out, AF.Sqrt, w, w)
    P.op('dve', lambda e: e.reciprocal(out=out, in_=out), w, w)


def r3(ap, a, b):
    return ap.rearrange('p (a b) -> p a b', a=a, b=b)


def r4(ap, a, b, c):
    return ap.rearrange('p (a b c) -> p a b c', a=a, b=b, c=c)


def build_program(debug=False, n_layers=L, stop_after=None):
    nc = bass.Bass('TRN2', target_bir_lowering=False)

    def din(name, shape, dt=F32):
        return nc.dram_tensor(name, list(shape), dt, kind='ExternalInput').ap()

    skind = 'ExternalOutput' if debug else 'Internal'

    def dsc(name, shape, dt):
        return nc.dram_tensor(name, list(shape), dt, kind=skind).ap()

    I = {}
    I['x'] = din('x', [T, D])
    I['ctx'] = din('ctx', [CT, D])
    I['cvec'] = din('cvec', [128, 16])
    I['w_mod'] = din('w_mod', [L, D, 6 * D])
    I['bmodT'] = din('bmodT', [L, 128, 48])
    I['bmod2'] = din('bmod2', [L, 2, 6 * D])
    I['g1T'] = din('g1T', [L, 128, 8])
    I['g2T'] = din('g2T', [L, 128, 8])
    I['w_in'] = din('w_in', [L, D, 4256])
    I['g_ckv'] = din('g_ckv', [L, 128, 1])
    I['w_ukv'] = din('w_ukv', [L, 128, 1024])
    I['g_cqT'] = din('g_cqT', [L, 128, 2])
    I['w_uq'] = din('w_uq', [L, 256, 768])
    I['g_qn'] = din('g_qn', [L, 1, 96])
    I['g_kn'] = din('g_kn', [L, 1, 96])
    I['wdwT'] = din('wdwT', [L, 128, 2 * 31])
    I['cvp'] = din('cvp', [L, 128, 6])
    I['wb_attn'] = din('wb_attn', [L, 512, D])
    I['wb_fnet'] = din('wb_fnet', [L, 256, D])
    I['wb_conv'] = din('wb_conv', [L, 256, D])
    I['w_out'] = din('w_out', [L, D, D])
    I['w_query'] = din('w_query', [L, D, D])
    I['skbd'] = din('skbd', [L, 128, 8 * 256])
    I['u_tab'] = din('u_tab', [L, 16384, D])
    I['v_tab'] = din('v_tab', [L, 16384, D])
    I['ropeC'] = din('ropeC', [T, 32])
    I['ropeS'] = din('ropeS', [T, 32])
    I['dftP'] = din('dftP', [2, T, T], BF16)
    I['dftC'] = din('dftC', [2, CT, CT], BF16)
    I['csbd'] = din('csbd', [128, 256])
    I['ident'] = din('ident', [128, 128])
    I['sel'] = din('sel', [2, 256])
    out_d = nc.dram_tensor('out', [T, D], F32, kind='ExternalOutput').ap()

    S = {}
    S['xres'] = dsc('xres', [NT, D], F32)
    S['QT'] = dsc('QT', [8, 96, NT], BF16)
    S['ZCS'] = dsc('ZCS', [NT, 512], BF16)
    S['UT'] = dsc('UT', [2, 128, NT], F32)
    S['aT'] = dsc('aT', [4, 128, NT], BF16)
    S['fT'] = dsc('fT', [2, 128, NT], BF16)
    S['cvT'] = dsc('cvT', [2, 128, NT], BF16)
    S['uTs'] = nc.dram_tensor('uTs', [NCH, 128, 1024], BF16, kind='Internal').ap()
    S['vs'] = nc.dram_tensor('vs', [NCH, 128, 1024], BF16, kind='Internal').ap()

    P = Prog(nc)
    PB = P.pbuf

    identF = P.alloc(128)
    b_idF = Buf()
    P.dma(identF, I['ident'], b_idF, True)
    identB = P.alloc(128, BF16)
    b_idB = Buf()
    CP(P, 'dve', identB, identF, [b_idF], [b_idB])
    sel = P.alloc(256)
    b_sel = Buf()
    P.dma(sel[0:2, :], I['sel'], b_sel, True)
    cvec = P.alloc(16)
    b_cvec = Buf()
    P.dma(cvec, I['cvec'], b_cvec, True)
    scv = P.alloc(16)
    b_scv = Buf()
    ACT(P, scv, cvec, AF.Silu, [b_cvec], [b_scv])
    ones256 = P.alloc(128)
    b_ones = Buf()
    MEMSET(P, 'dve', ones256, 1.0 / 256, [b_ones])
    iota16 = P.alloc(16)
    iota128 = P.alloc(128)
    b_iota = Buf()
    P.op('pool', lambda e: e.iota(iota16, pattern=[[1, 16]], base=0, channel_multiplier=0,
                                  allow_small_or_imprecise_dtypes=True), (), [b_iota])
    P.op('pool', lambda e: e.iota(iota128, pattern=[[1, 128]], base=0, channel_multiplier=0,
                                  allow_small_or_imprecise_dtypes=True), (), [b_iota])
    iota128h = P.alloc(128, BF16)
    CP(P, 'dve', iota128h, iota128, [b_iota], [b_iota])
    csbdF = P.alloc(256)
    b_csF = Buf()
    P.dma(csbdF, I['csbd'], b_csF, True)
    csbd = P.alloc(256, BF16)
    b_cs = Buf()
    CP(P, 'dve', csbd, csbdF, [b_csF], [b_cs])

    modF = P.alloc(4 * 8 * 2)
    b_modF = Buf()
    gtb = [P.alloc(1024) for _ in range(4)]
    b_gtb = [Buf() for _ in range(4)]

    def xsrc(l, r0, n):
        if l == 0:
            if r0 < CT:
                return I['ctx'][r0:r0 + n, :]
            return I['x'][r0 - CT:r0 - CT + n, :]
        return S['xres'][r0:r0 + n, :]

    def phase_adaln(l):
        P.mark()
        gT = [P.alloc(8), P.alloc(8)]
        b_g = Buf()
        P.dma(gT[0], I['g1T'][l], b_g, True)
        b_g2 = Buf()
        P.dma(gT[1], I['g2T'][l], b_g2, True)
        bmT = P.alloc(48)
        b_bm = Buf()
        P.dma(bmT, I['bmodT'][l], b_bm, True)
        bm2 = P.alloc(6 * D)
        b_bm2 = Buf()
        P.dma(bm2[0:2, :], I['bmod2'][l], b_bm2, True)
        rows = P.alloc(2048)
        b_rows = Buf()
        wslot = [P.alloc(8 * 512) for _ in range(2)]
        b_ws = [Buf(), Buf()]
        wsrc = I['w_mod'][l].rearrange('(k p) n -> p k n', p=128)
        modF4 = r4(modF, 4, 8, 2)
        kindmap = {0: 0, 1: 1, 3: 2, 4: 3}
        for nb in range(12):
            ws, bw = wslot[nb % 2], b_ws[nb % 2]
            ws3 = r3(ws, 8, 512)
            P.dma(ws3, wsrc[:, :, nb * 512:(nb + 1) * 512], bw, True)
            kind = nb // 2
            if kind in (2, 5):
                ps = P.bank(0)
                for k in range(8):
                    MM(P, ps[0:2, :], r3(scv, 8, 2)[:, k, :], ws3[:, k, :], k == 0, k == 7, [bw, b_scv], [PB[0]])
                gi = 0 if kind == 2 else 1
                c0 = gi * 1024 + (nb % 2) * 512
                TT(P, 'dve', rows[0:2, c0:c0 + 512], ps[0:2, :], bm2[0:2, nb * 512:(nb + 1) * 512], ALU.add,
                   [PB[0], b_bm2], [b_rows])
            else:
                ps = P.bank(1)
                for cc in range(4):
                    for k in range(8):
                        MM(P, ps[:, cc * 2:cc * 2 + 2], ws3[:, k, cc * 128:(cc + 1) * 128], r3(scv, 8, 2)[:, k, :],
                           k == 0, k == 7, [bw, b_scv], [PB[1]])
                for cc in range(4):
                    gc = nb * 4 + cc
                    kk = gc % 8
                    dst = modF4[:, kindmap[kind], kk, :]
                    if kind in (0, 3):
                        TS(P, 'dve', dst, ps[:, cc * 2:cc * 2 + 2], bmT[:, gc:gc + 1], None, ALU.add, None,
                           [PB[1], b_bm], [b_modF])
                    else:
                        g = gT[0] if kind == 1 else gT[1]
                        TS(P, 'dve', dst, ps[:, cc * 2:cc * 2 + 2], bmT[:, gc:gc + 1], 1.0, ALU.add, ALU.add,
                           [PB[1], b_bm], [b_modF])
                        TS(P, 'dve', dst, dst, g[:, kk:kk + 1], None, ALU.mult, None, [b_modF, b_g, b_g2], [b_modF])
        for gi in range(2):
            for mi in range(2):
                for hf in range(2):
                    ps = P.bank(2 + hf)
                    MM(P, ps, sel[0:2, mi * 128:(mi + 1) * 128], rows[0:2, gi * 1024 + hf * 512: gi * 1024 + hf * 512 + 512],
                       True, True, [b_sel, b_rows], [PB[2 + hf]])
                    CP(P, 'act', gtb[gi * 2 + mi][:, hf * 512:(hf + 1) * 512], ps, [PB[2 + hf]], [b_gtb[gi * 2 + mi]])
        P.barrier()
        P.release()

    class HT:
        def __init__(self, nx=4, nh=2):
            self.nx, self.nh = nx, nh
            self.x = [P.alloc(1024) for _ in range(nx)]
            self.bx = [Buf() for _ in range(nx)]
            self.xn = [P.alloc(1024, BF16) for _ in range(2)]
            self.bxn = [Buf(), Buf()]
            self.junk = P.alloc(1024)
            self.bj = Buf()
            self.ss = P.alloc(2)
            self.bss = Buf()
            self.hT = [P.alloc(8 * TB, BF16) for _ in range(nh)]
            self.bh = [Buf() for _ in range(nh)]
            self.n = 0

        def begin(self, l, tb, from_res=False):
            xs, bxs = [], []
            for tt in range(2):
                xi = (self.n * 2 + tt) % self.nx
                xt, bx = self.x[xi], self.bx[xi]
                r0_ = tb * TB + tt * 128
                P.dma(xt, S['xres'][r0_:r0_ + 128, :] if from_res else xsrc(l, r0_, 128), bx, True)
                TTR(P, self.junk, xt, xt, self.ss[:, tt:tt + 1], [bx], [self.bj, self.bss])
                xs.append(xt)
                bxs.append(bx)
            RSTD(P, self.ss, self.ss, D, [self.bss], [self.bss])
            for tt in range(2):
                ACT(P, self.xn[tt], xs[tt], AF.Copy, [bxs[tt], self.bss], [self.bxn[tt]], scale=self.ss[:, tt:tt + 1])
            self.cur = (tb, xs, bxs)

        def finish(self, kind, tbank=0):
            tb, xs, bxs = self.cur
            mi = 1 if tb == 0 else 0
            modF4 = r4(modF, 4, 8, 2)
            slot = self.n % self.nh
            hT, bh = self.hT[slot], self.bh[slot]
            hT3 = r3(hT, 8, TB)
            for tt in range(2):
                xn, bxn = self.xn[tt], self.bxn[tt]
                pt = r3(P.bank(tbank).bitcast(BF16), 8, 128)
                for k in range(8):
                    TR(P, pt[:, k, :], xn[:, k * 128:(k + 1) * 128], identB, [bxn, b_idB], [PB[tbank]])
                for k in range(8):
                    if k % 2 == 1:
                        ACT(P, hT3[:, k, tt * 128:(tt + 1) * 128], pt[:, k, :], AF.Identity, [PB[tbank], b_modF], [bh],
                            scale=modF4[:, kind + 1, k, mi:mi + 1], bias=modF4[:, kind, k, mi:mi + 1])
                    else:
                        TS(P, 'dve', hT3[:, k, tt * 128:(tt + 1) * 128], pt[:, k, :], modF4[:, kind + 1, k, mi:mi + 1],
                           modF4[:, kind, k, mi:mi + 1], ALU.mult, ALU.add, [PB[tbank], b_modF], [bh])
            self.n += 1
            return hT3, bh, xs, bxs

        def make(self, l, tb, kind, tbank=0, from_res=False):
            self.begin(l, tb, from_res)
            return self.finish(kind, tbank)

    def load_w_bf16(dst3, bdst, src3, ncols, stg, bstg, k_n):
        for k in range(k_n):
            s, bs = stg[k % len(stg)], bstg[k % len(stg)]
            P.dma(s[:, 0:ncols], src3[:, k, :], bs, True)
            if k % 2 == 0:
                CP(P, 'act', dst3[:, k, :], s[:, 0:ncols], [bs], [bdst])
            else:
                CP(P, 'pool', dst3[:, k, :], s[:, 0:ncols], [bs], [bdst])

    AT = {}

    def alloc_attn():
        P.mark()
        KT = P.alloc(8 * NT, BF16)
        Vs = P.alloc(34 * 8 * 65, BF16)
        AT['b_KT'] = Buf()
        AT['b_V'] = Buf()
        AT['KT3'] = r3(KT, 8, NT)
        AT['V4'] = r4(Vs, 34, 8, 65)
        MEMSET(P, 'pool', Vs, 1.0, [AT['b_V']])

    def phase_A(l):
        alloc_attn()
        KT3, V4, b_KT, b_V = AT['KT3'], AT['V4'], AT['b_KT'], AT['b_V']
        P.mark()
        win = I['w_in'][l].rearrange('(k p) n -> p k n', p=128)
        w_tm = P.alloc(8 * 416, BF16)
        w_fm = P.alloc(8 * 768, BF16)
        b_wtm, b_wfm = Buf(), Buf()
        stg = [P.alloc(1184)] * 2
        bstg = [Buf()] * 2
        w_tm3, w_fm3 = r3(w_tm, 8, 416), r3(w_fm, 8, 768)
        for k in range(8):
            s, bs = stg[k % 2], bstg[k % 2]
            P.dma(s, win[:, k, 0:1184], bs, True)
            CP(P, 'act', w_tm3[:, k, :], s[:, 0:416], [bs], [b_wtm])
            CP(P, 'pool', w_fm3[:, k, :], s[:, 416:1184], [bs], [b_wfm])
        gck = P.alloc(1)
        gcq = P.alloc(2)
        b_gc = Buf()
        P.dma(gck, I['g_ckv'][l], b_gc, True)
        b_gq2 = Buf()
        P.dma(gcq, I['g_cqT'][l], b_gq2, True)
        wukv = P.alloc(1024, BF16)
        b_wukv = Buf()
        P.dma(stg[0][:, 0:1024], I['w_ukv'][l], bstg[0], True)
        TS(P, 'dve', wukv, stg[0][:, 0:1024], gck[:, 0:1], None, ALU.mult, None, [bstg[0], b_gc], [b_wukv])
        wuq = P.alloc(2 * 768, BF16)
        b_wuq = Buf()
        wuq3 = r3(wuq, 2, 768)
        for kk in range(2):
            s, bs = stg[1 - kk], bstg[1 - kk]
            P.dma(s[:, 0:768], I['w_uq'][l][kk * 128:(kk + 1) * 128, :], bs, True)
            TS(P, 'dve', wuq3[:, kk, :], s[:, 0:768], gcq[:, kk:kk + 1], None, ALU.mult, None, [bs, b_gq2], [b_wuq])
        gqb = P.alloc(96)
        gkb = P.alloc(96)
        b_gqk = Buf()
        P.dma(gqb, I['g_qn'][l].partition_broadcast(128), b_gqk, True)
        b_gqk2 = Buf()
        P.dma(gkb, I['g_kn'][l].partition_broadcast(128), b_gqk2, True)
        gvec = [b_gqk, b_gqk2]

        ht = HT(nx=2, nh=1)
        tmz = P.alloc(416)
        b_tmz = Buf()
        zfT = P.alloc(2 * TB, BF16)
        b_zfT = Buf()
        zfT3 = r3(zfT, 2, TB)
        sg = P.alloc(TB)
        b_sg = Buf()
        ut = [P.alloc(2 * TB) for _ in range(2)]
        b_ut = [Buf(), Buf()]
        zcs = [P.alloc(512, BF16) for _ in range(2)]
        b_zcs = [Buf(), Buf()]
        st = P.alloc(24)
        b_st = Buf()
        cn = P.alloc(384, BF16)
        b_cn = Buf()
        cT = P.alloc(3 * 128, BF16)
        b_cT = Buf()
        cT3 = r3(cT, 3, 128)
        kvs = P.alloc(1024)
        b_kvs = Buf()
        qs = P.alloc(768)
        b_qs = Buf()
        sq = P.alloc(768)
        b_sq = Buf()
        ktm = P.alloc(768, BF16)
        b_ktm = Buf()
        qtm = P.alloc(768, BF16)
        b_qtm = Buf()
        krr = P.alloc(32)
        b_krr = Buf()
        rtmp = P.alloc(256)
        b_rtmp = Buf()
        rc = P.alloc(32)
        rs = P.alloc(32)
        b_rope = Buf()
        qTt = [P.alloc(8 * 128, BF16) for _ in range(2)]
        b_qTt = [Buf(), Buf()]

        for tb in range(NBLK):
            hT3, bh, xs, bxs = ht.make(l, tb, 0, tbank=0)
            is_ctx = (tb == 0)
            u_t, b_u = ut[tb % 2], b_ut[tb % 2]
            u3 = r3(u_t, 2, TB)
            for cc in range(2):
                bk = 2 + (cc % 2)
                ps = P.bank(bk)[:, 0:TB]
                for k in range(8):
                    MM(P, ps, w_fm3[:, k, cc * 128:(cc + 1) * 128], hT3[:, k, :], k == 0, k == 7, [b_wfm, bh], [PB[bk]])
                CP(P, 'act', zfT3[:, cc, :], ps, [PB[bk]], [b_zfT])
            for c2 in range(2):
                pa = P.bank(2)[:, 0:TB]
                pg = P.bank(3)[:, 0:TB]
                for k in range(8):
                    MM(P, pa, w_fm3[:, k, (2 + c2) * 128:(3 + c2) * 128], hT3[:, k, :], k == 0, k == 7, [b_wfm, bh], [PB[2]])
                for k in range(8):
                    MM(P, pg, w_fm3[:, k, (4 + c2) * 128:(5 + c2) * 128], hT3[:, k, :], k == 0, k == 7, [b_wfm, bh], [PB[3]])
                ACT(P, sg, pg, AF.Sigmoid, [PB[3]], [b_sg])
                TT(P, 'dve', u3[:, c2, :], pa, sg, ALU.mult, [PB[2], b_sg], [b_u])
            P.dma(S['UT'][:, :, tb * TB:(tb + 1) * TB].rearrange('c p t -> p c t'), u3, b_u, False)
            for tt in range(2):
                t0 = tb * TB + tt * 128
                ti = t0 // 128
                tsl = slice(tt * 128, (tt + 1) * 128)
                z, bz = zcs[tt], b_zcs[tt]
                pz = P.bank(4)
                for kc in range(2):
                    MM(P, pz[:, kc * 256:(kc + 1) * 256], zfT3[:, kc, tsl], csbd, True, True, [b_zfT, b_cs], [PB[4]])
                CP(P, 'act', z, pz, [PB[4]], [bz])
                P.dma(S['ZCS'][t0:t0 + 128, :], z, bz, False)
                pp = P.bank(1)[:, 0:416]
                for k in range(8):
                    MM(P, pp, hT3[:, k, tsl], w_tm3[:, k, :], k == 0, k == 7, [bh, b_wtm], [PB[1]])
                CP(P, 'act', tmz, pp, [PB[1]], [b_tmz])
                TTR(P, sq[:, 0:128], tmz[:, 0:128], tmz[:, 0:128], st[:, 0:1], [b_tmz], [b_sq, b_st])
                TTR(P, sq[:, 0:256], tmz[:, 160:416], tmz[:, 160:416], st[:, 1:2], [b_tmz], [b_sq, b_st])
                TS(P, 'dve', st[:, 1:2], st[:, 1:2], 0.5, None, ALU.mult, None, [b_st], [b_st])
                RSTD(P, st[:, 0:2], st[:, 0:2], 128, [b_st], [b_st])
                ACT(P, cn[:, 0:128], tmz[:, 0:128], AF.Copy, [b_tmz, b_st], [b_cn], scale=st[:, 0:1])
                ACT(P, cn[:, 128:384], tmz[:, 160:416], AF.Copy, [b_tmz, b_st], [b_cn], scale=st[:, 1:2])
                pt = r3(P.bank(0).bitcast(BF16), 8, 128)
                for j in range(3):
                    TR(P, pt[:, j, :], cn[:, j * 128:(j + 1) * 128], identB, [b_cn, b_idB], [PB[0]])
                CP(P, 'dve', cT3, pt[:, 0:3, :], [PB[0]], [b_cT])
                for hf in range(2):
                    MM(P, P.bank(5 + hf), cT3[:, 0, :], wukv[:, hf * 512:(hf + 1) * 512], True, True, [b_cT, b_wukv], [PB[5 + hf]])
                for hf in range(2):
                    CP(P, 'act', kvs[:, hf * 512:(hf + 1) * 512], P.bank(5 + hf), [PB[5 + hf]], [b_kvs])
                pq = P.bank(6, 2)
                for nh in range(2):
                    for kk in range(2):
                        MM(P, pq[:, nh * 512:nh * 512 + 384], cT3[:, 1 + kk, :], wuq3[:, kk, nh * 384:(nh + 1) * 384],
                           kk == 0, kk == 1, [b_cT, b_wuq], [PB[6], PB[7]])
                kv4 = r3(kvs, 8, 128)
                CP(P, 'pool', V4[:, ti, :, 0:64], kv4[:, :, 64:128], [b_kvs], [b_V])
                sq3 = r3(sq[:, 0:512], 8, 64)
                TT(P, 'dve', sq3, kv4[:, :, 0:64], kv4[:, :, 0:64], ALU.mult, [b_kvs], [b_sq])
                RED(P, st[:, 8:16], sq3, ALU.add, [b_sq], [b_st])
                TTR(P, sq[:, 512:544], tmz[:, 128:160], tmz[:, 128:160], st[:, 2:3], [b_tmz], [b_sq, b_st])
                TS(P, 'dve', st[:, 8:16], st[:, 8:16], st[:, 2:3], None, ALU.add, None, [b_st], [b_st])
                RSTD(P, st[:, 8:16], st[:, 8:16], 96, [b_st], [b_st])
                ktm3 = r3(ktm, 8, 96)
                TT(P, 'dve', sq3, kv4[:, :, 0:64], st[:, 8:16].unsqueeze(2).to_broadcast([128, 8, 64]), ALU.mult,
                   [b_kvs, b_st], [b_sq])
                TT(P, 'dve', ktm3[:, :, 0:64], sq3, gkb[:, 0:64].unsqueeze(1).to_broadcast([128, 8, 64]), ALU.mult,
                   [b_sq] + gvec, [b_ktm])
                TT(P, 'dve', krr, tmz[:, 128:160], gkb[:, 64:96], ALU.mult, [b_tmz] + gvec, [b_krr])
                if not is_ctx:
                    P.dma(rc, I['ropeC'][t0 - CT:t0 - CT + 128, :], b_rope, True)
                    P.dma(rs, I['ropeS'][t0 - CT:t0 - CT + 128, :], b_rope, True)
                    kr4 = r3(krr, 4, 8)
                    rt4 = r3(rtmp[:, 0:32], 4, 8)
                    for hb in range(2):
                        for blk in range(2):
                            TT(P, 'dve', rt4[:, hb * 2 + blk, :], kr4[:, hb * 2 + (1 - blk), :],
                               r3(rs, 4, 8)[:, hb * 2 + blk, :], ALU.mult, [b_krr, b_rope], [b_rtmp])
                    TT(P, 'dve', krr, krr, rc, ALU.mult, [b_krr, b_rope], [b_krr])
                    TT(P, 'dve', krr, krr, rtmp[:, 0:32], ALU.add, [b_krr, b_rtmp], [b_krr])
                TT(P, 'dve', ktm3[:, :, 64:96], krr.unsqueeze(1).to_broadcast([128, 8, 32]),
                   st[:, 8:16].unsqueeze(2).to_broadcast([128, 8, 32]), ALU.mult, [b_krr, b_st], [b_ktm])
                pk = r3(P.bank(0).bitcast(BF16), 8, 128)
                for h in range(8):
                    TR(P, pk[0:96, h, :], ktm3[:, h, :], identB, [b_ktm, b_idB], [PB[0]])
                CP(P, 'act', KT3[0:96, :, t0:t0 + 128], pk[0:96, :, :], [PB[0]], [b_KT])
                for nh in range(2):
                    CP(P, 'act', qs[:, nh * 384:(nh + 1) * 384], pq[:, nh * 512:nh * 512 + 384], [PB[6], PB[7]], [b_qs])
                q3 = r3(qs, 8, 96)
                s3 = r3(sq, 8, 96)
                TT(P, 'dve', s3, q3, q3, ALU.mult, [b_qs], [b_sq])
                RED(P, st[:, 16:24], s3, ALU.add, [b_sq], [b_st])
                RSTD(P, st[:, 16:24], st[:, 16:24], 96, [b_st], [b_st])
                TT(P, 'dve', s3, q3, st[:, 16:24].unsqueeze(2).to_broadcast([128, 8, 96]), ALU.mult, [b_qs, b_st], [b_sq])
                TT(P, 'dve', q3, s3, gqb.unsqueeze(1).to_broadcast([128, 8, 96]), ALU.mult, [b_sq] + gvec, [b_qs])
                qtm3 = r3(qtm, 8, 96)
                CP(P, 'pool', qtm3[:, :, 0:64], q3[:, :, 0:64], [b_qs], [b_qtm])
                if not is_ctx:
                    q5 = qs.rearrange('p (h c) -> p h c', h=8, c=96)[:, :, 64:96].rearrange('p h (a b) -> p h a b', a=4, b=8)
                    rt5 = rtmp.rearrange('p (h a b) -> p h a b', h=8, a=4, b=8)
                    rs4 = r3(rs, 4, 8)
                    for hb in range(2):
                        for blk in range(2):
                            TT(P, 'dve', rt5[:, :, hb * 2 + blk, :], q5[:, :, hb * 2 + (1 - blk), :],
                               rs4[:, hb * 2 + blk, :].unsqueeze(1).to_broadcast([128, 8, 8]), ALU.mult,
                               [b_qs, b_rope], [b_rtmp])
                    qr = q3[:, :, 64:96]
                    TT(P, 'dve', s3[:, :, 0:32], qr, rc.unsqueeze(1).to_broadcast([128, 8, 32]), ALU.mult,
                       [b_qs, b_rope], [b_sq])
                    TT(P, 'dve', qtm3[:, :, 64:96], s3[:, :, 0:32], r3(rtmp, 8, 32), ALU.add, [b_sq, b_rtmp], [b_qtm])
                else:
                    CP(P, 'pool', qtm3[:, :, 64:96], q3[:, :, 64:96], [b_qs], [b_qtm])
                pqT = r3(P.bank(4).bitcast(BF16), 8, 128)
                for h in range(8):
                    TR(P, pqT[0:96, h, :], qtm3[:, h, :], identB, [b_qtm, b_idB], [PB[4]])
                qq, bqq = qTt[tt], b_qTt[tt]
                CP(P, 'act', r3(qq, 8, 128)[0:96, :, :], pqT[0:96, :, :], [PB[4]], [bqq])
                P.dma(S['QT'][:, :, t0:t0 + 128].rearrange('h p t -> p h t'), r3(qq, 8, 128)[0:96, :, :], bqq, False)
        if debug:
            kd = nc.dram_tensor('KTd%d' % l, [128, 8 * NT], BF16, kind='ExternalOutput').ap()
            vd = nc.dram_tensor('Vd%d' % l, [128, 34 * 8 * 65], BF16, kind='ExternalOutput').ap()
            P.dma(kd, KT3.rearrange('p a b -> p (a b)'), b_KT, False)
            P.dma(vd, V4.rearrange('p a b c -> p (a b c)'), b_V, False)
        P.barrier()
        P.release()

    def phase_B(l):
        KT3, V4, b_KT, b_V = AT['KT3'], AT['V4'], AT['b_KT'], AT['b_V']
        P.mark()
        qt = [P.alloc(8 * 512, BF16) for _ in range(2)]
        b_qt = [Buf(), Buf()]
        E = [P.alloc(1024, BF16) for _ in range(2)]
        b_E = [Buf() for _ in range(2)]
        atm = P.alloc(4 * 512)
        b_atm = Buf()
        atb = P.alloc(4 * 512, BF16)
        b_atb = Buf()
        rcp = P.alloc(4)
        b_rcp = Buf()
        aTt = [P.alloc(4 * 512, BF16) for _ in range(2)]
        b_aTt = [Buf(), Buf()]
        scale = 96.0 ** -0.5
        tgen = gen_T(l, 7)
        blocks = [(CT + i * 512, 512, list(range(34))) for i in range(8)]
        if l < L - 1:
            blocks.append((0, 256, [0, 1]))
        ne = 0
        for bi, (q0, nq, kts) in enumerate(blocks):
            nqi = nq // 128
            q_t, bq = qt[bi % 2], b_qt[bi % 2]
            q3 = r3(q_t, 8, 512)
            P.dma(q3[0:96, :, 0:nq], S['QT'][:, :, q0:q0 + nq].rearrange('h p t -> p h t'), bq, True)
            atm3 = r3(atm, 4, 512)
            pairs = [kts[a:a + 2] for a in range(0, len(kts), 2)]
            steps = [(h, pi, pr) for h in range(8) for pi, pr in enumerate(pairs)]

            def qk(i):
                h, pi, pr = steps[i]
                sp_ = (ne + i) % 2
                for c, kt in enumerate(pr):
                    bk = 2 * sp_ + c
                    MM(P, P.bank(bk)[:, 0:nq], KT3[0:96, h, kt * 128:(kt + 1) * 128], q3[0:96, h, 0:nq], True, True,
                       [b_KT, bq], [PB[bk]])

            qk(0)
            for i, (h, pi, pr) in enumerate(steps):
                if i + 1 < len(steps):
                    qk(i + 1)
                next(tgen, None)
                sp_ = (ne + i) % 2
                accb = 4 + (h % 2)
                acc = r3(P.bank(accb)[:, 0:4 * 65], 4, 65)
                e_t, be = E[sp_], b_E[sp_]
                e3 = r3(e_t, 2, 512)
                ps2 = r3(P.ps[:, 2 * sp_ * 512:(2 * sp_ + 2) * 512], 2, 512)
                ACT(P, e3[:, :, 0:nq], ps2[:, :, 0:nq], AF.Exp, [PB[2 * sp_], PB[2 * sp_ + 1]], [be], scale=scale)
                for c, kt in enumerate(pr):
                    for qi in range(nqi):
                        MM(P, acc[:, qi, :], e3[:, c, qi * 128:(qi + 1) * 128], V4[:, kt, h, :],
                           pi == 0 and c == 0 and qi == 0, pi == len(pairs) - 1 and c == len(pr) - 1,
                           [be, b_V], [PB[accb]], skip=True)
                if pi == len(pairs) - 1:
                    P.op('dve', lambda e, acc=acc, nqi=nqi: e.reciprocal(out=rcp[:, 0:nqi], in_=acc[:, 0:nqi, 64]),
                         [PB[accb]], [b_rcp])
                    TT(P, 'dve', atm3[:, 0:nqi, h * 64:(h + 1) * 64], acc[:, 0:nqi, 0:64],
                       rcp[:, 0:nqi].unsqueeze(2).to_broadcast([128, nqi, 64]), ALU.mult, [PB[accb], b_rcp], [b_atm])
            ne += len(steps)
            CP(P, 'pool', atb, atm, [b_atm], [b_atb])
            atb3 = r3(atb, 4, 512)
            a_t, ba = aTt[bi % 2], b_aTt[bi % 2]
            a3 = r3(a_t, 4, 512)
            for qi in range(nqi):
                pt = r3(P.bank(6).bitcast(BF16)[:, 0:512], 4, 128)
                for c in range(4):
                    TR(P, pt[:, c, :], atb3[:, qi, c * 128:(c + 1) * 128], identB, [b_atb, b_idB], [PB[6]])
                CP(P, 'act', a3[:, :, qi * 128:(qi + 1) * 128], pt, [PB[6]], [ba])
            P.dma(S['aT'][:, :, q0:q0 + nq].rearrange('c p t -> p c t'), a3[:, :, 0:nq], ba, False)
        for _ in tgen:
            pass
        P.barrier()
        P.release()
        P.release()

    def phase_C1(l):
        P.mark()
        Z = P.alloc(32 * 512, BF16)
        bZ = Buf()
        Z3 = r3(Z, 32, 512)
        P.dma(Z3, S['ZCS'][CT:NT, :].rearrange('(tt p) n -> p tt n', p=128), bZ, True)
        slab = [P.alloc(8 * 512, BF16) for _ in range(4)]
        b_slab = [Buf() for _ in range(4)]
        fo = [P.alloc(2 * 512, BF16) for _ in range(2)]
        b_fo = [Buf(), Buf()]
        ns = 0
        for kb in range(8):
            for cs in range(2):
                src = I['dftP'][cs].rearrange('(tt p) k -> p tt k', p=128)
                for tg in range(4):
                    sl, bs = slab[ns % 4], b_slab[ns % 4]
                    sl3 = r3(sl, 8, 512)
                    P.dma(sl3, src[:, tg * 8:(tg + 1) * 8, kb * 512:(kb + 1) * 512], bs, True)
                    for t8 in range(8):
                        tt = tg * 8 + t8
                        first = (cs == 0 and tt == 0)
                        last = (cs == 1 and tt == 31)
                        for mc in range(2):
                            MM(P, P.bank(mc), Z3[:, tt, mc * 256 + cs * 128: mc * 256 + cs * 128 + 128], sl3[:, t8, :],
                               first, last, [bZ, bs], [PB[mc]])
                    ns += 1
            f_t, bf = fo[kb % 2], b_fo[kb % 2]
            f3 = r3(f_t, 2, 512)
            CP(P, 'act', f3[:, 0, :], P.bank(0), [PB[0]], [bf])
            CP(P, 'dve', f3[:, 1, :], P.bank(1), [PB[1]], [bf])
            P.dma(S['fT'][:, :, CT + kb * 512:CT + (kb + 1) * 512].rearrange('c p t -> p c t'), f3, bf, False)
        if l < L - 1:
            Zc = P.alloc(2 * 512, BF16)
            bZc = Buf()
            Zc3 = r3(Zc, 2, 512)
            P.dma(Zc3, S['ZCS'][0:CT, :].rearrange('(tt p) n -> p tt n', p=128), bZc, True)
            dc = P.alloc(2 * 2 * 256, BF16)
            bdc = Buf()
            dc4 = r4(dc, 2, 2, 256)
            for cs in range(2):
                P.dma(dc4[:, cs, :, :], I['dftC'][cs].rearrange('(tt p) k -> p tt k', p=128), bdc, True)
            for mc in range(2):
                n = 0
                for cs in range(2):
                    for tt in range(2):
                        MM(P, P.bank(2 + mc)[:, 0:256], Zc3[:, tt, mc * 256 + cs * 128: mc * 256 + cs * 128 + 128],
                           dc4[:, cs, tt, :], n == 0, n == 3, [bZc, bdc], [PB[2 + mc]])
                        n += 1
            f_t, bf = fo[0], b_fo[0]
            f3 = r3(f_t, 2, 512)
            CP(P, 'act', f3[:, 0, 0:256], P.bank(2)[:, 0:256], [PB[2]], [bf])
            CP(P, 'dve', f3[:, 1, 0:256], P.bank(3)[:, 0:256], [PB[3]], [bf])
            P.dma(S['fT'][:, :, 0:CT].rearrange('c p t -> p c t'), f3[:, :, 0:256], bf, False)
        P.barrier()
        P.release()

    def phase_C2(l):
        P.mark()
        LB = 15 + CT + 15 + T + 15
        U = P.alloc(2 * LB)
        bU = Buf()
        U3 = r3(U, 2, LB)
        MEMSET(P, 'pool', U, 0.0, [bU])
        OC, OX = 15, 15 + CT + 15
        for c in range(2):
            P.dma(U3[:, c, OC:OC + CT], S['UT'][c, :, 0:CT], bU, True)
            P.dma(U3[:, c, OX:OX + T], S['UT'][c, :, CT:NT], bU, True)
        wd = P.alloc(62)
        bwd = Buf()
        P.dma(wd, I['wdwT'][l], bwd, True)
        wd3 = r3(wd, 2, 31)
        cp = P.alloc(6)
        bcp = Buf()
        P.dma(cp, I['cvp'][l], bcp, True)
        cp3 = r3(cp, 2, 3)
        A = P.alloc(2 * LB)
        bA = Buf()
        A3 = r3(A, 2, LB)
        NV = LB - 30
        for c in range(2):
            TS(P, 'dve', A3[:, c, 15:15 + NV], U3[:, c, 0:NV], wd3[:, c, 0:1], cp3[:, c, 0:1], ALU.mult, ALU.add,
               [bU, bwd, bcp], [bA])
            for w in range(1, 31):
                STT(P, A3[:, c, 15:15 + NV], U3[:, c, w:w + NV], wd3[:, c, w:w + 1], A3[:, c, 15:15 + NV], ALU.mult, ALU.add,
                    [bU, bwd, bA], [bA])
        sqt = P.alloc(2 * 512)
        bsq = Buf()
        sq3 = r3(sqt, 2, 512)
        mean = P.alloc(512)
        bmean = Buf()
        var = P.alloc(512)
        bvar = Buf()
        y = P.alloc(512)
        by = Buf()
        co = [P.alloc(2 * 512, BF16) for _ in range(2)]
        bco = [Buf(), Buf()]
        blocks = [(OX + i * 512, CT + i * 512, 512) for i in range(8)]
        if l < L - 1:
            blocks.append((OC, 0, 256))
        for bi, (o, t0, n) in enumerate(blocks):
            for c in range(2):
                ACT(P, sq3[:, c, 0:n], A3[:, c, o:o + n], AF.Square, [bA], [bsq])
            for c in range(2):
                MM(P, P.bank(0)[:, 0:n], ones256, A3[:, c, o:o + n], c == 0, c == 1, [b_ones, bA], [PB[0]])
            for c in range(2):
                MM(P, P.bank(1)[:, 0:n], ones256, sq3[:, c, 0:n], c == 0, c == 1, [b_ones, bsq], [PB[1]])
            CP(P, 'act', mean[:, 0:n], P.bank(0)[:, 0:n], [PB[0]], [bmean])
            TT(P, 'dve', var[:, 0:n], mean[:, 0:n], mean[:, 0:n], ALU.mult, [bmean], [bvar])
            TT(P, 'dve', var[:, 0:n], P.bank(1)[:, 0:n], var[:, 0:n], ALU.subtract, [PB[1], bvar], [bvar])
            TS(P, 'dve', var[:, 0:n], var[:, 0:n], EPS, None, ALU.add, None, [bvar], [bvar])
            ACT(P, var[:, 0:n], var[:, 0:n], AF.Sqrt, [bvar], [bvar])
            P.op('dve', lambda e, n=n: e.reciprocal(out=var[:, 0:n], in_=var[:, 0:n]), [bvar], [bvar])
            c_t, bc = co[bi % 2], bco[bi % 2]
            c3 = r3(c_t, 2, 512)
            for c in range(2):
                TT(P, 'dve', y[:, 0:n], A3[:, c, o:o + n], mean[:, 0:n], ALU.subtract, [bA, bmean], [by])
                TT(P, 'dve', y[:, 0:n], y[:, 0:n], var[:, 0:n], ALU.mult, [by, bvar], [by])
                ACT(P, c3[:, c, 0:n], y[:, 0:n], AF.Silu, [by, bcp], [bc], scale=cp3[:, c, 1:2], bias=cp3[:, c, 2:3])
            P.dma(S['cvT'][:, :, t0:t0 + n].rearrange('c p t -> p c t'), c3[:, :, 0:n], bc, False)
        P.barrier()
        P.release()

    def phase_D(l):
        P.mark()
        win = I['w_in'][l].rearrange('(k p) n -> p k n', p=128)
        stg = [P.alloc(1024) for _ in range(3)]
        bstg = [Buf() for _ in range(3)]
        wg = P.alloc(8 * 3072, BF16)
        b_wg = Buf()
        wg3 = r3(wg, 8, 3072)
        for part in range(3):
            load_w_bf16(wg3[:, :, part * 1024:(part + 1) * 1024], b_wg, win[:, :, 1184 + part * 1024:1184 + (part + 1) * 1024],
                        1024, stg, bstg, 8)
        wba = P.alloc(4 * 1024, BF16)
        wbf = P.alloc(2 * 1024, BF16)
        wbc = P.alloc(2 * 1024, BF16)
        wo = P.alloc(8 * 1024, BF16)
        b_wb = Buf()
        load_w_bf16(r3(wba, 4, 1024), b_wb, I['wb_attn'][l].rearrange('(k p) n -> p k n', p=128), 1024, stg, bstg, 4)
        load_w_bf16(r3(wbf, 2, 1024), b_wb, I['wb_fnet'][l].rearrange('(k p) n -> p k n', p=128), 1024, stg, bstg, 2)
        load_w_bf16(r3(wbc, 2, 1024), b_wb, I['wb_conv'][l].rearrange('(k p) n -> p k n', p=128), 1024, stg, bstg, 2)
        load_w_bf16(r3(wo, 8, 1024), b_wb, I['w_out'][l].rearrange('(k p) n -> p k n', p=128), 1024, stg, bstg, 8)
        wbr = [(r3(wba, 4, 1024), 4), (r3(wbf, 2, 1024), 2), (r3(wbc, 2, 1024), 2)]
        wo3 = r3(wo, 8, 1024)
        ht = HT()
        sgt = P.alloc(24 * TB)
        b_sgt = Buf()
        sg3 = r3(sgt, 24, TB)
        br = [P.alloc(8 * TB, BF16) for _ in range(2)]
        b_br = [Buf(), Buf()]
        mT = P.alloc(8 * TB, BF16)
        b_mT = Buf()
        mT3 = r3(mT, 8, TB)
        t1 = P.alloc(TB)
        t2 = P.alloc(TB)
        b_t1, b_t2 = Buf(), Buf()
        xo = [P.alloc(1024) for _ in range(2)]
        b_xo = [Buf(), Buf()]
        nblk = NBLK if l < L - 1 else NBLK
        for tb in range(nblk):
            if tb == 0 and l == L - 1:
                continue
            mi = 1 if tb == 0 else 0
            hT3, bh, xs, bxs = ht.make(l, tb, 0, tbank=0)
            b_t, bb = br[tb % 2], b_br[tb % 2]
            b3 = r3(b_t, 8, TB)
            tsl = slice(tb * TB, (tb + 1) * TB)
            P.dma(b3[:, 0:4, :], S['aT'][:, :, tsl].rearrange('c p t -> p c t'), bb, True)
            P.dma(b3[:, 4:6, :], S['fT'][:, :, tsl].rearrange('c p t -> p c t'), bb, True)
            P.dma(b3[:, 6:8, :], S['cvT'][:, :, tsl].rearrange('c p t -> p c t'), bb, True)
            for gc in range(24):
                bk = 1 + (gc % 2)
                ps = P.bank(bk)[:, 0:TB]
                for k in range(8):
                    MM(P, ps, wg3[:, k, gc * 128:(gc + 1) * 128], hT3[:, k, :], k == 0, k == 7, [b_wg, bh], [PB[bk]])
                ACT(P, sg3[:, gc, :], ps, AF.Sigmoid, [PB[bk]], [b_sgt])
            for oc in range(8):
                koff = 0
                for bi, (w3, nk) in enumerate(wbr):
                    ps = P.bank(3 + bi)[:, 0:TB]
                    for kc in range(nk):
                        MM(P, ps, w3[:, kc, oc * 128:(oc + 1) * 128], b3[:, koff + kc, :], kc == 0, kc == nk - 1, [b_wb, bb], [PB[3 + bi]])
                    koff += nk
                TT(P, 'dve', t1, P.bank(3)[:, 0:TB], sg3[:, oc, :], ALU.mult, [PB[3], b_sgt], [b_t1])
                TT(P, 'dve', t2, P.bank(4)[:, 0:TB], sg3[:, 8 + oc, :], ALU.mult, [PB[4], b_sgt], [b_t2])
                TT(P, 'dve', t1, t1, t2, ALU.add, [b_t1, b_t2], [b_t1])
                TT(P, 'dve', t2, P.bank(5)[:, 0:TB], sg3[:, 16 + oc, :], ALU.mult, [PB[5], b_sgt], [b_t2])
                TT(P, 'dve', mT3[:, oc, :], t1, t2, ALU.add, [b_t1, b_t2], [b_mT])
            for tt in range(2):
                x_o, bxo = xo[tt], b_xo[tt]
                for nh in range(2):
                    ps = P.bank(6 + nh)
                    for k in range(8):
                        MM(P, ps, mT3[:, k, tt * 128:(tt + 1) * 128], wo3[:, k, nh * 512:(nh + 1) * 512], k == 0, k == 7,
                           [b_mT, b_wb], [PB[6 + nh]])
                    TT(P, 'dve', x_o[:, nh * 512:(nh + 1) * 512], ps, gtb[mi][:, nh * 512:(nh + 1) * 512], ALU.mult,
                       [PB[6 + nh], b_gtb[mi]], [bxo])
                TT(P, 'pool', x_o, x_o, xs[tt], ALU.add, [bxo, bxs[tt]], [bxo])
                P.dma(S['xres'][tb * TB + tt * 128: tb * TB + (tt + 1) * 128, :], x_o, bxo, False)
        P.barrier()
        P.release()

    def gen_T(l, bank):
        us = [P.alloc(1024) for _ in range(2)]
        b_us = [Buf(), Buf()]
        vsl = [P.alloc(1024) for _ in range(2)]
        b_vs = [Buf(), Buf()]
        ub = [P.alloc(1024, BF16) for _ in range(2)]
        b_ub = [Buf(), Buf()]
        uo = [P.alloc(1024, BF16) for _ in range(2)]
        b_uo = [Buf(), Buf()]
        vb = [P.alloc(1024, BF16) for _ in range(2)]
        b_vb = [Buf(), Buf()]
        usrc = I['u_tab'][l].rearrange('(i j) d -> j i d', j=NCH)
        vsrc = I['v_tab'][l].rearrange('(i j) d -> j i d', j=NCH)
        for j in range(NCH):
            s = j % 2
            P.dma(us[s], usrc[j], b_us[s], True)
            P.dma(vsl[s], vsrc[j], b_vs[s], True)
            yield
            yield
            CP(P, 'pool', vb[s], vsl[s], [b_vs[s]], [b_vb[s]])
            CP(P, 'dve', ub[s], us[s], [b_us[s]], [b_ub[s]])
            yield
            P.dma(S['vs'][j], vb[s], b_vb[s], False)
            pt = r3(P.bank(bank).bitcast(BF16), 8, 128)
            for k in range(8):
                TR(P, pt[:, k, :], ub[s][:, k * 128:(k + 1) * 128], identB, [b_ub[s], b_idB], [PB[bank]])
            yield
            CP(P, 'dve', uo[s], P.bank(bank).bitcast(BF16), [PB[bank]], [b_uo[s]])
            yield
            P.dma(S['uTs'][j], uo[s], b_uo[s], False)

    def phase_E(l):
        P.mark()
        last = (l == L - 1)
        wq = P.alloc(8 * 1024, BF16)
        b_wq = Buf()
        wq3 = r3(wq, 8, 1024)
        skb = P.alloc(8 * 256, BF16)
        b_sk = Buf()
        P.mark()
        stg = [P.alloc(1024) for _ in range(2)]
        bstg = [Buf(), Buf()]
        load_w_bf16(wq3, b_wq, I['w_query'][l].rearrange('(k p) n -> p k n', p=128), 1024, stg, bstg, 8)
        for hf in range(2):
            P.dma(stg[hf], I['skbd'][l][:, hf * 1024:(hf + 1) * 1024], bstg[hf], True)
            CP(P, 'dve', skb[:, hf * 1024:(hf + 1) * 1024], stg[hf], [bstg[hf]], [b_sk])
        P.barrier()
        P.release()
        skb3 = r3(skb, 8, 256)
        ht = HT(nx=4, nh=2)
        qT = P.alloc(8 * TB, BF16)
        b_qT = Buf()
        qT3 = r3(qT, 8, TB)
        sc = P.alloc(2048)
        b_sc = Buf()
        sc4 = r4(sc, 8, 2, 128)
        tmp = P.alloc(128)
        b_tmp = Buf()
        val = P.alloc(256)
        b_val = Buf()
        val4 = r4(val, 8, 2, 16)
        idx = P.alloc(256).bitcast(U32)
        b_idx = Buf()
        idx4 = r4(idx, 8, 2, 16)
        idxf = P.alloc(256)
        b_idxf = Buf()
        idxf4 = r4(idxf, 8, 2, 16)
        cand = P.alloc(2048)
        b_cand = Buf()
        cand4 = r4(cand, 8, 16, 16)
        ctmp = P.alloc(256)
        b_ctmp = Buf()
        best = P.alloc(128)
        b_best = Buf()
        best3 = r3(best, 8, 16)
        pos = P.alloc(128).bitcast(U32)
        b_pos = Buf()
        pos3 = r3(pos, 8, 16)
        pa = P.alloc(128).bitcast(U32)
        pb_ = P.alloc(128).bitcast(U32)
        paf = P.alloc(128)
        pbf = P.alloc(128)
        b_pab = Buf()
        oh = cand
        b_oh = b_cand
        oh4 = r4(oh, 8, 16, 16)
        tok3 = P.alloc(3 * 128)
        b_tok = Buf()
        tok33 = r3(tok3, 3, 128)
        gsum = P.alloc(8)
        b_gsum = Buf()
        ijw = [P.alloc(3 * TB, BF16) for _ in range(2)]
        b_ijwl = [Buf(), Buf()]
        GT = 16
        Aoh = [P.alloc(GT * 128, BF16) for _ in range(2)]
        Boh = [P.alloc(GT * 128, BF16) for _ in range(2)]
        b_A = [Buf(), Buf()]
        b_B = [Buf(), Buf()]
        G = P.alloc(TB * 128, BF16)
        b_G = Buf()
        G3 = r3(G, TB, 128)
        NS = 3
        utr = [P.alloc(1024, BF16) for _ in range(NS)]
        b_utr = [Buf() for _ in range(NS)]
        vtr = [P.alloc(1024, BF16) for _ in range(NS)]
        b_vtr = [Buf() for _ in range(NS)]
        gs = [P.alloc(TB, BF16) for _ in range(2)]
        b_gs = [Buf(), Buf()]
        av = [P.alloc(TB, BF16) for _ in range(2)]
        b_av = [Buf(), Buf()]
        xo = [P.alloc(1024) for _ in range(2)]
        b_xo = [Buf(), Buf()]
        iota128b = iota128h.unsqueeze(1).to_broadcast([128, GT, 128])
        state = {}

        def prep(tb, slot):
            ht.begin(l, tb, from_res=True)
            yield
            yield
            yield
            hT3, bh, xs, bxs = ht.finish(2, tbank=0)
            yield
            yield
            for h in range(8):
                ps = P.bank(1)[:, 0:TB]
                for k in range(8):
                    MM(P, ps, wq3[:, k, h * 128:(h + 1) * 128], hT3[:, k, :], k == 0, k == 7, [b_wq, bh], [PB[1]])
                yield
                CP(P, 'act', qT3[:, h, :], ps, [PB[1]], [b_qT])
            ijwT3 = r3(ijw[slot], 3, TB)
            b_ijw = b_ijwl[slot]
            for tt in range(2):
                tsl = slice(tt * 128, (tt + 1) * 128)
                yield
                for h in range(8):
                    ps = P.bank(1)[:, 0:256]
                    MM(P, ps, qT3[:, h, tsl], skb3[:, h, :], True, True, [b_qT, b_sk], [PB[1]])
                    yield
                    CP(P, 'act', sc[:, h * 256:(h + 1) * 256], ps, [PB[1]], [b_sc])
                yield
                for h in range(8):
                    for p in range(2):
                        s_hp = sc4[:, h, p, :]
                        v16 = val4[:, h, p, :]
                        i16 = idx4[:, h, p, :]
                        P.op('dve', lambda e, o=v16[:, 0:8], i=s_hp: e.max(out=o, in_=i), [b_sc], [b_val])
                        P.op('dve', lambda e, o=i16[:, 0:8], m=v16[:, 0:8], i=s_hp: e.max_index(out=o, in_max=m, in_values=i),
                             [b_sc, b_val], [b_idx])
                        P.op('dve', lambda e, o=tmp[:, 0:128], m=v16[:, 0:8], i=s_hp: e.match_replace(
                            out=o, in_to_replace=m, in_values=i, imm_value=-1e30), [b_sc, b_val], [b_tmp])
                        yield
                        P.op('dve', lambda e, o=v16[:, 8:16], i=tmp[:, 0:128]: e.max(out=o, in_=i), [b_tmp], [b_val])
                        P.op('dve', lambda e, o=i16[:, 8:16], m=v16[:, 8:16], i=tmp[:, 0:128]: e.max_index(
                            out=o, in_max=m, in_values=i), [b_tmp, b_val], [b_idx])
                        yield
                CP(P, 'dve', idxf, idx, [b_idx], [b_idxf])
                TT(P, 'dve', cand4, val4[:, :, 0, :].unsqueeze(3).to_broadcast([128, 8, 16, 16]),
                   val4[:, :, 1, :].unsqueeze(2).to_broadcast([128, 8, 16, 16]), ALU.add, [b_val], [b_cand])
                yield
                for h in range(8):
                    c_h = cand[:, h * 256:(h + 1) * 256]
                    b16 = best3[:, h, :]
                    p16 = pos3[:, h, :]
                    P.op('dve', lambda e, o=b16[:, 0:8], i=c_h: e.max(out=o, in_=i), [b_cand], [b_best])
                    P.op('dve', lambda e, o=p16[:, 0:8], m=b16[:, 0:8], i=c_h: e.max_index(out=o, in_max=m, in_values=i),
                         [b_cand, b_best], [b_pos])
                    P.op('dve', lambda e, o=ctmp, m=b16[:, 0:8], i=c_h: e.match_replace(
                        out=o, in_to_replace=m, in_values=i, imm_value=-1e30), [b_cand, b_best], [b_ctmp])
                    yield
                    P.op('dve', lambda e, o=b16[:, 8:16], i=ctmp: e.max(out=o, in_=i), [b_ctmp], [b_best])
                    P.op('dve', lambda e, o=p16[:, 8:16], m=b16[:, 8:16], i=ctmp: e.max_index(out=o, in_max=m, in_values=i),
                         [b_ctmp, b_best], [b_pos])
                    yield
                P.op('dve', lambda e: e.tensor_single_scalar(out=pa, in_=pos, scalar=4, op=ALU.logical_shift_right),
                     [b_pos], [b_pab])
                P.op('dve', lambda e: e.tensor_single_scalar(out=pb_, in_=pos, scalar=15, op=ALU.bitwise_and),
                     [b_pos], [b_pab])
                yield
                CP(P, 'dve', paf, pa, [b_pab], [b_pab])
                CP(P, 'dve', pbf, pb_, [b_pab], [b_pab])
                yield
                for which, pf in enumerate((paf, pbf)):
                    pf3 = r3(pf, 8, 16)
                    TT(P, 'dve', oh4, iota16.unsqueeze(1).unsqueeze(1).to_broadcast([128, 8, 16, 16]),
                       pf3.unsqueeze(3).to_broadcast([128, 8, 16, 16]), ALU.is_equal, [b_iota, b_pab], [b_oh])
                    yield
                    TT(P, 'dve', oh4, oh4, idxf4[:, :, which, :].unsqueeze(2).to_broadcast([128, 8, 16, 16]), ALU.mult,
                       [b_oh, b_idxf], [b_oh])
                    yield
                    RED(P, tok33[:, which, :], r3(oh, 128, 16), ALU.add, [b_oh], [b_tok])
                    yield
                w3 = r3(tok33[:, 2, :], 8, 16)
                TT(P, 'dve', w3, best3, best3[:, :, 0:1].to_broadcast([128, 8, 16]), ALU.subtract, [b_best], [b_tok])
                yield
                ACT(P, tok33[:, 2, :], tok33[:, 2, :], AF.Exp, [b_tok], [b_tok])
                yield
                RED(P, gsum, w3, ALU.add, [b_tok], [b_gsum])
                P.op('dve', lambda e: e.reciprocal(out=gsum, in_=gsum), [b_gsum], [b_gsum])
                TT(P, 'dve', w3, w3, gsum.unsqueeze(2).to_broadcast([128, 8, 16]), ALU.mult, [b_tok, b_gsum], [b_tok])
                yield
                yield
                pT = r3(P.bank(1)[:, 0:384], 3, 128)
                for c in range(3):
                    TR(P, pT[:, c, :], tok33[:, c, :], identF, [b_tok, b_idF], [PB[1]])
                yield
                CP(P, 'act', ijwT3[:, :, tsl], pT, [PB[1]], [b_ijw])
            state[tb] = (hT3, bh, xs, bxs, ijwT3, b_ijw)

        blocks = [tb for tb in range(NBLK) if not (tb == 0 and last)]
        for _ in prep(blocks[0], 0):
            pass
        nchunk = 0
        for bi, tb in enumerate(blocks):
            mi = 1 if tb == 0 else 0
            hT3, bh, xs, bxs, ijwT3, b_ijw = state.pop(tb)
            for g in range(TB // GT):
                A_t, bA = Aoh[g % 2], b_A[g % 2]
                B_t, bB = Boh[g % 2], b_B[g % 2]
                A3, B3 = r3(A_t, GT, 128), r3(B_t, GT, 128)
                gsl = slice(g * GT, (g + 1) * GT)
                TT(P, 'dve', A3, iota128b, ijwT3[:, 0, gsl].unsqueeze(2).to_broadcast([128, GT, 128]), ALU.is_equal,
                   [b_iota, b_ijw], [bA])
                TT(P, 'dve', B3, iota128b, ijwT3[:, 1, gsl].unsqueeze(2).to_broadcast([128, GT, 128]), ALU.is_equal,
                   [b_iota, b_ijw], [bB])
                TT(P, 'dve', B3, B3, ijwT3[:, 2, gsl].unsqueeze(2).to_broadcast([128, GT, 128]), ALU.mult,
                   [bB, b_ijw], [bB])
                for t4 in range(GT // 4):
                    bk = t4 % 2
                    for ti in range(4):
                        tloc = t4 * 4 + ti
                        MM(P, P.bank(bk)[:, ti * 128:(ti + 1) * 128], A3[:, tloc, :], B3[:, tloc, :], True, True, [bA, bB], [PB[bk]])
                    tg0 = g * GT + t4 * 4
                    CP(P, 'act', G[:, tg0 * 128:(tg0 + 4) * 128], P.bank(bk), [PB[bk]], [b_G])
            nxt = prep(blocks[bi + 1], (bi + 1) % 2) if bi + 1 < len(blocks) else None

            def load(j):
                s_ = (nchunk + j) % NS
                P.dma(utr[s_], S['uTs'][j], b_utr[s_], True, q='sp')
                P.dma(vtr[s_], S['vs'][j], b_vtr[s_], True, q='sp')

            def mm1(j):
                s_ = (nchunk + j) % NS
                u3 = r3(utr[s_], 8, 128)
                sb = 2 + ((nchunk + j) % 2)
                for k in range(8):
                    MM(P, P.bank(sb)[:, 0:TB], u3[:, k, :], hT3[:, k, :], k == 0, k == 7, [b_utr[s_], bh], [PB[sb]])

            for j in range(min(NS, NCH)):
                load(j)
            mm1(0)
            for j in range(NCH):
                if j + 1 < NCH:
                    mm1(j + 1)
                s_ = (nchunk + j) % NS
                sb = 2 + ((nchunk + j) % 2)
                g_t, bg = gs[(nchunk + j) % 2], b_gs[(nchunk + j) % 2]
                ACT(P, g_t, P.bank(sb)[:, 0:TB], AF.Gelu_apprx_tanh, [PB[sb]], [bg])
                a_t, ba = av[(nchunk + j) % 2], b_av[(nchunk + j) % 2]
                TT(P, 'pool', a_t, g_t, G3[:, :, j], ALU.mult, [bg, b_G], [ba])
                for tt in range(2):
                    for nh in range(2):
                        ob = 4 + tt * 2 + nh
                        MM(P, P.bank(ob), a_t[:, tt * 128:(tt + 1) * 128], vtr[s_][:, nh * 512:(nh + 1) * 512],
                           j == 0, j == NCH - 1, [ba, b_vtr[s_]], [PB[ob]])
                if j + NS < NCH:
                    load(j + NS)
                if nxt is not None:
                    next(nxt, None)
                    if j % 4 == 0:
                        next(nxt, None)
            nchunk += NCH
            if nxt is not None:
                for _ in nxt:
                    pass
            for tt in range(2):
                x_o, bxo = xo[tt], b_xo[tt]
                for nh in range(2):
                    ob = 4 + tt * 2 + nh
                    TT(P, 'dve', x_o[:, nh * 512:(nh + 1) * 512], P.bank(ob), gtb[2 + mi][:, nh * 512:(nh + 1) * 512], ALU.mult,
                       [PB[ob], b_gtb[2 + mi]], [bxo])
                TT(P, 'pool', x_o, x_o, xs[tt], ALU.add, [bxo, bxs[tt]], [bxo])
                r0 = tb * TB + tt * 128
                if last:
                    P.dma(out_d[r0 - CT:r0 - CT + 128, :], x_o, bxo, False)
                else:
                    P.dma(S['xres'][r0:r0 + 128, :], x_o, bxo, False)
        print('phase E sbuf words', P.sb_off, 'of', P.sb_words)
        P.barrier()
        P.release()

    phases = []
    for l in range(n_layers):
        phases += [('adaln', phase_adaln, l), ('A', phase_A, l), ('B', phase_B, l), ('C1', phase_C1, l),
                   ('C2', phase_C2, l), ('D', phase_D, l), ('E', phase_E, l)]
    for name, fn, l in phases:
        fn(l)
        if stop_after is not None and stop_after == (name, l):
            break
    P.emit()
    return nc


def _consts():
    t = np.arange(T, dtype=np.float64)
    row = np.repeat(np.arange(T // 64, dtype=np.float32), 64)
    col = np.tile(np.arange(64, dtype=np.float32), T // 64)
    inv = (np.float32(10000.0) ** (-np.arange(0, 16, 2, dtype=np.float32) / np.float32(16))).astype(np.float32)
    ar = row[:, None] * inv
    ac = col[:, None] * inv
    ang = np.concatenate([ar, ar, ac, ac], axis=-1).astype(np.float32)
    ropeC = np.cos(ang).astype(np.float32)
    sn = np.sin(ang).astype(np.float32)
    sign = np.tile(np.concatenate([-np.ones(8, np.float32), np.ones(8, np.float32)]), 2)
    ropeS = (sn * sign[None, :]).astype(np.float32)
    n = (np.outer(np.arange(T), np.arange(T)) % T).astype(np.float64)
    dftP = np.stack([np.cos(2 * np.pi * n / T) / 64.0, -np.sin(2 * np.pi * n / T) / 64.0]).astype(ml_dtypes.bfloat16)
    n2 = (np.outer(np.arange(CT), np.arange(CT)) % CT).astype(np.float64)
    dftC = np.stack([np.cos(2 * np.pi * n2 / CT) / 16.0, -np.sin(2 * np.pi * n2 / CT) / 16.0]).astype(ml_dtypes.bfloat16)
    c = np.arange(128)
    m = np.arange(128)
    same = (c[:, None] // 64) == (m[None, :] // 64)
    ph = 2 * np.pi * ((c[:, None] % 64) * (m[None, :] % 64) % 64) / 64.0
    csbd = np.concatenate([np.where(same, np.cos(ph) / 8.0, 0.0), np.where(same, np.sin(ph) / 8.0, 0.0)], axis=1).astype(np.float32)
    ident = np.eye(128, dtype=np.float32)
    sel = np.zeros((2, 256), np.float32)
    sel[0, 0:128] = 1.0
    sel[1, 128:256] = 1.0
    return dict(ropeC=ropeC, ropeS=ropeS, dftP=dftP, dftC=dftC, csbd=csbd, ident=ident, sel=sel)


_CONSTS = None
_NC_CACHE = {}


def make_in_maps(inp, cores):
    global _CONSTS
    if _CONSTS is None:
        _CONSTS = _consts()
    f = lambda a: np.ascontiguousarray(np.asarray(a, dtype=np.float32))
    shared = dict(_CONSTS)
    shared['w_mod'] = f(inp['w_mod'])
    bm = f(inp['b_mod'])
    shared['bmodT'] = np.ascontiguousarray(bm.reshape(L, 48, 128).transpose(0, 2, 1))
    shared['bmod2'] = np.ascontiguousarray(np.repeat(bm[:, None, :], 2, axis=1))
    shared['g1T'] = np.ascontiguousarray(f(inp['g_norm1']).reshape(L, 8, 128).transpose(0, 2, 1))
    shared['g2T'] = np.ascontiguousarray(f(inp['g_norm2']).reshape(L, 8, 128).transpose(0, 2, 1))
    shared['w_in'] = f(inp['w_in'])
    shared['g_ckv'] = f(inp['g_ckv']).reshape(L, 128, 1)
    shared['w_ukv'] = f(inp['w_ukv'])
    shared['g_cqT'] = np.ascontiguousarray(f(inp['g_cq']).reshape(L, 2, 128).transpose(0, 2, 1))
    shared['w_uq'] = f(inp['w_uq'])
    shared['g_qn'] = f(inp['g_qn']).reshape(L, 1, 96)
    shared['g_kn'] = f(inp['g_kn']).reshape(L, 1, 96)
    wdw = f(inp['w_dw']).reshape(L, 31, 2, 128)
    shared['wdwT'] = np.ascontiguousarray(wdw.transpose(0, 3, 2, 1)).reshape(L, 128, 62)
    cvp = np.stack([f(inp['b_dw']), f(inp['g_cln']), f(inp['b_cln'])], axis=-1)
    shared['cvp'] = np.ascontiguousarray(cvp.reshape(L, 2, 128, 3).transpose(0, 2, 1, 3)).reshape(L, 128, 6)
    for k in ('wb_attn', 'wb_fnet', 'wb_conv', 'w_out', 'w_query', 'u_tab', 'v_tab'):
        shared[k] = f(inp[k])
    sk = f(inp['sub_keys'])
    skbd = np.zeros((L, 128, 8, 256), np.float32)
    for p in range(2):
        skbd[:, p * 64:(p + 1) * 64, :, p * 128:(p + 1) * 128] = sk[:, :, p].transpose(0, 3, 1, 2)
    shared['skbd'] = skbd.reshape(L, 128, 2048)
    x = f(inp['x'])
    ctx = f(inp['ctx'])
    c = f(inp['c'])
    cc = f(inp['c_ctx'])
    maps = []
    for b in cores:
        m = dict(shared)
        m['x'] = x[b]
        m['ctx'] = ctx[b]
        cv = np.stack([c[b].reshape(8, 128).T, cc.reshape(8, 128).T], axis=-1)
        m['cvec'] = np.ascontiguousarray(cv).reshape(128, 16)
        maps.append(m)
    return maps


def kernel(**inputs):
    if 'full' not in _NC_CACHE:
        _NC_CACHE['full'] = build_program()
    nc = _NC_CACHE['full']
    maps = make_in_maps(inputs, list(range(8)))
    res = run_bass_kernel_spmd(nc, maps, core_ids=list(range(8)))
    return np.stack([np.asarray(r['out'], dtype=np.float32) for r in res.results], axis=0)
```
